# Optimizing a Trainium2 kernel written in Bass

```python
import math
import jax, jax.numpy as jnp
from jax import lax
import numpy as np

D_MODEL = 1024
BATCH = 4
SEQ = 4096
DEPTH = 4

GRID_W = 64
CTX_LEN = 256
N_MIXERS = 3
EXPAND = 2
D_INNER = EXPAND * D_MODEL
EPS = 1e-6
ROPE_BASE = 10000.0
BLOCK = 128

DA_HEADS = D_INNER // 128
DA_HD = 64
DA_VD = 2 * DA_HD

POOL_WINDOWS = (2, 4, 8, 16)
POOL_GROUPS = len(POOL_WINDOWS)
POOL_GW = D_INNER // POOL_GROUPS

WC_HD = 128
WC_HEADS = D_INNER // WC_HD
WC_KV = 4
WC_G = WC_HEADS // WC_KV
WINDOW = 128

N_A = (DEPTH + 2) // 3
N_B = (DEPTH + 1) // 3
N_C = DEPTH // 3

kernel_name = 'hybrid_diffattn_pool_swa_prefix_trunk'


def rmsnorm(x, g):
    xf = x.astype(jnp.float32)
    y = xf * lax.rsqrt(jnp.mean(xf * xf, axis=-1, keepdims=True) + EPS)
    return (y * g.astype(jnp.float32)).astype(x.dtype)


def axial_rope_tables(rows, head_dim):
    quarter = head_dim // 4
    inv = ROPE_BASE ** (-jnp.arange(quarter, dtype=jnp.float32) / quarter)
    row = jnp.repeat(jnp.arange(rows, dtype=jnp.float32), GRID_W)
    col = jnp.tile(jnp.arange(GRID_W, dtype=jnp.float32), rows)
    ar = row[:, None] * inv[None, :]
    ac = col[:, None] * inv[None, :]
    return (jnp.cos(ar), jnp.sin(ar), jnp.cos(ac), jnp.sin(ac))


def apply_axial_rope(t, rope):
    cos_r, sin_r, cos_c, sin_c = rope
    half = t.shape[-1] // 2
    tf = t.astype(jnp.float32)

    def rot(u, cos, sin):
        u1, u2 = jnp.split(u, 2, axis=-1)
        return jnp.concatenate([u1 * cos - u2 * sin, u2 * cos + u1 * sin], axis=-1)

    out = jnp.concatenate([rot(tf[..., :half], cos_r, sin_r), rot(tf[..., half:], cos_c, sin_c)], axis=-1)
    return out.astype(t.dtype)


def adaln(cond, w, b):
    m = jax.nn.silu(cond) @ w + b
    return jnp.split(m, 3, axis=-1)


def _diff_project(t, w_in):
    B, T, _ = t.shape
    q, k, v, z = jnp.split(t @ w_in, 4, axis=-1)
    q = q.reshape(B, T, DA_HEADS, 2, DA_HD).transpose(0, 2, 3, 1, 4)
    k = k.reshape(B, T, DA_HEADS, 2, DA_HD).transpose(0, 2, 3, 1, 4)
    v = v.reshape(B, T, DA_HEADS, DA_VD).transpose(0, 2, 1, 3)
    return q, k, v, z


def diff_attend(q, k, v, lam):
    s = jnp.einsum('bhmqd,bhmkd->bhmqk', q, k).astype(jnp.float32) * (DA_HD ** -0.5)
    p = jax.nn.softmax(s, axis=-1)
    a = p[:, :, 0] - lam * p[:, :, 1]
    return jnp.einsum('bhqk,bhkd->bhqd', a.astype(v.dtype), v)


def _diff_finish(o, z, subln_g, lam_init, w_out):
    B, H, T, VD = o.shape
    o = rmsnorm(o, subln_g) * (1.0 - lam_init)
    o = o.transpose(0, 2, 1, 3).reshape(B, T, H * VD)
    return (o * jax.nn.silu(z)) @ w_out


def diff_attention_mixer(h, hc, w_in, w_out, lq1, lk1, lq2, lk2, subln_g, layer_idx, rows, need_ctx):
    B, S, _ = h.shape
    lam_init = 0.8 - 0.6 * math.exp(-0.3 * layer_idx)
    lam = (jnp.exp(jnp.sum(lq1.astype(jnp.float32) * lk1.astype(jnp.float32)))
           - jnp.exp(jnp.sum(lq2.astype(jnp.float32) * lk2.astype(jnp.float32))) + lam_init)
    rope = axial_rope_tables(rows, DA_HD)
    q, k, v, z = _diff_project(h, w_in)
    q = apply_axial_rope(q, rope)
    k = apply_axial_rope(k, rope)
    qc, kc, vc, zc = _diff_project(hc, w_in)
    k_all = jnp.concatenate([k, kc], axis=3)
    v_all = jnp.concatenate([v, vc], axis=2)
    nb = S // BLOCK
    qb = jnp.moveaxis(q.reshape(B, DA_HEADS, 2, nb, BLOCK, DA_HD), 3, 0)
    ob = lax.map(lambda qi: diff_attend(qi, k_all, v_all, lam), qb)
    o = jnp.moveaxis(ob, 0, 2).reshape(B, DA_HEADS, S, DA_VD)
    y = _diff_finish(o, z, subln_g, lam_init, w_out)
    yc = None
    if need_ctx:
        yc = _diff_finish(diff_attend(qc, kc, vc, lam), zc, subln_g, lam_init, w_out)
    return y, yc


def centred_mean(u, w):
    T = u.shape[1]
    cs = jnp.pad(jnp.cumsum(u.astype(jnp.float32), axis=1), ((0, 0), (1, 0), (0, 0)))
    t = jnp.arange(T)
    lo = jnp.clip(t - w // 2, 0, T)
    hi = jnp.clip(t - w // 2 + w, 0, T)
    total = cs[:, hi] - cs[:, lo]
    cnt = (hi - lo).astype(jnp.float32)
    return (total / cnt[None, :, None]).astype(u.dtype)


def pool_mixer(h, w_in, w_grp, b_grp, scale, w_out):
    B, T, _ = h.shape
    u, z = jnp.split(h @ w_in, 2, axis=-1)
    ug = u.reshape(B, T, POOL_GROUPS, POOL_GW)
    pooled = jnp.stack([centred_mean(ug[:, :, g], w) for g, w in enumerate(POOL_WINDOWS)], axis=2)
    d = pooled - ug
    y = jnp.einsum('btgc,gcd->btgd', d, w_grp) + b_grp.reshape(POOL_GROUPS, POOL_GW)
    y = y.reshape(B, T, D_INNER) * scale
    return (y * jax.nn.silu(z)) @ w_out


def _gqa_project(t, w_in):
    B, T, _ = t.shape
    kvw = WC_KV * WC_HD
    q, k, v, z = jnp.split(t @ w_in, [D_INNER, D_INNER + kvw, D_INNER + 2 * kvw], axis=-1)
    q = q.reshape(B, T, WC_KV, WC_G, WC_HD).transpose(0, 2, 3, 1, 4)
    k = k.reshape(B, T, WC_KV, WC_HD).transpose(0, 2, 1, 3)
    v = v.reshape(B, T, WC_KV, WC_HD).transpose(0, 2, 1, 3)
    return q, k, v, z


def sink_attend(q, k, v, sink, mask):
    s = jnp.einsum('bngqd,bnkd->bngqk', q, k).astype(jnp.float32) * (WC_HD ** -0.5)
    if mask is not None:
        s = jnp.where(mask, s, -jnp.inf)
    sk = sink.astype(jnp.float32).reshape(1, WC_KV, WC_G, 1, 1)
    m = jnp.maximum(jnp.max(s, axis=-1, keepdims=True), sk)
    e = jnp.exp(s - m)
    p = e / (jnp.sum(e, axis=-1, keepdims=True) + jnp.exp(sk - m))
    return jnp.einsum('bngqk,bnkd->bngqd', p.astype(v.dtype), v)


def _gqa_finish(o, z, w_out):
    B, KV, G, T, HD = o.shape
    o = o.transpose(0, 3, 1, 2, 4).reshape(B, T, KV * G * HD)
    return (o * jax.nn.silu(z)) @ w_out


def window_gqa_mixer(h, hc, w_in, sink, w_out, rows, need_ctx):
    B, S, _ = h.shape
    rope = axial_rope_tables(rows, WC_HD)
    q, k, v, z = _gqa_project(h, w_in)
    q = apply_axial_rope(q, rope)
    k = apply_axial_rope(k, rope)
    qc, kc, vc, zc = _gqa_project(hc, w_in)
    nb = S // BLOCK
    pad = ((0, 0), (0, 0), (BLOCK, BLOCK), (0, 0))
    kp = jnp.pad(k, pad)
    vp = jnp.pad(v, pad)
    qb = jnp.moveaxis(q.reshape(B, WC_KV, WC_G, nb, BLOCK, WC_HD), 3, 0)
    ctx_valid = jnp.ones((BLOCK, kc.shape[2]), dtype=bool)

    def band_block(args):
        qi, bi = args
        start = bi * BLOCK
        kb = lax.dynamic_slice_in_dim(kp, start, 3 * BLOCK, axis=2)
        vb = lax.dynamic_slice_in_dim(vp, start, 3 * BLOCK, axis=2)
        qpos = start + jnp.arange(BLOCK)
        kpos = start - BLOCK + jnp.arange(3 * BLOCK)
        valid = ((jnp.abs(qpos[:, None] - kpos[None, :]) <= WINDOW)
                 & (kpos >= 0)[None, :] & (kpos < S)[None, :])
        mask = jnp.concatenate([valid, ctx_valid], axis=1)
        return sink_attend(qi, jnp.concatenate([kb, kc], axis=2), jnp.concatenate([vb, vc], axis=2), sink, mask)

    ob = lax.map(band_block, (qb, jnp.arange(nb)))
    o = jnp.moveaxis(ob, 0, 3).reshape(B, WC_KV, WC_G, S, WC_HD)
    y = _gqa_finish(o, z, w_out)
    yc = None
    if need_ctx:
        yc = _gqa_finish(sink_attend(qc, kc, vc, sink, None), zc, w_out)
    return y, yc


def setup_inputs(seed: int = 0) -> dict:
    key = jax.random.key(seed)
    ks = jax.random.split(key, 24)
    f32 = jnp.float32
    nrm = lambda k, shape, s: jax.random.normal(k, shape, f32) * s
    w_in_a_cols = 4 * D_INNER
    w_in_c_cols = 2 * D_INNER + 2 * WC_KV * WC_HD
    return {
        'x': nrm(ks[0], (BATCH, SEQ, D_MODEL), 1.0),
        'c': nrm(ks[1], (BATCH, D_MODEL), 1.0),
        'ctx': nrm(ks[2], (BATCH, CTX_LEN, D_MODEL), 1.0),
        'c_ctx': nrm(ks[3], (D_MODEL,), 1.0),
        'norm_g': 1.0 + nrm(ks[4], (DEPTH, D_MODEL), 0.05),
        'w_ada': nrm(ks[5], (DEPTH, D_MODEL, 3 * D_MODEL), 0.5 * D_MODEL ** -0.5),
        'b_ada': nrm(ks[6], (DEPTH, 3 * D_MODEL), 0.02),
        'a_w_in': nrm(ks[7], (N_A, D_MODEL, w_in_a_cols), D_MODEL ** -0.5),
        'a_w_out': nrm(ks[8], (N_A, D_INNER, D_MODEL), D_INNER ** -0.5),
        'a_lam_q1': nrm(ks[9], (N_A, DA_HD), 0.1),
        'a_lam_k1': nrm(ks[10], (N_A, DA_HD), 0.1),
        'a_lam_q2': nrm(ks[11], (N_A, DA_HD), 0.1),
        'a_lam_k2': nrm(ks[12], (N_A, DA_HD), 0.1),
        'a_subln_g': 1.0 + nrm(ks[13], (N_A, DA_VD), 0.05),
        'b_w_in': nrm(ks[14], (N_B, D_MODEL, 2 * D_INNER), D_MODEL ** -0.5),
        'b_w_grp': nrm(ks[15], (N_B, POOL_GROUPS, POOL_GW, POOL_GW), POOL_GW ** -0.5),
        'b_b_grp': nrm(ks[16], (N_B, D_INNER), 0.02),
        'b_scale': 1.0 + nrm(ks[17], (N_B, D_INNER), 0.1),
        'b_w_out': nrm(ks[18], (N_B, D_INNER, D_MODEL), D_INNER ** -0.5),
        'c_w_in': nrm(ks[19], (N_C, D_MODEL, w_in_c_cols), D_MODEL ** -0.5),
        'c_sink': nrm(ks[20], (N_C, WC_HEADS), 1.0),
        'c_w_out': nrm(ks[21], (N_C, D_INNER, D_MODEL), D_INNER ** -0.5),
        'final_g': 1.0 + nrm(ks[22], (D_MODEL,), 0.05),
    }


def reference(x, c, ctx, c_ctx, norm_g, w_ada, b_ada,
              a_w_in, a_w_out, a_lam_q1, a_lam_k1, a_lam_q2, a_lam_k2, a_subln_g,
              b_w_in, b_w_grp, b_b_grp, b_scale, b_w_out,
              c_w_in, c_sink, c_w_out, final_g):
    S = x.shape[1]
    ROWS = S // GRID_W
    xc = ctx
    for i in range(DEPTH):
        m = i % N_MIXERS
        j = i // N_MIXERS
        need_ctx = i < DEPTH - 1
        sh, sc, gt = adaln(c, w_ada[i], b_ada[i])
        h = rmsnorm(x, norm_g[i]) * (1.0 + sc[:, None, :]) + sh[:, None, :]
        if need_ctx or m != 1:
            csh, csc, cgt = adaln(c_ctx, w_ada[i], b_ada[i])
            hc = rmsnorm(xc, norm_g[i]) * (1.0 + csc) + csh
        if m == 0:
            y, yc = diff_attention_mixer(h, hc, a_w_in[j], a_w_out[j], a_lam_q1[j], a_lam_k1[j],
                                         a_lam_q2[j], a_lam_k2[j], a_subln_g[j], i, ROWS, need_ctx)
        elif m == 1:
            y = pool_mixer(h, b_w_in[j], b_w_grp[j], b_b_grp[j], b_scale[j], b_w_out[j])
            yc = pool_mixer(hc, b_w_in[j], b_w_grp[j], b_b_grp[j], b_scale[j], b_w_out[j]) if need_ctx else None
        else:
            y, yc = window_gqa_mixer(h, hc, c_w_in[j], c_sink[j], c_w_out[j], ROWS, need_ctx)
        x = x + gt[:, None, :] * y
        if need_ctx:
            xc = xc + cgt * yc
    return rmsnorm(x, final_g)
```

```python
import contextlib
import math
import numpy as np
import concourse.bass as bass
import concourse.mybir as mybir
from concourse.bass_utils import run_bass_kernel_spmd

F32 = mybir.dt.float32
BF16 = mybir.dt.bfloat16
ALU = mybir.AluOpType
AF = mybir.ActivationFunctionType

D = 1024
DI = 2048
DEPTH = 4
GRID_W = 64
EPS = 1e-6
POOL_WINDOWS = (2, 4, 8, 16)
PAD = 8
PAIR_GROUPS = [[0, 1], [2, 3], [4, 5], [6, 7]]

SAME_ENGINE_SYNC = True
STAGE_LIMIT = 99
SEM_EPOCH_LIMIT = 6000
EMBED_WAIT = True
N_DMA_SEMS = 12


class Buf:
    __slots__ = ("last_w", "readers", "dma_readers")

    def __init__(self):
        self.last_w = None
        self.readers = {}
        self.dma_readers = []


class Op:
    __slots__ = ("eng", "meth", "args", "kw", "deps", "is_dma", "sig_needed", "sig_val", "sem", "val", "epoch")

    def __init__(self, eng, meth, args, kw, is_dma):
        self.eng = eng
        self.meth = meth
        self.args = args
        self.kw = kw
        self.deps = []
        self.is_dma = is_dma
        self.sig_needed = False
        self.sig_val = 0
        self.sem = None
        self.val = 0
        self.epoch = 0


class Sched:
    ENGS = ("pe", "act", "dve", "pool", "sp")
    ATTR = {"pe": "tensor", "act": "scalar", "dve": "vector", "pool": "gpsimd", "sp": "sync"}

    def __init__(self, nc):
        self.nc = nc
        self.ops = {e: [] for e in self.ENGS}
        self.dma_count = {e: 0 for e in self.ENGS}
        self.dma_hist = {e: [] for e in self.ENGS}
        self.bufs = {}
        self.last_real = {e: None for e in self.ENGS}
        self.dmas_since = []
        self.n_coll = 0

    def buf(self, key):
        b = self.bufs.get(key)
        if b is None:
            b = Buf()
            self.bufs[key] = b
        return b

    def _track(self, op, reads, writes):
        deps = op.deps
        rb = [self.buf(k) for k in reads]
        wb = [self.buf(k) for k in writes]
        for k, b in zip(reads, rb):
            if b.last_w is not None:
                deps.append(b.last_w)
            if isinstance(k, tuple) and k[0] == "ps":
                for e2, r in b.readers.items():
                    if e2 != op.eng:
                        deps.append(r)
        for b in wb:
            if b.last_w is not None:
                deps.append(b.last_w)
            deps.extend(b.readers.values())
            deps.extend(b.dma_readers)
        for b in rb:
            if op.is_dma:
                b.dma_readers.append(op)
            else:
                b.readers[op.eng] = op
        for b in wb:
            b.last_w = op
            b.readers = {}
            b.dma_readers = []

    def op(self, eng, meth, *args, reads=(), writes=(), **kw):
        o = Op(eng, meth, args, kw, False)
        self._track(o, reads, writes)
        self.ops[eng].append(o)
        self.last_real[eng] = o
        return o

    def dma(self, eng, out, in_, reads=(), writes=(), **kw):
        o = Op(eng, "dma_start", (), dict(out=out, in_=in_, **kw), True)
        i = self.dma_count[eng]
        self.dma_count[eng] = i + 1
        o.sem = (eng, i % N_DMA_SEMS)
        o.val = 16 * (i // N_DMA_SEMS + 1)
        hist = self.dma_hist[eng]
        if i >= N_DMA_SEMS:
            o.deps.append(hist[i - N_DMA_SEMS])
        hist.append(o)
        self._track(o, reads, writes)
        self.ops[eng].append(o)
        self.dmas_since.append(o)
        return o

    def coll(self, kind, groups, in_ap, out_ap, reads=()):
        o = Op("pool", "collective_compute", (kind, ALU.bypass), dict(replica_groups=groups, ins=[in_ap], outs=[out_ap]), True)
        self.n_coll += 1
        o.sem = ("cc", self.n_coll)
        o.val = 1
        self._track(o, reads, ())
        self.ops["pool"].append(o)
        self.dmas_since.append(o)
        return o

    def barrier(self):
        deps = [o for o in self.last_real.values() if o is not None] + self.dmas_since
        for e in self.ENGS:
            o = Op(e, None, (), {}, False)
            o.deps = list(deps)
            self.ops[e].append(o)
        self.dmas_since = []
        self.bufs = {}

    def finalize(self):
        nc = self.nc
        self.barrier()
        for e in self.ENGS:
            for o in self.ops[e]:
                for d in o.deps:
                    if d.is_dma:
                        continue
                    if d.eng == o.eng and not o.is_dma and (not SAME_ENGINE_SYNC or d.eng == "pe"):
                        continue
                    d.sig_needed = True
        nep = {}
        for e in self.ENGS:
            c = 0
            ep = 0
            for o in self.ops[e]:
                if o.meth is None and c > SEM_EPOCH_LIMIT:
                    ep += 1
                    c = 0
                o.epoch = ep
                if not o.is_dma and o.sig_needed:
                    c += 1
                    o.sig_val = c
            nep[e] = ep + 1
        with contextlib.ExitStack() as stack:
            esem = {}
            dsem = {}
            for e in self.ENGS:
                for ep in range(nep[e]):
                    esem[(e, ep)] = stack.enter_context(nc.semaphore("es_%s_%d" % (e, ep)))
                if self.dma_count[e]:
                    for j in range(N_DMA_SEMS):
                        dsem[(e, j)] = stack.enter_context(nc.semaphore("ds_%s_%d" % (e, j)))
            for j in range(1, self.n_coll + 1):
                dsem[("cc", j)] = stack.enter_context(nc.semaphore("cc_%d" % j))
            block = stack.enter_context(nc.Block())
            for e in self.ENGS:
                self._emit_engine(block, e, self.ops[e], esem, dsem)

    def _emit_engine(self, block, e, ops, esem, dsem):
        known = {}

        def body(eng):
            for o in ops:
                w = {}
                for d in o.deps:
                    if d.is_dma:
                        key = ("d",) + d.sem
                        v = d.val
                    else:
                        if d.eng == e and not o.is_dma and (not SAME_ENGINE_SYNC or e == "pe"):
                            continue
                        key = ("e", d.eng, d.epoch)
                        v = d.sig_val
                    if known.get(key, 0) >= v:
                        continue
                    if w.get(key, 0) < v:
                        w[key] = v
                wl = list(w.items())
                embed = None
                if EMBED_WAIT and o.meth is not None and wl and not o.is_dma:
                    embed = wl.pop()
                for key, v in wl:
                    sem = esem[(key[1], key[2])] if key[0] == "e" else dsem[(key[1], key[2])]
                    eng.wait_ge(sem, v)
                    known[key] = v
                if o.meth is None:
                    continue
                inst = getattr(eng, o.meth)(*o.args, **o.kw)
                if embed is not None:
                    key, v = embed
                    sem = esem[(key[1], key[2])] if key[0] == "e" else dsem[(key[1], key[2])]
                    inst._wait_ge(sem, v)
                    known[key] = v
                if o.is_dma:
                    inst.then_inc(dsem[o.sem], 1 if o.sem[0] == "cc" else 16)
                elif o.sig_needed:
                    inst.then_inc(esem[(e, o.epoch)], 1)

        getattr(block, self.ATTR[e])(body)


def rope_tables(head_dim, rep, pos0, n):
    half = head_dim // 2
    quarter = head_dim // 4
    inv = (10000.0 ** (-np.arange(quarter, dtype=np.float32) / quarter)).astype(np.float32)
    pos = np.arange(pos0, pos0 + n)
    row = (pos // GRID_W).astype(np.float32)
    col = (pos % GRID_W).astype(np.float32)
    cos = np.zeros((128, n), np.float32)
    sin = np.zeros((128, n), np.float32)
    perm = np.zeros((128, 128), np.float32)
    for p in range(128):
        d = p % head_dim
        base = p - d
        hsel = d // half
        dd = d % half
        i = dd % quarter
        ang = ((row if hsel == 0 else col) * inv[i]).astype(np.float32)
        cos[p] = np.cos(ang)
        if dd < quarter:
            partner = d + quarter
            sin[p] = -np.sin(ang)
        else:
            partner = d - quarter
            sin[p] = np.sin(ang)
        perm[base + partner, p] = 1.0
    return cos, sin, perm


def invcnt_tables(nlat, nctx, lat0, lat_total, ctx0, ctx_total):
    T = nlat + nctx
    out = np.zeros((4, T), np.float32)
    for g, w in enumerate(POOL_WINDOWS):
        for (n, o, p0, tot) in ((nlat, 0, lat0, lat_total), (nctx, nlat, ctx0, ctx_total)):
            t = np.arange(p0, p0 + n)
            lo = np.clip(t - w // 2, 0, tot)
            hi = np.clip(t - w // 2 + w, 0, tot)
            out[g, o:o + n] = 1.0 / (hi - lo).astype(np.float32)
    return out


def band_mask():
    k = np.arange(128)[:, None]
    q = np.arange(128)[None, :]
    m = np.ones((128, 384), np.float32)
    m[:, 0:128] = (k <= q)
    m[:, 256:384] = (k >= q)
    return m


def build_program(NLAT, NCTX, depth=DEPTH, debug_x=False, groups=PAIR_GROUPS):
    T = NLAT + NCTX
    NT = T // 128
    NLT = NLAT // 128
    NTK = 2 * NT
    CTX_TILES = [NT - 1, 2 * NT - 1]
    CHUNKS = [(o, 512, 0) for o in range(0, NLAT, 512)] + [(NLAT + o, min(512, NCTX - o), 1) for o in range(0, NCTX, 512)]
    LP = T + 4 * PAD

    nc = bass.Bass("TRN2", target_bir_lowering=False)
    S = Sched(nc)
    coll_groups = groups

    declared = set()

    def din(name, shape, dt=F32):
        declared.add(name)
        return nc.dram_tensor(name, list(shape), dt, kind="ExternalInput").ap()

    x_tok = din("x_tok", [NLAT, D])
    ctx_tok = din("ctx_tok", [NCTX, D])
    cvec = din("cvec", [128, 8, 2])
    w_ada = din("w_ada", [DEPTH, D, 3 * D])
    b_adaT = din("b_adaT", [DEPTH, 128, 24])
    norm_gT = din("norm_gT", [DEPTH, 128, 8])
    final_gT = din("final_gT", [128, 8])
    a_w_in = din("a_w_in", [2, D, 4 * DI]) if depth >= 1 else None
    a_w_out = din("a_w_out", [2, DI, D]) if depth >= 1 else None
    a_lam = din("a_lam", [2, 4, 64])
    a_subln = din("a_subln", [2, 128])
    b_w_in = din("b_w_in", [1, D, 2 * DI]) if depth >= 2 else None
    b_w_grp = din("b_w_grp", [1, 4, 512, 512]) if depth >= 2 else None
    b_bT = din("b_bT", [128, 16])
    b_sT = din("b_sT", [128, 16])
    b_w_out = din("b_w_out", [1, DI, D]) if depth >= 2 else None
    c_w_in = din("c_w_in", [1, D, 5120]) if depth >= 3 else None
    c_sink = din("c_sink", [1, 16])
    c_w_out = din("c_w_out", [1, DI, D]) if depth >= 3 else None
    ident_d = din("ident", [128, 128])
    ropeA_d = din("ropeA", [2, 128, NLAT])
    ropeC_d = din("ropeC", [2, 128, NLAT])
    permA_d = din("permA", [128, 128])
    permC_d = din("permC", [128, 128])
    edge_d = din("edge", [4, 32])
    band_d = din("band", [128, 384])
    hmask_d = din("hmask", [128, 2])
    out_d = nc.dram_tensor("out", [NLAT, D], F32, kind="ExternalOutput").ap()

    xT = nc.dram_tensor("xT_s", [128, 8, T], F32).ap()
    QT = nc.dram_tensor("QT_s", [16, 128, T], BF16).ap()
    KT = nc.dram_tensor("KT_s", [16, 128, T], BF16).ap()
    VD = nc.dram_tensor("VD_s", [16, T, 128], BF16).ap()
    SZ = nc.dram_tensor("SZ_s", [T, DI], F32).ap()
    GT = nc.dram_tensor("GT_s", [128, 16, T], BF16).ap()
    KT_all = nc.dram_tensor("KT_all", [8, 2, 2, 128, T], BF16).ap()
    VD_all = nc.dram_tensor("VD_all", [8, 2, 2, T, 128], BF16).ap()
    KTc = nc.dram_tensor("KTc_s", [4, 128, T], BF16).ap()
    VDc = nc.dram_tensor("VDc_s", [4, T, 128], BF16).ap()
    KTc_all = nc.dram_tensor("KTc_all", [2, 2, 2, 128, T], BF16).ap()
    VDc_all = nc.dram_tensor("VDc_all", [2, 2, 2, T, 128], BF16).ap()
    xh_send = nc.dram_tensor("xh_send", [128, 8, 32], F32).ap()
    xh_all = nc.dram_tensor("xh_all", [2, 128, 8, 32], F32).ap()
    dbg = None
    if debug_x:
        dbg = nc.dram_tensor("dbg", [128, 8, T], F32, kind="ExternalOutput").ap()

    PSall = nc.alloc_psum_tensor("psall", [128, 8 * 512], F32).ap()
    PS = [PSall[:, i * 512:(i + 1) * 512] for i in range(8)]

    def sb(name, shape, dt=F32):
        return nc.alloc_sbuf_tensor(name, list(shape), dt).ap()

    _uid = [0]

    def sbt(name, shape, dt):
        _uid[0] += 1
        return nc.sbuf_tensor("%s_u%d" % (name, _uid[0]), list(shape), dt)

    ident = sb("ident_f", [128, 128])
    identb = sb("ident_b", [128, 128], BF16)
    onesb = sb("ones_b", [128, 128], BF16)
    MODS = sb("mods", [128, DEPTH, 2, 3, 8])
    neglam = sb("neglam", [128, 2])
    subg = sb("subg", [128, 2, 128])
    esink = sb("esink", [128, 16])
    mhalf = sb("mhalf", [128, 1])
    hmask = sb("hmask_sb", [128, 2])
    epsc = sb("epsc", [128, 1])

    def prologue():
        with contextlib.ExitStack() as es:
            def tmp(name, shape, dt=F32):
                return es.enter_context(sbt(name, list(shape), dt)).ap()
            S.dma("sp", ident, ident_d, writes=["ident"])
            S.dma("sp", hmask, hmask_d, writes=["hmask"])
            S.dma("pool", identb, ident_d, writes=["identb"])
            S.op("dve", "memset", onesb, 1.0, writes=["onesb"])
            S.op("dve", "memset", mhalf, -0.5, writes=["mhalf"])
            S.op("dve", "memset", epsc, EPS, writes=["epsc"])
            xin = [tmp("xin%d" % i, [128, D]) for i in range(2)]
            xst = [tmp("xst%d" % i, [128, 8, 128]) for i in range(2)]
            for t in range(NT):
                s = t % 2
                src = x_tok[t * 128:(t + 1) * 128, :] if t < NLT else ctx_tok[(t - NLT) * 128:(t - NLT + 1) * 128, :]
                S.dma("sp", xin[s], src, writes=[("xin", s)])
                for hb in range(2):
                    bank = (t % 2) * 2 + hb
                    for k4 in range(4):
                        k = hb * 4 + k4
                        S.op("pe", "transpose", PS[bank][:, k4 * 128:(k4 + 1) * 128], xin[s][:, k * 128:(k + 1) * 128], ident,
                             reads=[("xin", s), "ident"], writes=[("ps", bank)])
                    dst = xst[s][:, hb * 4:(hb + 1) * 4, :]
                    srcp = PS[bank].rearrange("p (k c) -> p k c", k=4)
                    if hb == 0:
                        S.op("dve", "tensor_copy", dst, srcp, reads=[("ps", bank)], writes=[("xst", s, hb)])
                    else:
                        S.op("act", "copy", dst, srcp, reads=[("ps", bank)], writes=[("xst", s, hb)])
                S.dma("sp", xT[:, :, t * 128:(t + 1) * 128], xst[s], reads=[("xst", s, 0), ("xst", s, 1)])
            cv = tmp("cv", [128, 8, 2])
            cs = tmp("cs", [128, 8, 2])
            S.dma("sp", cv, cvec, writes=["cv"])
            S.op("act", "activation", cs, cv, AF.Silu, reads=["cv"], writes=["cs"])
            wa = [tmp("wa%d" % i, [128, 8, 512]) for i in range(2)]
            modraw = tmp("modraw", [128, 24, 2])
            bT = tmp("bT", [128, DEPTH, 24])
            gT_ = tmp("gT_", [128, DEPTH, 8])
            S.dma("sp", bT, b_adaT.rearrange("l p c -> p l c"), writes=["bT"])
            S.dma("sp", gT_, norm_gT.rearrange("l p c -> p l c"), writes=["gT_"])
            mod = tmp("mod", [128, 2, 24])
            pi = 0
            for i in range(depth):
                for pc in range(6):
                    s = pi % 2
                    pi += 1
                    S.dma("sp", wa[s], w_ada[i].rearrange("(k p) c -> p k c", p=128)[:, :, pc * 512:(pc + 1) * 512], writes=[("wa", s)])
                    for blk in range(4):
                        cb = pc * 4 + blk
                        for k in range(8):
                            S.op("pe", "matmul", PS[4][:, cb * 2:cb * 2 + 2], wa[s][:, k, blk * 128:(blk + 1) * 128], cs[:, k, :],
                                 start=(k == 0), stop=(k == 7), reads=[("wa", s), "cs"], writes=[("ps", 4)])
                S.op("dve", "tensor_copy", modraw, PS[4][:, 0:48].rearrange("p (c j) -> p c j", j=2), reads=[("ps", 4)], writes=["modraw"])
                for kind in range(2):
                    S.op("dve", "tensor_tensor", mod[:, kind, :], modraw[:, :, kind], bT[:, i, :], ALU.add,
                         reads=["modraw", "bT"], writes=[("mod", kind)])
                    S.op("dve", "scalar_tensor_tensor", MODS[:, i, kind, 0, :], mod[:, kind, 8:16], 1.0, gT_[:, i, :], ALU.add, ALU.mult,
                         reads=[("mod", kind), "gT_"], writes=["MODS"])
                    S.op("dve", "tensor_copy", MODS[:, i, kind, 1, :], mod[:, kind, 0:8], reads=[("mod", kind)], writes=["MODS"])
                    S.op("dve", "tensor_copy", MODS[:, i, kind, 2, :], mod[:, kind, 16:24], reads=[("mod", kind)], writes=["MODS"])
            lv = tmp("lv", [128, 2, 4, 64])
            S.dma("sp", lv, a_lam.rearrange("j f d -> (j f d)").partition_broadcast(128).rearrange("p (j f d) -> p j f d", j=2, f=4), writes=["lv"])
            junk = tmp("junk", [128, 64])
            ssum = tmp("ssum", [128, 4])
            esum = tmp("esum", [128, 4])
            for j in range(2):
                for q in range(2):
                    S.op("dve", "scalar_tensor_tensor", junk, lv[:, j, 2 * q, :], 1.0, lv[:, j, 2 * q + 1, :], ALU.mult, ALU.mult,
                         accum_out=ssum[:, 2 * j + q:2 * j + q + 1], reads=["lv"], writes=["junk", "ssum"])
            S.op("act", "activation", esum, ssum, AF.Exp, reads=["ssum"], writes=["esum"])
            for j in range(2):
                li = 0.8 - 0.6 * math.exp(-0.3 * (3 * j))
                S.op("dve", "scalar_tensor_tensor", neglam[:, j:j + 1], esum[:, 2 * j + 1:2 * j + 2], -li, esum[:, 2 * j:2 * j + 1], ALU.add, ALU.subtract,
                     reads=["esum"], writes=["neglam"])
                S.dma("sp", subg[:, j, :], a_subln[j].partition_broadcast(128), writes=[("subg", j)])
                S.op("dve", "tensor_scalar", subg[:, j, :], subg[:, j, :], 1.0 - li, None, ALU.mult, reads=[("subg", j)], writes=[("subg", j)])
            sk = tmp("sk", [128, 16])
            S.dma("sp", sk, c_sink[0].partition_broadcast(128), writes=["sk"])
            S.op("act", "activation", esink, sk, AF.Exp, reads=["sk"], writes=["esink"])
            S.barrier()

    def norm_stage(li, hT, es):
        xc = [es.enter_context(sbt("n_xc%d" % i, [128, 8, 512], F32)).ap() for i in range(2)]
        sq = [es.enter_context(sbt("n_sq%d" % i, [128, 8, 512], BF16)).ap() for i in range(2)]
        rstd = [es.enter_context(sbt("n_rs%d" % i, [128, 512], F32)).ap() for i in range(2)]
        tmpx = [es.enter_context(sbt("n_tx%d" % i, [128, 512], F32)).ap() for i in range(2)]
        for ci, (off, w, kind) in enumerate(CHUNKS):
            s = ci % 2
            bank = ci % 2
            S.dma("sp", xc[s][:, :, :w], xT[:, :, off:off + w], writes=[("xc", s)])
            S.op("dve", "tensor_tensor", sq[s][:, :, :w], xc[s][:, :, :w], xc[s][:, :, :w], ALU.mult, reads=[("xc", s)], writes=[("sq", s)])
            for k in range(8):
                S.op("pe", "matmul", PS[bank][:, :w], onesb, sq[s][:, k, :w], start=(k == 0), stop=(k == 7),
                     reads=["onesb", ("sq", s)], writes=[("ps", bank)])
            S.op("act", "activation", rstd[s][:, :w], PS[bank][:, :w], AF.Sqrt, bias=epsc, scale=1.0 / D,
                 reads=[("ps", bank), "epsc"], writes=[("rstd", s)])
            S.op("dve", "reciprocal", rstd[s][:, :w], rstd[s][:, :w], reads=[("rstd", s)], writes=[("rstd", s)])
            for k in range(8):
                ts = k % 2
                S.op("dve", "tensor_tensor", tmpx[ts][:, :w], xc[s][:, k, :w], rstd[s][:, :w], ALU.mult,
                     reads=[("xc", s), ("rstd", s)], writes=[("tmpx", ts)])
                S.op("act", "activation", hT[:, k, off:off + w], tmpx[ts][:, :w], AF.Identity,
                     bias=MODS[:, li, kind, 1, k:k + 1], scale=MODS[:, li, kind, 0, k:k + 1],
                     reads=[("tmpx", ts)], writes=[("hT", ci)])

    def inproj_stage(w_d, groups, hT, rope_d, perm_d, es, kt_dst, vd_dst, hook=None, hook_at=-1):
        wt = [es.enter_context(sbt("ip_w%d" % i, [128, 8, 512], BF16)).ap() for i in range(2)]
        cosT = es.enter_context(sbt("ip_cos", [128, NLAT], F32)).ap()
        sinT = es.enter_context(sbt("ip_sin", [128, NLAT], F32)).ap()
        permb = es.enter_context(sbt("ip_perm", [128, 128], BF16)).ap()
        qb = [es.enter_context(sbt("ip_qb%d" % i, [128, 512], BF16)).ap() for i in range(2)]
        t1 = [es.enter_context(sbt("ip_t1%d" % i, [128, 512], F32)).ap() for i in range(2)]
        t2 = [es.enter_context(sbt("ip_t2%d" % i, [128, 512], F32)).ap() for i in range(2)]
        stg = [es.enter_context(sbt("ip_st%d" % i, [128, T], BF16)).ap() for i in range(2)]
        vst = [es.enter_context(sbt("ip_vs%d" % i, [128, 512], BF16)).ap() for i in range(2)]
        zst = [es.enter_context(sbt("ip_zs%d" % i, [128, 512], F32)).ap() for i in range(2)]
        S.dma("sp", cosT, rope_d[0], writes=["cosT"])
        S.dma("sp", sinT, rope_d[1], writes=["sinT"])
        S.dma("pool", permb, perm_d, writes=["permb"])
        wv = w_d.rearrange("(k p) c -> p k c", p=128)
        bi = 0
        ti = 0
        for gi, (col0, role, idx0) in enumerate(groups):
            s = gi % 2
            S.dma("pool", wt[s], wv[:, :, col0:col0 + 512], writes=[("wt", s)])
            if hook is not None and gi >= hook_at:
                hook(gi - hook_at, len(groups) - hook_at)
            if role in ("q", "k"):
                dst = QT if role == "q" else kt_dst
                for blk in range(4):
                    ss = bi % 2
                    pending = None

                    def rope_tail(p):
                        bank_, c2_, off_, w_ = p
                        pb = 4 + c2_
                        S.op("pe", "matmul", PS[pb][:, :w_], permb, qb[c2_][:, :w_], start=True, stop=True,
                             reads=["permb", ("qb", c2_)], writes=[("ps", pb)])
                        S.op("dve", "tensor_tensor", t1[c2_][:, :w_], PS[bank_][:, :w_], cosT[:, off_:off_ + w_], ALU.mult,
                             reads=[("ps", bank_), "cosT"], writes=[("t1", c2_)])
                        S.op("dve", "tensor_tensor", t2[c2_][:, :w_], PS[pb][:, :w_], sinT[:, off_:off_ + w_], ALU.mult,
                             reads=[("ps", pb), "sinT"], writes=[("t2", c2_)])
                        S.op("dve", "tensor_tensor", stg[ss][:, off_:off_ + w_], t1[c2_][:, :w_], t2[c2_][:, :w_], ALU.add,
                             reads=[("t1", c2_), ("t2", c2_)], writes=[("stg", ss)])

                    for ci, (off, w, kind) in enumerate(CHUNKS):
                        bank = bi % 2 * 2 + ci % 2
                        c2 = ci % 2
                        for k in range(8):
                            S.op("pe", "matmul", PS[bank][:, :w], wt[s][:, k, blk * 128:(blk + 1) * 128], hT[:, k, off:off + w],
                                 start=(k == 0), stop=(k == 7), reads=[("wt", s), ("hT", ci)], writes=[("ps", bank)])
                        if kind == 0:
                            S.op("act", "copy", qb[c2][:, :w], PS[bank][:, :w], reads=[("ps", bank)], writes=[("qb", c2)])
                            if pending is not None:
                                rope_tail(pending)
                            pending = (bank, c2, off, w)
                        else:
                            S.op("act", "copy", stg[ss][:, off:off + w], PS[bank][:, :w], reads=[("ps", bank)], writes=[("stg", ss)])
                    if pending is not None:
                        rope_tail(pending)
                    S.dma("sp", dst[idx0 + blk], stg[ss], reads=[("stg", ss)], writes=([("KTd", idx0 + blk)] if role == "k" else []))
                    bi += 1
            else:
                for t in range(NT):
                    bank = 6 + ti % 2
                    s2 = ti % 2
                    ti += 1
                    for k in range(8):
                        S.op("pe", "matmul", PS[bank], hT[:, k, t * 128:(t + 1) * 128], wt[s][:, k, :],
                             start=(k == 0), stop=(k == 7), reads=[("wt", s), ("hT", (t * 128) // 512 if t < NLT else len(CHUNKS) - 1)], writes=[("ps", bank)])
                    if role == "v":
                        S.op("dve", "tensor_copy", vst[s2], PS[bank], reads=[("ps", bank)], writes=[("vst", s2)])
                        S.dma("sp", vd_dst[idx0:idx0 + 4, t * 128:(t + 1) * 128, :].rearrange("h t c -> t h c"), vst[s2].rearrange("p (h c) -> p h c", h=4), reads=[("vst", s2)], writes=[("VDd", idx0 // 4, t)])
                    else:
                        S.op("act", "activation", zst[s2], PS[bank], AF.Silu, reads=[("ps", bank)], writes=[("zst", s2)])
                        S.dma("sp", SZ[t * 128:(t + 1) * 128, idx0:idx0 + 512], zst[s2], reads=[("zst", s2)])

    def outproj_stage(li, w_d, need_ctx):
        with contextlib.ExitStack() as es:
            wo = es.enter_context(sbt("op_w", [128, 16, D], BF16)).ap()
            gch = [es.enter_context(sbt("op_g%d" % i, [128, 16, 512], BF16)).ap() for i in range(2)]
            xch = [es.enter_context(sbt("op_x%d" % i, [128, 8, 512], F32)).ap() for i in range(2)]
            wv = w_d.rearrange("(k p) c -> p k c", p=128)
            for h in range(2):
                S.dma("pool", wo[:, h * 8:(h + 1) * 8, :], wv[:, h * 8:(h + 1) * 8, :], writes=[("wo", h)])
            ci = 0
            for (off, w, kind) in CHUNKS:
                if kind == 1 and not need_ctx:
                    continue
                s = ci % 2
                ci += 1
                S.dma("sp", gch[s][:, :, :w], GT[:, :, off:off + w], writes=[("gch", s)])
                S.dma("sp", xch[s][:, :, :w], xT[:, :, off:off + w], writes=[("xch", s)])
                for blk in range(8):
                    bank = blk % 4
                    for k in range(16):
                        S.op("pe", "matmul", PS[bank][:, :w], wo[:, k, blk * 128:(blk + 1) * 128], gch[s][:, k, :w],
                             start=(k == 0), stop=(k == 15), reads=[("wo", k // 8), ("gch", s)], writes=[("ps", bank)])
                    S.op("dve", "scalar_tensor_tensor", xch[s][:, blk, :w], PS[bank][:, :w], MODS[:, li, kind, 2, blk:blk + 1], xch[s][:, blk, :w],
                         ALU.mult, ALU.add, reads=[("ps", bank), ("xch", s)], writes=[("xch", s)])
                S.dma("sp", xT[:, :, off:off + w], xch[s][:, :, :w], reads=[("xch", s)])
            S.barrier()

    def layer_A(li, j, need_ctx):
        with contextlib.ExitStack() as es:
            hT = es.enter_context(sbt("hT", [128, 8, T], BF16)).ap()
            with contextlib.ExitStack() as es2:
                norm_stage(li, hT, es2)
                S.barrier()
            if STAGE_LIMIT <= 1:
                return None
            with contextlib.ExitStack() as es2:
                groups = [(2048 + cg * 512, "k", cg * 4) for cg in range(4)] + [(4096 + cg * 512, "v", cg * 4) for cg in range(4)] \
                    + [(cg * 512, "q", cg * 4) for cg in range(4)] + [(6144 + cg * 512, "z", cg * 512) for cg in range(4)]

                def gatherA(step, nsteps):
                    per = (8 + nsteps - 1) // nsteps
                    for gj in range(step * per, min(8, (step + 1) * per)):
                        S.coll("AllGather", coll_groups, KT[2 * gj:2 * gj + 2].rearrange("h p t -> (h p) t"),
                               KT_all[gj].rearrange("r h p t -> (r h p) t"), reads=[("KTd", 2 * gj), ("KTd", 2 * gj + 1)])
                        S.coll("AllGather", coll_groups, VD[2 * gj:2 * gj + 2].rearrange("h t c -> (h t) c"),
                               VD_all[gj].rearrange("r h t c -> (r h t) c"), reads=[("VDd", gj // 2, t) for t in range(NT)])
                inproj_stage(a_w_in[j], groups, hT, ropeA_d, permA_d, es2, KT, VD, hook=gatherA, hook_at=9)
                S.barrier()
        if STAGE_LIMIT <= 2:
            return
        with contextlib.ExitStack() as es:
            def tmp(name, shape, dt=F32):
                return es.enter_context(sbt(name, list(shape), dt)).ap()
            kt = [tmp("a_kt%d" % i, [128, 2 * T], BF16) for i in range(2)]
            qt = [tmp("a_qt%d" % i, [128, T], BF16) for i in range(2)]
            vt = [tmp("a_vt%d" % i, [128, NTK, 130], BF16) for i in range(2)]
            pt = [tmp("a_pt%d" % i, [128, 1024], BF16) for i in range(4)]
            accB = [tmp("a_acc%d" % i, [128, 3, 480]) for i in range(2)]
            szt = [tmp("a_sz%d" % i, [128, 4, 128]) for i in range(2)]
            wg = [tmp("a_wg%d" % i, [128, 4, 128]) for i in range(2)]
            rec = [tmp("a_rec%d" % i, [128, 8]) for i in range(2)]
            t0 = [tmp("a_t0%d" % i, [128, 128]) for i in range(4)]
            ot = [tmp("a_o%d" % i, [128, 128]) for i in range(4)]
            junk = [tmp("a_junk%d" % i, [128, 128]) for i in range(4)]
            ssq = tmp("a_ssq", [128, 4])
            rs = tmp("a_rs", [128, 4])
            gtok = [tmp("a_gt%d" % i, [128, 512], BF16) for i in range(2)]
            gst = [tmp("a_gs%d" % i, [128, T], BF16) for i in range(2)]
            for s in range(2):
                S.op("pool", "memset", vt[s][:, :, 128:130], 1.0, writes=[("vt", s)])
            SZv = SZ.rearrange("(t p) c -> p t c", p=128)
            trp = PS[4].bitcast(BF16)
            cidx = 0
            sidx = 0
            def load_head(h_):
                hs_ = h_ % 2
                S.dma("sp", kt[hs_].rearrange("p (r t) -> p r t", r=2), KT_all[h_ // 2, :, h_ % 2].rearrange("r p t -> p r t"), writes=[("kt", hs_)])
                S.dma("sp", qt[hs_], QT[h_], writes=[("qt", hs_)])
                for r_ in range(2):
                    S.dma("sp", vt[hs_][:, r_ * NT:(r_ + 1) * NT, 0:128], VD_all[h_ // 2, r_, h_ % 2].rearrange("(t p) c -> p t c", p=128), writes=[("vt", hs_)])

            load_head(0)
            for h in range(16):
                hs = h % 2
                for chi, (off, w, kind) in enumerate(CHUNKS):
                    if chi == 1 and h + 1 < 16:
                        load_head(h + 1)
                    if kind == 1 and not need_ctx:
                        continue
                    cs_ = cidx % 2
                    cidx += 1
                    nq = w // 128
                    t0i = off // 128
                    keytiles = list(range(NTK)) if kind == 0 else CTX_TILES
                    S.dma("sp", szt[cs_][:, :nq, :], SZv[:, t0i:t0i + nq, h * 128:(h + 1) * 128], writes=[("szt", cs_)])
                    for qs in range(nq):
                        S.op("pool", "tensor_tensor", wg[cs_][:, qs, :], szt[cs_][:, qs, :], subg[:, j, :], ALU.mult,
                             reads=[("szt", cs_)], writes=[("wg", cs_)])
                    started = set()
                    nkt = len(keytiles)

                    def qk_exp(kti):
                        pr = kti % 2
                        ktile = keytiles[kti]
                        for m in range(2):
                            bank = 2 * pr + m
                            S.op("pe", "matmul", PS[bank][:, :w], kt[hs][64 * m:64 * m + 64, ktile * 128:(ktile + 1) * 128],
                                 qt[hs][64 * m:64 * m + 64, off:off + w], start=True, stop=True,
                                 reads=[("kt", hs), ("qt", hs)], writes=[("ps", 2 * pr), ("ps", 2 * pr + 1)] if m == 1 else [("ps", 2 * pr)])
                        src = PSall[:, 2 * pr * 512:(2 * pr + 2) * 512].rearrange("p (m c) -> p m c", m=2)[:, :, 0:w]
                        dst = pt[kti % 4].rearrange("p (m c) -> p m c", m=2)[:, :, 0:w]
                        S.op("act", "activation", dst, src, AF.Exp, scale=0.125,
                             reads=[("ps", 2 * pr), ("ps", 2 * pr + 1)], writes=[("pt", kti % 4)])

                    qk_exp(0)
                    if nkt > 1:
                        qk_exp(1)
                    for kti, ktile in enumerate(keytiles):
                        if kti + 2 < nkt:
                            qk_exp(kti + 2)
                        pr = kti % 4
                        for m in range(2):
                            for qs in range(nq):
                                a = m * nq + qs
                                bank = 5 + a // 3
                                c0 = (a % 3) * 160
                                st = bank not in started
                                started.add(bank)
                                S.op("pe", "matmul", PS[bank][:, c0:c0 + 129], pt[pr][:, m * 512 + qs * 128:m * 512 + (qs + 1) * 128], vt[hs][:, ktile, 0:129],
                                     start=st, stop=(kti == nkt - 1), skip_group_check=True,
                                     reads=[("pt", pr), ("vt", hs)], writes=[("ps", bank)])
                    na = 2 * nq
                    accv = accB[cs_].rearrange("p b (a c) -> p (b a) c", a=3)
                    for b3 in range((na + 2) // 3):
                        cnt = min(3, na - 3 * b3)
                        S.op("dve", "tensor_copy", accv[:, 3 * b3:3 * b3 + cnt, 0:129],
                             PS[5 + b3][:, 0:480].rearrange("p (a c) -> p a c", a=3)[:, 0:cnt, 0:129],
                             reads=[("ps", 5 + b3)], writes=[("accB", cs_)])
                    S.op("dve", "reciprocal", rec[cs_][:, 0:na], accv[:, 0:na, 128], reads=[("accB", cs_)], writes=[("rec", cs_)])
                    S.op("dve", "tensor_scalar", rec[cs_][:, nq:na], rec[cs_][:, nq:na], neglam[:, j:j + 1], None, ALU.mult,
                         reads=[("rec", cs_)], writes=[("rec", cs_)])
                    for qs in range(nq):
                        S.op("dve", "tensor_scalar", t0[qs], accv[:, qs, 0:128], rec[cs_][:, qs:qs + 1], None, ALU.mult,
                             reads=[("accB", cs_), ("rec", cs_)], writes=[("t0", qs)])
                    for qs in range(nq):
                        S.op("dve", "scalar_tensor_tensor", ot[qs], accv[:, nq + qs, 0:128], rec[cs_][:, nq + qs:nq + qs + 1], t0[qs], ALU.mult, ALU.add,
                             reads=[("accB", cs_), ("rec", cs_), ("t0", qs)], writes=[("ot", qs)])
                    for qs in range(nq):
                        S.op("dve", "scalar_tensor_tensor", junk[qs], ot[qs], 1.0 / 128.0, ot[qs], ALU.mult, ALU.mult,
                             accum_out=ssq[:, qs:qs + 1], reads=[("ot", qs)], writes=[("junk", qs), ("ssq", qs)])
                    for qs in range(nq):
                        S.op("dve", "tensor_scalar", ssq[:, qs:qs + 1], ssq[:, qs:qs + 1], EPS, None, ALU.add,
                             reads=[("ssq", qs)], writes=[("ssq", qs)])
                    for qs in range(nq):
                        S.op("pool", "tensor_tensor", rs[:, qs:qs + 1], ssq[:, qs:qs + 1], mhalf, ALU.pow,
                             reads=[("ssq", qs), "mhalf"], writes=[("rs", qs)])
                    for qs in range(nq):
                        S.op("dve", "scalar_tensor_tensor", gtok[cs_][:, qs * 128:(qs + 1) * 128], ot[qs], rs[:, qs:qs + 1], wg[cs_][:, qs, :],
                             ALU.mult, ALU.mult, reads=[("ot", qs), ("rs", qs), ("wg", cs_)], writes=[("gtok", cs_, qs)])
                    for qs in range(nq):
                        S.op("pe", "transpose", trp[:, qs * 128:(qs + 1) * 128], gtok[cs_][:, qs * 128:(qs + 1) * 128], identb,
                             reads=[("gtok", cs_, qs), "identb"], writes=[("ps", 4)])
                    S.op("dve", "tensor_copy", gst[hs][:, off:off + w], trp[:, 0:w], reads=[("ps", 4)], writes=[("gst", hs)])
                if need_ctx:
                    S.dma("sp", GT[:, h, :], gst[hs], reads=[("gst", hs)])
                else:
                    S.dma("sp", GT[:, h, 0:NLAT], gst[hs][:, 0:NLAT], reads=[("gst", hs)])
            S.barrier()
        if STAGE_LIMIT <= 3:
            return
        outproj_stage(li, a_w_out[j], need_ctx)

    def layer_C(li, need_ctx):
        with contextlib.ExitStack() as es:
            hT = es.enter_context(sbt("hT", [128, 8, T], BF16)).ap()
            with contextlib.ExitStack() as es2:
                norm_stage(li, hT, es2)
                S.barrier()
            with contextlib.ExitStack() as es2:
                groups = [(2048, "k", 0), (2560, "v", 0)] + [(cg * 512, "q", cg * 4) for cg in range(4)] \
                    + [(3072 + cg * 512, "z", cg * 512) for cg in range(4)]

                def gatherC(step, nsteps):
                    for gj in ([0, 1] if step == 0 else []):
                        S.coll("AllGather", coll_groups, KTc[2 * gj:2 * gj + 2].rearrange("h p t -> (h p) t"),
                               KTc_all[gj].rearrange("r h p t -> (r h p) t"), reads=[("KTd", 2 * gj), ("KTd", 2 * gj + 1)])
                        S.coll("AllGather", coll_groups, VDc[2 * gj:2 * gj + 2].rearrange("h t c -> (h t) c"),
                               VDc_all[gj].rearrange("r h t c -> (r h t) c"), reads=[("VDd", 0, t) for t in range(NT)])
                inproj_stage(c_w_in[0], groups, hT, ropeC_d, permC_d, es2, KTc, VDc, hook=gatherC, hook_at=3)
                S.barrier()
        scale = 128.0 ** -0.5
        XB, XA, XC0, XC1 = NT, NT + 1, NT + 2, NT + 3
        with contextlib.ExitStack() as es:
            def tmp(name, shape, dt=F32):
                return es.enter_context(sbt(name, list(shape), dt)).ap()
            kt = [tmp("c_kt%d" % i, [128, (NT + 4) * 128], BF16) for i in range(2)]
            qt = [tmp("c_qt%d" % i, [128, T], BF16) for i in range(2)]
            vt = [tmp("c_vt%d" % i, [128, NT + 4, 130], BF16) for i in range(2)]
            pt = [tmp("c_pt%d" % i, [128, 512], BF16) for i in range(4)]
            bandm = tmp("c_band", [128, 384], BF16)
            hbm = tmp("c_hbm", [128, 128], BF16)
            ham = tmp("c_ham", [128, 128], BF16)
            accB = [tmp("c_acc%d" % i, [128, 2, 480]) for i in range(2)]
            szt = [tmp("c_sz%d" % i, [128, 4, 128]) for i in range(2)]
            den = [tmp("c_den%d" % i, [128, 4]) for i in range(2)]
            gtok = [tmp("c_gt%d" % i, [128, 512], BF16) for i in range(2)]
            gst = [tmp("c_gs%d" % i, [128, T], BF16) for i in range(2)]
            S.dma("pool", bandm, band_d, writes=["bandm"])
            S.op("dve", "tensor_scalar", hbm, bandm[:, 256:384], hmask[:, 0:1], None, ALU.mult, reads=["bandm"], writes=["hbm"])
            S.op("dve", "tensor_scalar", ham, bandm[:, 0:128], hmask[:, 1:2], None, ALU.mult, reads=["bandm"], writes=["ham"])
            for s in range(2):
                S.op("pool", "memset", vt[s][:, :, 128:130], 1.0, writes=[("vt", s)])
            SZv = SZ.rearrange("(t p) c -> p t c", p=128)
            trp = PS[4].bitcast(BF16)
            cidx = 0
            sidx = 0
            hcount = 0
            def load_kv(n_):
                ns_ = n_ % 2
                S.dma("sp", kt[ns_][:, 0:NLAT], KTc[n_][:, 0:NLAT], writes=[("kt", ns_)])
                ka = KTc_all[n_ // 2, :, n_ % 2]
                va = VDc_all[n_ // 2, :, n_ % 2]
                S.dma("sp", kt[ns_][:, XB * 128:(XB + 1) * 128], ka[0][:, NLAT - 128:NLAT], writes=[("kt", ns_)])
                S.dma("sp", kt[ns_][:, XA * 128:(XA + 1) * 128], ka[1][:, 0:128], writes=[("kt", ns_)])
                S.dma("sp", kt[ns_][:, XC0 * 128:(XC0 + 1) * 128], ka[0][:, NLAT:NLAT + 128], writes=[("kt", ns_)])
                S.dma("sp", kt[ns_][:, XC1 * 128:(XC1 + 1) * 128], ka[1][:, NLAT:NLAT + 128], writes=[("kt", ns_)])
                S.dma("sp", vt[ns_][:, 0:NLT, 0:128], VDc[n_][0:NLAT].rearrange("(t p) c -> p t c", p=128), writes=[("vt", ns_)])
                S.dma("sp", vt[ns_][:, XB, 0:128], va[0][NLAT - 128:NLAT, :], writes=[("vt", ns_)])
                S.dma("sp", vt[ns_][:, XA, 0:128], va[1][0:128, :], writes=[("vt", ns_)])
                S.dma("sp", vt[ns_][:, XC0, 0:128], va[0][NLAT:NLAT + 128, :], writes=[("vt", ns_)])
                S.dma("sp", vt[ns_][:, XC1, 0:128], va[1][NLAT:NLAT + 128, :], writes=[("vt", ns_)])

            load_kv(0)
            S.dma("sp", qt[0], QT[0], writes=[("qt", 0)])
            for n in range(4):
                ns = n % 2
                for g in range(4):
                    hq = 4 * n + g
                    hs = hcount % 2
                    hcount += 1
                    if hq + 1 < 16:
                        S.dma("sp", qt[(hs + 1) % 2], QT[hq + 1], writes=[("qt", (hs + 1) % 2)])
                    if g == 1 and n + 1 < 4:
                        load_kv(n + 1)
                    for (off, w, kind) in CHUNKS:
                        if kind == 1 and not need_ctx:
                            continue
                        cs_ = cidx % 2
                        cidx += 1
                        nq = w // 128
                        qb0 = off // 128
                        S.dma("sp", szt[cs_][:, :nq, :], SZv[:, qb0:qb0 + nq, hq * 128:(hq + 1) * 128], writes=[("szt", cs_)])
                        items = []
                        if kind == 0:
                            for kb in range(qb0 - 1, qb0 + nq + 1):
                                qlo = max(kb - 1, qb0)
                                qhi = min(kb + 1, qb0 + nq - 1)
                                if kb < 0:
                                    items.append((XB, qlo, qhi, {0: hbm}))
                                elif kb >= NLT:
                                    items.append((XA, qlo, qhi, {NLT - 1: ham}))
                                else:
                                    mk = {}
                                    for qb in range(qlo, qhi + 1):
                                        if kb == qb + 1:
                                            mk[qb] = bandm[:, 0:128]
                                        elif kb == qb - 1:
                                            mk[qb] = bandm[:, 256:384]
                                    items.append((kb, qlo, qhi, mk))
                        items.append((XC0, qb0, qb0 + nq - 1, {}))
                        items.append((XC1, qb0, qb0 + nq - 1, {}))
                        lastc = {}
                        for ii, (kb, qlo, qhi, mk) in enumerate(items):
                            for qb in range(qlo, qhi + 1):
                                lastc[qb] = ii
                        started = set()
                        sbase = sidx
                        sidx += len(items)

                        def s_exp(ii):
                            kb, qlo, qhi, mk = items[ii]
                            bank = (sbase + ii) % 4
                            ncol = (qhi - qlo + 1) * 128
                            S.op("pe", "matmul", PS[bank][:, :ncol], kt[ns][:, kb * 128:(kb + 1) * 128], qt[hs][:, qlo * 128:(qhi + 1) * 128],
                                 start=True, stop=True, reads=[("kt", ns), ("qt", hs)], writes=[("ps", bank)])
                            S.op("act", "activation", pt[bank][:, :ncol], PS[bank][:, :ncol], AF.Exp, scale=scale,
                                 reads=[("ps", bank)], writes=[("pt", bank)])
                            for qb in range(qlo, qhi + 1):
                                c = (qb - qlo) * 128
                                if qb in mk:
                                    S.op("pool", "tensor_tensor", pt[bank][:, c:c + 128], pt[bank][:, c:c + 128], mk[qb], ALU.mult,
                                         reads=[("pt", bank), "bandm", "hbm", "ham"], writes=[("pt", bank)])

                        s_exp(0)
                        if len(items) > 1:
                            s_exp(1)
                        for ii, (kb, qlo, qhi, mk) in enumerate(items):
                            if ii + 2 < len(items):
                                s_exp(ii + 2)
                            bank = (sbase + ii) % 4
                            for qb in range(qlo, qhi + 1):
                                c = (qb - qlo) * 128
                                a = qb - qb0
                                abank = 5 + a // 3
                                c0 = (a % 3) * 160
                                st = abank not in started
                                started.add(abank)
                                S.op("pe", "matmul", PS[abank][:, c0:c0 + 129], pt[bank][:, c:c + 128], vt[ns][:, kb, 0:129],
                                     start=st, stop=(lastc[qb] == ii), skip_group_check=True,
                                     reads=[("pt", bank), ("vt", ns)], writes=[("ps", abank)])
                        accv = accB[cs_].rearrange("p b (a c) -> p (b a) c", a=3)
                        for b3 in range((nq + 2) // 3):
                            cnt = min(3, nq - 3 * b3)
                            S.op("dve", "tensor_copy", accv[:, 3 * b3:3 * b3 + cnt, 0:129],
                                 PS[5 + b3][:, 0:480].rearrange("p (a c) -> p a c", a=3)[:, 0:cnt, 0:129],
                                 reads=[("ps", 5 + b3)], writes=[("accB", cs_)])
                        S.op("dve", "tensor_scalar", den[cs_][:, 0:nq], accv[:, 0:nq, 128], esink[:, hq:hq + 1], None, ALU.add,
                             reads=[("accB", cs_), "esink"], writes=[("den", cs_)])
                        S.op("dve", "reciprocal", den[cs_][:, 0:nq], den[cs_][:, 0:nq], reads=[("den", cs_)], writes=[("den", cs_)])
                        for qs in range(nq):
                            S.op("dve", "scalar_tensor_tensor", gtok[cs_][:, qs * 128:(qs + 1) * 128], accv[:, qs, 0:128], den[cs_][:, qs:qs + 1],
                                 szt[cs_][:, qs, :], ALU.mult, ALU.mult,
                                 reads=[("accB", cs_), ("den", cs_), ("szt", cs_)], writes=[("gtok", cs_, qs)])
                        for qs in range(nq):
                            S.op("pe", "transpose", trp[:, qs * 128:(qs + 1) * 128], gtok[cs_][:, qs * 128:(qs + 1) * 128], identb,
                                 reads=[("gtok", cs_, qs), "identb"], writes=[("ps", 4)])
                        S.op("dve", "tensor_copy", gst[hs][:, off:off + w], trp[:, 0:w], reads=[("ps", 4)], writes=[("gst", hs)])
                    if need_ctx:
                        S.dma("sp", GT[:, hq, :], gst[hs], reads=[("gst", hs)])
                    else:
                        S.dma("sp", GT[:, hq, 0:NLAT], gst[hs][:, 0:NLAT], reads=[("gst", hs)])
            S.barrier()
        outproj_stage(li, c_w_out[0], need_ctx)

    def layer_B(li, need_ctx):
        with contextlib.ExitStack() as es:
            def tmp(name, shape, dt=F32):
                return es.enter_context(sbt(name, list(shape), dt)).ap()
            hT = tmp("hT", [128, 8, T], BF16)
            hTh = tmp("hTh", [128, 8, 32], BF16)
            S.dma("sp", xh_send[:, :, 0:8], xT[:, :, 0:8])
            S.dma("sp", xh_send[:, :, 8:16], xT[:, :, NLAT - 8:NLAT])
            S.dma("sp", xh_send[:, :, 16:24], xT[:, :, NLAT:NLAT + 8])
            S.dma("sp", xh_send[:, :, 24:32], xT[:, :, T - 8:T])
            S.barrier()
            S.coll("AllGather", coll_groups, xh_send.rearrange("p k c -> (p k) c"), xh_all.rearrange("r p k c -> (r p k) c"))
            S.barrier()
            with contextlib.ExitStack() as es2:
                norm_stage(li, hT, es2)
                hx = es2.enter_context(sbt("b_hx", [128, 8, 32], F32)).ap()
                sqh = es2.enter_context(sbt("b_sqh", [128, 8, 32], BF16)).ap()
                rsh = es2.enter_context(sbt("b_rsh", [128, 32], F32)).ap()
                tmh = es2.enter_context(sbt("b_tmh", [128, 8, 32], F32)).ap()
                S.dma("sp", hx[:, :, 0:8], xh_all[0][:, :, 8:16], writes=["hx"])
                S.dma("sp", hx[:, :, 8:16], xh_all[1][:, :, 0:8], writes=["hx"])
                S.dma("sp", hx[:, :, 16:24], xh_all[0][:, :, 24:32], writes=["hx"])
                S.dma("sp", hx[:, :, 24:32], xh_all[1][:, :, 16:24], writes=["hx"])
                S.op("dve", "tensor_tensor", sqh, hx, hx, ALU.mult, reads=["hx"], writes=["sqh"])
                for k in range(8):
                    S.op("pe", "matmul", PS[7][:, 0:32], onesb, sqh[:, k, :], start=(k == 0), stop=(k == 7),
                         reads=["onesb", "sqh"], writes=[("ps", 7)])
                S.op("act", "activation", rsh, PS[7][:, 0:32], AF.Sqrt, bias=epsc, scale=1.0 / D, reads=[("ps", 7)], writes=["rsh"])
                S.op("dve", "reciprocal", rsh, rsh, reads=["rsh"], writes=["rsh"])
                for k in range(8):
                    S.op("dve", "tensor_tensor", tmh[:, k, :], hx[:, k, :], rsh, ALU.mult, reads=["hx", "rsh"], writes=[("tmh", k)])
                    for (c0, c1, kind) in ((0, 16, 0), (16, 32, 1)):
                        S.op("act", "activation", hTh[:, k, c0:c1], tmh[:, k, c0:c1], AF.Identity,
                             bias=MODS[:, li, kind, 1, k:k + 1], scale=MODS[:, li, kind, 0, k:k + 1],
                             reads=[("tmh", k)], writes=["hTh"])
                S.barrier()
            up = tmp("b_up", [128, LP])
            ab = [tmp("b_ab%d" % i, [128, LP]) for i in range(2)]
            dT = [tmp("b_d%d" % i, [128, T], BF16) for i in range(4)]
            w1 = tmp("b_w1", [128, 8, 512], BF16)
            wgr = tmp("b_wg", [128, 4, 512], BF16)
            edge = tmp("b_edge", [128, 4, 32])
            etmp = tmp("b_etmp", [128, 8])
            bgs = tmp("b_bgs", [128, 2, 16])
            szs = [tmp("b_sz%d" % i, [128, 512]) for i in range(2)]
            tys = [tmp("b_ty%d" % i, [128, 512]) for i in range(2)]
            gch = [tmp("b_gc%d" % i, [128, 512], BF16) for i in range(2)]
            S.dma("sp", bgs[:, 0, :], b_bT, writes=["bgs"])
            S.dma("sp", bgs[:, 1, :], b_sT, writes=["bgs"])
            S.dma("sp", edge, edge_d.rearrange("g e -> (g e)").partition_broadcast(128).rearrange("p (g e) -> p g e", g=4), writes=["edge"])
            S.op("dve", "memset", up, 0.0, writes=["up"])
            wv = b_w_in[0].rearrange("(k p) c -> p k c", p=128)
            regions = [(PAD, 0, NLAT), (3 * PAD + NLAT, NLAT, NCTX)]
            ci2 = 0
            for g in range(4):
                wd = POOL_WINDOWS[g]
                nst = int(math.log2(wd))
                S.dma("pool", w1, wv[:, :, g * 512:(g + 1) * 512], writes=["w1"])
                S.dma("pool", wgr, b_w_grp[0, g].rearrange("(k p) c -> p k c", p=128), writes=["wgr"])
                for kc in range(4):
                    for ci, (off, w, kind) in enumerate(CHUNKS):
                        bank = ci % 4
                        for k in range(8):
                            S.op("pe", "matmul", PS[bank][:, :w], w1[:, k, kc * 128:(kc + 1) * 128], hT[:, k, off:off + w],
                                 start=(k == 0), stop=(k == 7), reads=["w1", ("hT", ci)], writes=[("ps", bank)])
                        pos = (PAD + off) if kind == 0 else (3 * PAD + off)
                        S.op("act", "copy", up[:, pos:pos + w], PS[bank][:, :w], reads=[("ps", bank)], writes=["up"])
                    for k in range(8):
                        S.op("pe", "matmul", PS[7][:, 0:32], w1[:, k, kc * 128:(kc + 1) * 128], hTh[:, k, :],
                             start=(k == 0), stop=(k == 7), reads=["w1", "hTh"], writes=[("ps", 7)])
                    for (pc, hc0, mi) in ((0, 0, 0), (PAD + NLAT, 8, 1), (2 * PAD + NLAT, 16, 0), (3 * PAD + T, 24, 1)):
                        S.op("dve", "tensor_scalar", up[:, pc:pc + 8], PS[7][:, hc0:hc0 + 8], hmask[:, mi:mi + 1], None, ALU.mult,
                             reads=[("ps", 7)], writes=["up"])
                    cur = up
                    curkey = "up"
                    L = LP
                    for s_ in range(nst):
                        sh = 2 ** s_
                        nxt = ab[s_ % 2]
                        nkey = ("ab", s_ % 2)
                        Ln = L - sh
                        S.op("pool", "tensor_tensor", nxt[:, 0:Ln], cur[:, 0:Ln], cur[:, sh:sh + Ln], ALU.add,
                             reads=[curkey], writes=[nkey])
                        cur, curkey, L = nxt, nkey, Ln
                    for (p0, t0_, n_) in regions:
                        S.op("dve", "scalar_tensor_tensor", dT[kc][:, t0_:t0_ + n_], cur[:, p0 - wd // 2:p0 - wd // 2 + n_], 1.0 / wd,
                             up[:, p0:p0 + n_], ALU.mult, ALU.subtract, reads=[curkey, "up"], writes=[("dT", kc)])
                    for ri, (p0, t0_, n_) in enumerate(regions):
                        for side in range(2):
                            e0 = 0 if side == 0 else n_ - 8
                            S.op("dve", "tensor_tensor", etmp, cur[:, p0 - wd // 2 + e0:p0 - wd // 2 + e0 + 8],
                                 edge[:, g, (ri * 2 + side) * 8:(ri * 2 + side) * 8 + 8], ALU.mult, reads=[curkey, "edge"], writes=["etmp"])
                            S.op("dve", "tensor_tensor", dT[kc][:, t0_ + e0:t0_ + e0 + 8], etmp, up[:, p0 + e0:p0 + e0 + 8], ALU.subtract,
                                 reads=["etmp", "up"], writes=[("dT", kc)])
                S.dma("pool", w1, wv[:, :, 2048 + g * 512:2048 + (g + 1) * 512], writes=["w1"])
                for ob in range(4):
                    jb = 4 * g + ob
                    for ci, (off, w, kind) in enumerate(CHUNKS):
                        if kind == 1 and not need_ctx:
                            continue
                        c2 = ci2 % 2
                        ci2 += 1
                        by = 4 + c2
                        bz = 6 + c2
                        for kc in range(4):
                            S.op("pe", "matmul", PS[by][:, :w], wgr[:, kc, ob * 128:(ob + 1) * 128], dT[kc][:, off:off + w],
                                 start=(kc == 0), stop=(kc == 3), reads=["wgr", ("dT", kc)], writes=[("ps", by)])
                        for k in range(8):
                            S.op("pe", "matmul", PS[bz][:, :w], w1[:, k, ob * 128:(ob + 1) * 128], hT[:, k, off:off + w],
                                 start=(k == 0), stop=(k == 7), reads=["w1", ("hT", ci)], writes=[("ps", bz)])
                        S.op("act", "activation", szs[c2][:, :w], PS[bz][:, :w], AF.Silu, reads=[("ps", bz)], writes=[("szs", c2)])
                        S.op("dve", "tensor_scalar", tys[c2][:, :w], PS[by][:, :w], bgs[:, 0, jb:jb + 1], bgs[:, 1, jb:jb + 1], ALU.add, ALU.mult,
                             reads=[("ps", by), "bgs"], writes=[("tys", c2)])
                        S.op("dve", "tensor_tensor", gch[c2][:, :w], tys[c2][:, :w], szs[c2][:, :w], ALU.mult,
                             reads=[("tys", c2), ("szs", c2)], writes=[("gch", c2)])
                        S.dma("sp", GT[:, jb, off:off + w], gch[c2][:, :w], reads=[("gch", c2)])
            S.barrier()
        outproj_stage(li, b_w_out[0], need_ctx)

    def final_stage():
        with contextlib.ExitStack() as es:
            def tmp(name, shape, dt=F32):
                return es.enter_context(sbt(name, list(shape), dt)).ap()
            fg = tmp("f_g", [128, 8])
            xc = [tmp("f_xc%d" % i, [128, 8, 512]) for i in range(2)]
            sq = [tmp("f_sq%d" % i, [128, 8, 512], BF16) for i in range(2)]
            rstd = [tmp("f_rs%d" % i, [128, 512]) for i in range(2)]
            ot = [tmp("f_ot%d" % i, [128, D]) for i in range(2)]
            S.dma("sp", fg, final_gT, writes=["fg"])
            oi = 0
            for ci, (off, w, kind) in enumerate(CHUNKS):
                if kind == 1:
                    continue
                s = ci % 2
                bank = ci % 2
                S.dma("sp", xc[s], xT[:, :, off:off + w], writes=[("xc", s)])
                S.op("dve", "tensor_tensor", sq[s], xc[s], xc[s], ALU.mult, reads=[("xc", s)], writes=[("sq", s)])
                for k in range(8):
                    S.op("pe", "matmul", PS[bank], onesb, sq[s][:, k, :], start=(k == 0), stop=(k == 7),
                         reads=["onesb", ("sq", s)], writes=[("ps", bank)])
                S.op("act", "activation", rstd[s], PS[bank], AF.Sqrt, bias=epsc, scale=1.0 / D, reads=[("ps", bank)], writes=[("rstd", s)])
                S.op("dve", "reciprocal", rstd[s], rstd[s], reads=[("rstd", s)], writes=[("rstd", s)])
                for k in range(8):
                    S.op("dve", "scalar_tensor_tensor", xc[s][:, k, :], xc[s][:, k, :], fg[:, k:k + 1], rstd[s], ALU.mult, ALU.mult,
                         reads=[("xc", s), ("rstd", s), "fg"], writes=[("xc", s)])
                for tt in range(4):
                    o2 = oi % 2
                    oi += 1
                    for hb in range(2):
                        bank2 = 2 + (oi % 2) * 2 + hb
                        for k4 in range(4):
                            k = hb * 4 + k4
                            S.op("pe", "transpose", PS[bank2][:, k4 * 128:(k4 + 1) * 128], xc[s][:, k, tt * 128:(tt + 1) * 128], ident,
                                 reads=[("xc", s), "ident"], writes=[("ps", bank2)])
                        if hb == 0:
                            S.op("dve", "tensor_copy", ot[o2][:, 0:512], PS[bank2], reads=[("ps", bank2)], writes=[("ot", o2, 0)])
                        else:
                            S.op("act", "copy", ot[o2][:, 512:1024], PS[bank2], reads=[("ps", bank2)], writes=[("ot", o2, 1)])
                    S.dma("sp", out_d[off + tt * 128:off + (tt + 1) * 128, :], ot[o2], reads=[("ot", o2, 0), ("ot", o2, 1)])
            S.barrier()

    prologue()
    for li in range(depth):
        m = li % 3
        need_ctx = li < DEPTH - 1
        if m == 0:
            layer_A(li, li // 3, need_ctx)
        elif m == 1:
            layer_B(li, need_ctx)
        else:
            layer_C(li, need_ctx)
    if dbg is not None:
        with sbt("dbg_t", [128, 8, T], F32) as dt_:
            S.dma("sp", dt_.ap(), xT)
            S.barrier()
            S.dma("sp", dbg, dt_.ap())
            S.barrier()
    if depth == DEPTH:
        final_stage()
    S.finalize()
    nc._declared_inputs = declared
    return nc


NLAT_CORE = 2048
NCTX_CORE = 128
_PROG_CACHE = {}


def make_in_maps(inputs, NLAT=None, NCTX=None, n_cores=8):
    f = lambda a: np.ascontiguousarray(np.asarray(a, dtype=np.float32))
    x = f(inputs["x"]); c = f(inputs["c"]); ctx = f(inputs["ctx"]); c_ctx = f(inputs["c_ctx"])
    NLAT = NLAT or NLAT_CORE
    NCTX = NCTX or NCTX_CORE
    shared = {
        "w_ada": f(inputs["w_ada"]),
        "b_adaT": f(np.asarray(inputs["b_ada"]).reshape(DEPTH, 24, 128).transpose(0, 2, 1)),
        "norm_gT": f(np.asarray(inputs["norm_g"]).reshape(DEPTH, 8, 128).transpose(0, 2, 1)),
        "final_gT": f(np.asarray(inputs["final_g"]).reshape(8, 128).T),
        "a_w_in": f(inputs["a_w_in"]),
        "a_w_out": f(inputs["a_w_out"]),
        "a_lam": f(np.stack([inputs["a_lam_q1"], inputs["a_lam_k1"], inputs["a_lam_q2"], inputs["a_lam_k2"]], axis=1)),
        "a_subln": f(inputs["a_subln_g"]),
        "b_w_in": f(inputs["b_w_in"]),
        "b_w_grp": f(inputs["b_w_grp"]),
        "b_bT": f(np.asarray(inputs["b_b_grp"]).reshape(16, 128).T),
        "b_sT": f(np.asarray(inputs["b_scale"]).reshape(16, 128).T),
        "b_w_out": f(inputs["b_w_out"]),
        "c_w_in": f(inputs["c_w_in"]),
        "c_sink": f(inputs["c_sink"]),
        "c_w_out": f(inputs["c_w_out"]),
        "ident": np.eye(128, dtype=np.float32),
        "band": band_mask(),
    }
    per_rank = []
    for s_ in range(2):
        cosA, sinA, permA = rope_tables(64, 2, s_ * NLAT, NLAT)
        cosC, sinC, permC = rope_tables(128, 1, s_ * NLAT, NLAT)
        ic = invcnt_tables(NLAT, NCTX, s_ * NLAT, 2 * NLAT, s_ * NCTX, 2 * NCTX)
        e = np.zeros((4, 32), np.float32)
        for g in range(4):
            e[g, 0:8] = ic[g, 0:8]
            e[g, 8:16] = ic[g, NLAT - 8:NLAT]
            e[g, 16:24] = ic[g, NLAT:NLAT + 8]
            e[g, 24:32] = ic[g, NLAT + NCTX - 8:NLAT + NCTX]
        hm = np.zeros((128, 2), np.float32)
        hm[:, 0] = 1.0 if s_ == 1 else 0.0
        hm[:, 1] = 1.0 if s_ == 0 else 0.0
        per_rank.append({"ropeA": f(np.stack([cosA, sinA])), "ropeC": f(np.stack([cosC, sinC])), "permA": permA, "permC": permC,
                         "edge": e, "hmask": hm})
    maps = []
    for core in range(n_cores):
        b = core // 2
        s_ = core % 2
        m = dict(shared)
        m.update(per_rank[s_])
        m["x_tok"] = np.ascontiguousarray(x[b][s_ * NLAT:(s_ + 1) * NLAT])
        m["ctx_tok"] = np.ascontiguousarray(ctx[b][s_ * NCTX:(s_ + 1) * NCTX])
        cv = np.stack([c[b].reshape(8, 128).T, c_ctx.reshape(8, 128).T], axis=-1)
        m["cvec"] = f(cv)
        maps.append(m)
    return maps


def kernel(**inputs):
    key = "split"
    if key not in _PROG_CACHE:
        _PROG_CACHE[key] = build_program(NLAT_CORE, NCTX_CORE)
    nc = _PROG_CACHE[key]
    maps = make_in_maps(inputs)
    maps = [{k: v for k, v in m.items() if k in nc._declared_inputs} for m in maps]
    res = run_bass_kernel_spmd(nc, maps, core_ids=list(range(8)))
    B = np.asarray(inputs["x"]).shape[0]
    out = np.stack([np.concatenate([np.asarray(res.results[2 * b + s_]["out"], dtype=np.float32) for s_ in range(2)], axis=0)
                    for b in range(B)], axis=0)
    return out
```

```python
import contextlib
import math
import numpy as np
import concourse.bass as bass
import concourse.mybir as mybir
from concourse.bass_utils import run_bass_kernel_spmd

F32 = mybir.dt.float32
BF16 = mybir.dt.bfloat16
ALU = mybir.AluOpType
AF = mybir.ActivationFunctionType

D = 1024
DI = 2048
DEPTH = 4
GRID_W = 64
EPS = 1e-6
POOL_WINDOWS = (2, 4, 8, 16)
PAD = 8
PAIR_GROUPS = [[0, 1], [2, 3], [4, 5], [6, 7]]

SAME_ENGINE_SYNC = True
STAGE_LIMIT = 99
SEM_EPOCH_LIMIT = 6000
EMBED_WAIT = True
N_DMA_SEMS = 12


class Buf:
    __slots__ = ("last_w", "readers", "dma_readers")

    def __init__(self):
        self.last_w = None
        self.readers = {}
        self.dma_readers = []


class Op:
    __slots__ = ("eng", "meth", "args", "kw", "deps", "is_dma", "sig_needed", "sig_val", "sem", "val", "epoch")

    def __init__(self, eng, meth, args, kw, is_dma):
        self.eng = eng
        self.meth = meth
        self.args = args
        self.kw = kw
        self.deps = []
        self.is_dma = is_dma
        self.sig_needed = False
        self.sig_val = 0
        self.sem = None
        self.val = 0
        self.epoch = 0


class Sched:
    ENGS = ("pe", "act", "dve", "pool", "sp")
    ATTR = {"pe": "tensor", "act": "scalar", "dve": "vector", "pool": "gpsimd", "sp": "sync"}

    def __init__(self, nc):
        self.nc = nc
        self.ops = {e: [] for e in self.ENGS}
        self.dma_count = {e: 0 for e in self.ENGS}
        self.dma_hist = {e: [] for e in self.ENGS}
        self.bufs = {}
        self.last_real = {e: None for e in self.ENGS}
        self.dmas_since = []
        self.n_coll = 0

    def buf(self, key):
        b = self.bufs.get(key)
        if b is None:
            b = Buf()
            self.bufs[key] = b
        return b

    def _track(self, op, reads, writes):
        deps = op.deps
        rb = [self.buf(k) for k in reads]
        wb = [self.buf(k) for k in writes]
        for k, b in zip(reads, rb):
            if b.last_w is not None:
                deps.append(b.last_w)
            if isinstance(k, tuple) and k[0] == "ps":
                for e2, r in b.readers.items():
                    if e2 != op.eng:
                        deps.append(r)
        for b in wb:
            if b.last_w is not None:
                deps.append(b.last_w)
            deps.extend(b.readers.values())
            deps.extend(b.dma_readers)
        for b in rb:
            if op.is_dma:
                b.dma_readers.append(op)
            else:
                b.readers[op.eng] = op
        for b in wb:
            b.last_w = op
            b.readers = {}
            b.dma_readers = []

    def op(self, eng, meth, *args, reads=(), writes=(), **kw):
        o = Op(eng, meth, args, kw, False)
        self._track(o, reads, writes)
        self.ops[eng].append(o)
        self.last_real[eng] = o
        return o

    def dma(self, eng, out, in_, reads=(), writes=(), **kw):
        o = Op(eng, "dma_start", (), dict(out=out, in_=in_, **kw), True)
        i = self.dma_count[eng]
        self.dma_count[eng] = i + 1
        o.sem = (eng, i % N_DMA_SEMS)
        o.val = 16 * (i // N_DMA_SEMS + 1)
        hist = self.dma_hist[eng]
        if i >= N_DMA_SEMS:
            o.deps.append(hist[i - N_DMA_SEMS])
        hist.append(o)
        self._track(o, reads, writes)
        self.ops[eng].append(o)
        self.dmas_since.append(o)
        return o

    def coll(self, kind, groups, in_ap, out_ap, reads=()):
        o = Op("pool", "collective_compute", (kind, ALU.bypass), dict(replica_groups=groups, ins=[in_ap], outs=[out_ap]), True)
        self.n_coll += 1
        o.sem = ("cc", self.n_coll)
        o.val = 1
        self._track(o, reads, ())
        self.ops["pool"].append(o)
        self.dmas_since.append(o)
        return o

    def barrier(self):
        deps = [o for o in self.last_real.values() if o is not None] + self.dmas_since
        for e in self.ENGS:
            o = Op(e, None, (), {}, False)
            o.deps = list(deps)
            self.ops[e].append(o)
        self.dmas_since = []
        self.bufs = {}

    def finalize(self):
        nc = self.nc
        self.barrier()
        for e in self.ENGS:
            for o in self.ops[e]:
                for d in o.deps:
                    if d.is_dma:
                        continue
                    if d.eng == o.eng and not o.is_dma and (not SAME_ENGINE_SYNC or d.eng == "pe"):
                        continue
                    d.sig_needed = True
        nep = {}
        for e in self.ENGS:
            c = 0
            ep = 0
            for o in self.ops[e]:
                if o.meth is None and c > SEM_EPOCH_LIMIT:
                    ep += 1
                    c = 0
                o.epoch = ep
                if not o.is_dma and o.sig_needed:
                    c += 1
                    o.sig_val = c
            nep[e] = ep + 1
        with contextlib.ExitStack() as stack:
            esem = {}
            dsem = {}
            for e in self.ENGS:
                for ep in range(nep[e]):
                    esem[(e, ep)] = stack.enter_context(nc.semaphore("es_%s_%d" % (e, ep)))
                if self.dma_count[e]:
                    for j in range(N_DMA_SEMS):
                        dsem[(e, j)] = stack.enter_context(nc.semaphore("ds_%s_%d" % (e, j)))
            for j in range(1, self.n_coll + 1):
                dsem[("cc", j)] = stack.enter_context(nc.semaphore("cc_%d" % j))
            block = stack.enter_context(nc.Block())
            for e in self.ENGS:
                self._emit_engine(block, e, self.ops[e], esem, dsem)

    def _emit_engine(self, block, e, ops, esem, dsem):
        known = {}

        def body(eng):
            for o in ops:
                w = {}
                for d in o.deps:
                    if d.is_dma:
                        key = ("d",) + d.sem
                        v = d.val
                    else:
                        if d.eng == e and not o.is_dma and (not SAME_ENGINE_SYNC or e == "pe"):
                            continue
                        key = ("e", d.eng, d.epoch)
                        v = d.sig_val
                    if known.get(key, 0) >= v:
                        continue
                    if w.get(key, 0) < v:
                        w[key] = v
                wl = list(w.items())
                embed = None
                if EMBED_WAIT and o.meth is not None and wl and not o.is_dma:
                    embed = wl.pop()
                for key, v in wl:
                    sem = esem[(key[1], key[2])] if key[0] == "e" else dsem[(key[1], key[2])]
                    eng.wait_ge(sem, v)
                    known[key] = v
                if o.meth is None:
                    continue
                inst = getattr(eng, o.meth)(*o.args, **o.kw)
                if embed is not None:
                    key, v = embed
                    sem = esem[(key[1], key[2])] if key[0] == "e" else dsem[(key[1], key[2])]
                    inst._wait_ge(sem, v)
                    known[key] = v
                if o.is_dma:
                    inst.then_inc(dsem[o.sem], 1 if o.sem[0] == "cc" else 16)
                elif o.sig_needed:
                    inst.then_inc(esem[(e, o.epoch)], 1)

        getattr(block, self.ATTR[e])(body)


def rope_tables(head_dim, rep, pos0, n):
    half = head_dim // 2
    quarter = head_dim // 4
    inv = (10000.0 ** (-np.arange(quarter, dtype=np.float32) / quarter)).astype(np.float32)
    pos = np.arange(pos0, pos0 + n)
    row = (pos // GRID_W).astype(np.float32)
    col = (pos % GRID_W).astype(np.float32)
    cos = np.zeros((128, n), np.float32)
    sin = np.zeros((128, n), np.float32)
    perm = np.zeros((128, 128), np.float32)
    for p in range(128):
        d = p % head_dim
        base = p - d
        hsel = d // half
        dd = d % half
        i = dd % quarter
        ang = ((row if hsel == 0 else col) * inv[i]).astype(np.float32)
        cos[p] = np.cos(ang)
        if dd < quarter:
            partner = d + quarter
            sin[p] = -np.sin(ang)
        else:
            partner = d - quarter
            sin[p] = np.sin(ang)
        perm[base + partner, p] = 1.0
    return cos, sin, perm


def invcnt_tables(nlat, nctx, lat0, lat_total, ctx0, ctx_total):
    T = nlat + nctx
    out = np.zeros((4, T), np.float32)
    for g, w in enumerate(POOL_WINDOWS):
        for (n, o, p0, tot) in ((nlat, 0, lat0, lat_total), (nctx, nlat, ctx0, ctx_total)):
            t = np.arange(p0, p0 + n)
            lo = np.clip(t - w // 2, 0, tot)
            hi = np.clip(t - w // 2 + w, 0, tot)
            out[g, o:o + n] = 1.0 / (hi - lo).astype(np.float32)
    return out


def band_mask():
    k = np.arange(128)[:, None]
    q = np.arange(128)[None, :]
    m = np.ones((128, 384), np.float32)
    m[:, 0:128] = (k <= q)
    m[:, 256:384] = (k >= q)
    return m


def build_program(NLAT, NCTX, depth=DEPTH, debug_x=False, groups=PAIR_GROUPS):
    T = NLAT + NCTX
    NT = T // 128
    NLT = NLAT // 128
    NTK = 2 * NT
    CTX_TILES = [NT - 1, 2 * NT - 1]
    CHUNKS = [(o, 512, 0) for o in range(0, NLAT, 512)] + [(NLAT + o, min(512, NCTX - o), 1) for o in range(0, NCTX, 512)]
    LP = T + 4 * PAD

    nc = bass.Bass("TRN2", target_bir_lowering=False)
    S = Sched(nc)
    coll_groups = groups

    declared = set()

    def din(name, shape, dt=F32):
        declared.add(name)
        return nc.dram_tensor(name, list(shape), dt, kind="ExternalInput").ap()

    x_tok = din("x_tok", [NLAT, D])
    ctx_tok = din("ctx_tok", [NCTX, D])
    cvec = din("cvec", [128, 8, 2])
    w_ada = din("w_ada", [DEPTH, D, 3 * D])
    b_adaT = din("b_adaT", [DEPTH, 128, 24])
    norm_gT = din("norm_gT", [DEPTH, 128, 8])
    final_gT = din("final_gT", [128, 8])
    a_w_in = din("a_w_in", [2, D, 4 * DI]) if depth >= 1 else None
    a_w_out = din("a_w_out", [2, DI, D]) if depth >= 1 else None
    a_lam = din("a_lam", [2, 4, 64])
    a_subln = din("a_subln", [2, 128])
    b_w_in = din("b_w_in", [1, D, 2 * DI]) if depth >= 2 else None
    b_w_grp = din("b_w_grp", [1, 4, 512, 512]) if depth >= 2 else None
    b_bT = din("b_bT", [128, 16])
    b_sT = din("b_sT", [128, 16])
    b_w_out = din("b_w_out", [1, DI, D]) if depth >= 2 else None
    c_w_in = din("c_w_in", [1, D, 5120]) if depth >= 3 else None
    c_sink = din("c_sink", [1, 16])
    c_w_out = din("c_w_out", [1, DI, D]) if depth >= 3 else None
    ident_d = din("ident", [128, 128])
    ropeA_d = din("ropeA", [2, 128, NLAT])
    ropeC_d = din("ropeC", [2, 128, NLAT])
    permA_d = din("permA", [128, 128])
    permC_d = din("permC", [128, 128])
    edge_d = din("edge", [4, 32])
    band_d = din("band", [128, 384])
    hmask_d = din("hmask", [128, 2])
    out_d = nc.dram_tensor("out", [NLAT, D], F32, kind="ExternalOutput").ap()

    xT = nc.dram_tensor("xT_s", [128, 8, T], F32).ap()
    QT = nc.dram_tensor("QT_s", [16, 128, T], BF16).ap()
    KT = nc.dram_tensor("KT_s", [16, 128, T], BF16).ap()
    VD = nc.dram_tensor("VD_s", [16, T, 128], BF16).ap()
    SZ = nc.dram_tensor("SZ_s", [T, DI], F32).ap()
    GT = nc.dram_tensor("GT_s", [128, 16, T], BF16).ap()
    KT_all = nc.dram_tensor("KT_all", [8, 2, 2, 128, T], BF16).ap()
    VD_all = nc.dram_tensor("VD_all", [8, 2, 2, T, 128], BF16).ap()
    KTc = nc.dram_tensor("KTc_s", [4, 128, T], BF16).ap()
    VDc = nc.dram_tensor("VDc_s", [4, T, 128], BF16).ap()
    KTc_all = nc.dram_tensor("KTc_all", [2, 2, 2, 128, T], BF16).ap()
    VDc_all = nc.dram_tensor("VDc_all", [2, 2, 2, T, 128], BF16).ap()
    xh_send = nc.dram_tensor("xh_send", [128, 8, 32], F32).ap()
    xh_all = nc.dram_tensor("xh_all", [2, 128, 8, 32], F32).ap()
    dbg = None
    if debug_x:
        dbg = nc.dram_tensor("dbg", [128, 8, T], F32, kind="ExternalOutput").ap()

    PSall = nc.alloc_psum_tensor("psall", [128, 8 * 512], F32).ap()
    PS = [PSall[:, i * 512:(i + 1) * 512] for i in range(8)]

    def sb(name, shape, dt=F32):
        return nc.alloc_sbuf_tensor(name, list(shape), dt).ap()

    _uid = [0]

    def sbt(name, shape, dt):
        _uid[0] += 1
        return nc.sbuf_tensor("%s_u%d" % (name, _uid[0]), list(shape), dt)

    ident = sb("ident_f", [128, 128])
    identb = sb("ident_b", [128, 128], BF16)
    onesb = sb("ones_b", [128, 128], BF16)
    MODS = sb("mods", [128, DEPTH, 2, 3, 8])
    neglam = sb("neglam", [128, 2])
    subg = sb("subg", [128, 2, 128])
    esink = sb("esink", [128, 16])
    mhalf = sb("mhalf", [128, 1])
    hmask = sb("hmask_sb", [128, 2])
    epsc = sb("epsc", [128, 1])

    def prologue():
        with contextlib.ExitStack() as es:
            def tmp(name, shape, dt=F32):
                return es.enter_context(sbt(name, list(shape), dt)).ap()
            S.dma("sp", ident, ident_d, writes=["ident"])
            S.dma("sp", hmask, hmask_d, writes=["hmask"])
            S.dma("pool", identb, ident_d, writes=["identb"])
            S.op("dve", "memset", onesb, 1.0, writes=["onesb"])
            S.op("dve", "memset", mhalf, -0.5, writes=["mhalf"])
            S.op("dve", "memset", epsc, EPS, writes=["epsc"])
            xin = [tmp("xin%d" % i, [128, D]) for i in range(2)]
            xst = [tmp("xst%d" % i, [128, 8, 128]) for i in range(2)]
            for t in range(NT):
                s = t % 2
                src = x_tok[t * 128:(t + 1) * 128, :] if t < NLT else ctx_tok[(t - NLT) * 128:(t - NLT + 1) * 128, :]
                S.dma("sp", xin[s], src, writes=[("xin", s)])
                for hb in range(2):
                    bank = (t % 2) * 2 + hb
                    for k4 in range(4):
                        k = hb * 4 + k4
                        S.op("pe", "transpose", PS[bank][:, k4 * 128:(k4 + 1) * 128], xin[s][:, k * 128:(k + 1) * 128], ident,
                             reads=[("xin", s), "ident"], writes=[("ps", bank)])
                    dst = xst[s][:, hb * 4:(hb + 1) * 4, :]
                    srcp = PS[bank].rearrange("p (k c) -> p k c", k=4)
                    if hb == 0:
                        S.op("dve", "tensor_copy", dst, srcp, reads=[("ps", bank)], writes=[("xst", s, hb)])
                    else:
                        S.op("act", "copy", dst, srcp, reads=[("ps", bank)], writes=[("xst", s, hb)])
                S.dma("sp", xT[:, :, t * 128:(t + 1) * 128], xst[s], reads=[("xst", s, 0), ("xst", s, 1)])
            cv = tmp("cv", [128, 8, 2])
            cs = tmp("cs", [128, 8, 2])
            S.dma("sp", cv, cvec, writes=["cv"])
            S.op("act", "activation", cs, cv, AF.Silu, reads=["cv"], writes=["cs"])
            wa = [tmp("wa%d" % i, [128, 8, 512]) for i in range(2)]
            modraw = tmp("modraw", [128, 24, 2])
            bT = tmp("bT", [128, DEPTH, 24])
            gT_ = tmp("gT_", [128, DEPTH, 8])
            S.dma("sp", bT, b_adaT.rearrange("l p c -> p l c"), writes=["bT"])
            S.dma("sp", gT_, norm_gT.rearrange("l p c -> p l c"), writes=["gT_"])
            mod = tmp("mod", [128, 2, 24])
            pi = 0
            for i in range(depth):
                for pc in range(6):
                    s = pi % 2
                    pi += 1
                    S.dma("sp", wa[s], w_ada[i].rearrange("(k p) c -> p k c", p=128)[:, :, pc * 512:(pc + 1) * 512], writes=[("wa", s)])
                    for blk in range(4):
                        cb = pc * 4 + blk
                        for k in range(8):
                            S.op("pe", "matmul", PS[4][:, cb * 2:cb * 2 + 2], wa[s][:, k, blk * 128:(blk + 1) * 128], cs[:, k, :],
                                 start=(k == 0), stop=(k == 7), reads=[("wa", s), "cs"], writes=[("ps", 4)])
                S.op("dve", "tensor_copy", modraw, PS[4][:, 0:48].rearrange("p (c j) -> p c j", j=2), reads=[("ps", 4)], writes=["modraw"])
                for kind in range(2):
                    S.op("dve", "tensor_tensor", mod[:, kind, :], modraw[:, :, kind], bT[:, i, :], ALU.add,
                         reads=["modraw", "bT"], writes=[("mod", kind)])
                    S.op("dve", "scalar_tensor_tensor", MODS[:, i, kind, 0, :], mod[:, kind, 8:16], 1.0, gT_[:, i, :], ALU.add, ALU.mult,
                         reads=[("mod", kind), "gT_"], writes=["MODS"])
                    S.op("dve", "tensor_copy", MODS[:, i, kind, 1, :], mod[:, kind, 0:8], reads=[("mod", kind)], writes=["MODS"])
                    S.op("dve", "tensor_copy", MODS[:, i, kind, 2, :], mod[:, kind, 16:24], reads=[("mod", kind)], writes=["MODS"])
            lv = tmp("lv", [128, 2, 4, 64])
            S.dma("sp", lv, a_lam.rearrange("j f d -> (j f d)").partition_broadcast(128).rearrange("p (j f d) -> p j f d", j=2, f=4), writes=["lv"])
            junk = tmp("junk", [128, 64])
            ssum = tmp("ssum", [128, 4])
            esum = tmp("esum", [128, 4])
            for j in range(2):
                for q in range(2):
                    S.op("dve", "scalar_tensor_tensor", junk, lv[:, j, 2 * q, :], 1.0, lv[:, j, 2 * q + 1, :], ALU.mult, ALU.mult,
                         accum_out=ssum[:, 2 * j + q:2 * j + q + 1], reads=["lv"], writes=["junk", "ssum"])
            S.op("act", "activation", esum, ssum, AF.Exp, reads=["ssum"], writes=["esum"])
            for j in range(2):
                li = 0.8 - 0.6 * math.exp(-0.3 * (3 * j))
                S.op("dve", "scalar_tensor_tensor", neglam[:, j:j + 1], esum[:, 2 * j + 1:2 * j + 2], -li, esum[:, 2 * j:2 * j + 1], ALU.add, ALU.subtract,
                     reads=["esum"], writes=["neglam"])
                S.dma("sp", subg[:, j, :], a_subln[j].partition_broadcast(128), writes=[("subg", j)])
                S.op("dve", "tensor_scalar", subg[:, j, :], subg[:, j, :], 1.0 - li, None, ALU.mult, reads=[("subg", j)], writes=[("subg", j)])
            sk = tmp("sk", [128, 16])
            S.dma("sp", sk, c_sink[0].partition_broadcast(128), writes=["sk"])
            S.op("act", "activation", esink, sk, AF.Exp, reads=["sk"], writes=["esink"])
            S.barrier()

    def norm_stage(li, hT, es):
        xc = [es.enter_context(sbt("n_xc%d" % i, [128, 8, 512], F32)).ap() for i in range(2)]
        sq = [es.enter_context(sbt("n_sq%d" % i, [128, 8, 512], BF16)).ap() for i in range(2)]
        rstd = [es.enter_context(sbt("n_rs%d" % i, [128, 512], F32)).ap() for i in range(2)]
        tmpx = [es.enter_context(sbt("n_tx%d" % i, [128, 512], F32)).ap() for i in range(2)]
        for ci, (off, w, kind) in enumerate(CHUNKS):
            s = ci % 2
            bank = ci % 2
            S.dma("sp", xc[s][:, :, :w], xT[:, :, off:off + w], writes=[("xc", s)])
            S.op("dve", "tensor_tensor", sq[s][:, :, :w], xc[s][:, :, :w], xc[s][:, :, :w], ALU.mult, reads=[("xc", s)], writes=[("sq", s)])
            for k in range(8):
                S.op("pe", "matmul", PS[bank][:, :w], onesb, sq[s][:, k, :w], start=(k == 0), stop=(k == 7),
                     reads=["onesb", ("sq", s)], writes=[("ps", bank)])
            S.op("act", "activation", rstd[s][:, :w], PS[bank][:, :w], AF.Sqrt, bias=epsc, scale=1.0 / D,
                 reads=[("ps", bank), "epsc"], writes=[("rstd", s)])
            S.op("dve", "reciprocal", rstd[s][:, :w], rstd[s][:, :w], reads=[("rstd", s)], writes=[("rstd", s)])
            for k in range(8):
                ts = k % 2
                S.op("dve", "tensor_tensor", tmpx[ts][:, :w], xc[s][:, k, :w], rstd[s][:, :w], ALU.mult,
                     reads=[("xc", s), ("rstd", s)], writes=[("tmpx", ts)])
                S.op("act", "activation", hT[:, k, off:off + w], tmpx[ts][:, :w], AF.Identity,
                     bias=MODS[:, li, kind, 1, k:k + 1], scale=MODS[:, li, kind, 0, k:k + 1],
                     reads=[("tmpx", ts)], writes=[("hT", ci)])

    def inproj_stage(w_d, groups, hT, rope_d, perm_d, es, kt_dst, vd_dst, hook=None, hook_at=-1):
        wt = [es.enter_context(sbt("ip_w%d" % i, [128, 8, 512], BF16)).ap() for i in range(2)]
        cosT = es.enter_context(sbt("ip_cos", [128, NLAT], F32)).ap()
        sinT = es.enter_context(sbt("ip_sin", [128, NLAT], F32)).ap()
        permb = es.enter_context(sbt("ip_perm", [128, 128], BF16)).ap()
        qb = [es.enter_context(sbt("ip_qb%d" % i, [128, 512], BF16)).ap() for i in range(2)]
        t1 = [es.enter_context(sbt("ip_t1%d" % i, [128, 512], F32)).ap() for i in range(2)]
        t2 = [es.enter_context(sbt("ip_t2%d" % i, [128, 512], F32)).ap() for i in range(2)]
        stg = [es.enter_context(sbt("ip_st%d" % i, [128, T], BF16)).ap() for i in range(2)]
        vst = [es.enter_context(sbt("ip_vs%d" % i, [128, 512], BF16)).ap() for i in range(2)]
        zst = [es.enter_context(sbt("ip_zs%d" % i, [128, 512], F32)).ap() for i in range(2)]
        S.dma("sp", cosT, rope_d[0], writes=["cosT"])
        S.dma("sp", sinT, rope_d[1], writes=["sinT"])
        S.dma("pool", permb, perm_d, writes=["permb"])
        wv = w_d.rearrange("(k p) c -> p k c", p=128)
        bi = 0
        ti = 0
        for gi, (col0, role, idx0) in enumerate(groups):
            s = gi % 2
            S.dma("pool", wt[s], wv[:, :, col0:col0 + 512], writes=[("wt", s)])
            if hook is not None and gi >= hook_at:
                hook(gi - hook_at, len(groups) - hook_at)
            if role in ("q", "k"):
                dst = QT if role == "q" else kt_dst
                for blk in range(4):
                    ss = bi % 2
                    pending = None

                    def rope_tail(p):
                        bank_, c2_, off_, w_ = p
                        pb = 4 + c2_
                        S.op("pe", "matmul", PS[pb][:, :w_], permb, qb[c2_][:, :w_], start=True, stop=True,
                             reads=["permb", ("qb", c2_)], writes=[("ps", pb)])
                        S.op("dve", "tensor_tensor", t1[c2_][:, :w_], PS[bank_][:, :w_], cosT[:, off_:off_ + w_], ALU.mult,
                             reads=[("ps", bank_), "cosT"], writes=[("t1", c2_)])
                        S.op("dve", "tensor_tensor", t2[c2_][:, :w_], PS[pb][:, :w_], sinT[:, off_:off_ + w_], ALU.mult,
                             reads=[("ps", pb), "sinT"], writes=[("t2", c2_)])
                        S.op("dve", "tensor_tensor", stg[ss][:, off_:off_ + w_], t1[c2_][:, :w_], t2[c2_][:, :w_], ALU.add,
                             reads=[("t1", c2_), ("t2", c2_)], writes=[("stg", ss)])

                    for ci, (off, w, kind) in enumerate(CHUNKS):
                        bank = bi % 2 * 2 + ci % 2
                        c2 = ci % 2
                        for k in range(8):
                            S.op("pe", "matmul", PS[bank][:, :w], wt[s][:, k, blk * 128:(blk + 1) * 128], hT[:, k, off:off + w],
                                 start=(k == 0), stop=(k == 7), reads=[("wt", s), ("hT", ci)], writes=[("ps", bank)])
                        if kind == 0:
                            S.op("act", "copy", qb[c2][:, :w], PS[bank][:, :w], reads=[("ps", bank)], writes=[("qb", c2)])
                            if pending is not None:
                                rope_tail(pending)
                            pending = (bank, c2, off, w)
                        else:
                            S.op("act", "copy", stg[ss][:, off:off + w], PS[bank][:, :w], reads=[("ps", bank)], writes=[("stg", ss)])
                    if pending is not None:
                        rope_tail(pending)
                    S.dma("sp", dst[idx0 + blk], stg[ss], reads=[("stg", ss)], writes=([("KTd", idx0 + blk)] if role == "k" else []))
                    bi += 1
            else:
                for t in range(NT):
                    bank = 6 + ti % 2
                    s2 = ti % 2
                    ti += 1
                    for k in range(8):
                        S.op("pe", "matmul", PS[bank], hT[:, k, t * 128:(t + 1) * 128], wt[s][:, k, :],
                             start=(k == 0), stop=(k == 7), reads=[("wt", s), ("hT", (t * 128) // 512 if t < NLT else len(CHUNKS) - 1)], writes=[("ps", bank)])
                    if role == "v":
                        S.op("dve", "tensor_copy", vst[s2], PS[bank], reads=[("ps", bank)], writes=[("vst", s2)])
                        S.dma("sp", vd_dst[idx0:idx0 + 4, t * 128:(t + 1) * 128, :].rearrange("h t c -> t h c"), vst[s2].rearrange("p (h c) -> p h c", h=4), reads=[("vst", s2)], writes=[("VDd", idx0 // 4, t)])
                    else:
                        S.op("act", "activation", zst[s2], PS[bank], AF.Silu, reads=[("ps", bank)], writes=[("zst", s2)])
                        S.dma("sp", SZ[t * 128:(t + 1) * 128, idx0:idx0 + 512], zst[s2], reads=[("zst", s2)])

    def prefetch_wout(w_d, es):
        wo = es.enter_context(sbt("op_w", [128, 16, D], BF16)).ap()
        wv = w_d.rearrange("(k p) c -> p k c", p=128)
        for h in range(2):
            S.dma("pool", wo[:, h * 8:(h + 1) * 8, :], wv[:, h * 8:(h + 1) * 8, :], writes=[("wo", h)])
        return wo

    def outproj_stage(li, wo, need_ctx):
        with contextlib.ExitStack() as es:
            gch = [es.enter_context(sbt("op_g%d" % i, [128, 16, 512], BF16)).ap() for i in range(2)]
            xch = [es.enter_context(sbt("op_x%d" % i, [128, 8, 512], F32)).ap() for i in range(2)]
            ci = 0
            for (off, w, kind) in CHUNKS:
                if kind == 1 and not need_ctx:
                    continue
                s = ci % 2
                ci += 1
                S.dma("sp", gch[s][:, :, :w], GT[:, :, off:off + w], writes=[("gch", s)])
                S.dma("sp", xch[s][:, :, :w], xT[:, :, off:off + w], writes=[("xch", s)])
                for blk in range(8):
                    bank = blk % 4
                    for k in range(16):
                        S.op("pe", "matmul", PS[bank][:, :w], wo[:, k, blk * 128:(blk + 1) * 128], gch[s][:, k, :w],
                             start=(k == 0), stop=(k == 15), reads=[("wo", k // 8), ("gch", s)], writes=[("ps", bank)])
                    S.op("dve", "scalar_tensor_tensor", xch[s][:, blk, :w], PS[bank][:, :w], MODS[:, li, kind, 2, blk:blk + 1], xch[s][:, blk, :w],
                         ALU.mult, ALU.add, reads=[("ps", bank), ("xch", s)], writes=[("xch", s)])
                S.dma("sp", xT[:, :, off:off + w], xch[s][:, :, :w], reads=[("xch", s)])
            S.barrier()

    def layer_A(li, j, need_ctx):
        with contextlib.ExitStack() as es:
            hT = es.enter_context(sbt("hT", [128, 8, T], BF16)).ap()
            with contextlib.ExitStack() as es2:
                norm_stage(li, hT, es2)
                S.barrier()
            if STAGE_LIMIT <= 1:
                return None
            with contextlib.ExitStack() as es2:
                groups = [(2048 + cg * 512, "k", cg * 4) for cg in range(4)] + [(4096 + cg * 512, "v", cg * 4) for cg in range(4)] \
                    + [(cg * 512, "q", cg * 4) for cg in range(4)] + [(6144 + cg * 512, "z", cg * 512) for cg in range(4)]

                def gatherA(step, nsteps):
                    per = (8 + nsteps - 1) // nsteps
                    for gj in range(step * per, min(8, (step + 1) * per)):
                        S.coll("AllGather", coll_groups, KT[2 * gj:2 * gj + 2].rearrange("h p t -> (h p) t"),
                               KT_all[gj].rearrange("r h p t -> (r h p) t"), reads=[("KTd", 2 * gj), ("KTd", 2 * gj + 1)])
                        S.coll("AllGather", coll_groups, VD[2 * gj:2 * gj + 2].rearrange("h t c -> (h t) c"),
                               VD_all[gj].rearrange("r h t c -> (r h t) c"), reads=[("VDd", gj // 2, t) for t in range(NT)])
                inproj_stage(a_w_in[j], groups, hT, ropeA_d, permA_d, es2, KT, VD, hook=gatherA, hook_at=9)
                S.barrier()
        if STAGE_LIMIT <= 2:
            return
        eso = contextlib.ExitStack()
        wo_pre = prefetch_wout(a_w_out[j], eso)
        with contextlib.ExitStack() as es:
            def tmp(name, shape, dt=F32):
                return es.enter_context(sbt(name, list(shape), dt)).ap()
            kt = [tmp("a_kt%d" % i, [128, 2 * T], BF16) for i in range(2)]
            qt = [tmp("a_qt%d" % i, [128, T], BF16) for i in range(2)]
            vt = [tmp("a_vt%d" % i, [128, NTK, 130], BF16) for i in range(2)]
            pt = [tmp("a_pt%d" % i, [128, 1024], BF16) for i in range(4)]
            accB = [tmp("a_acc%d" % i, [128, 3, 480]) for i in range(2)]
            szt = [tmp("a_sz%d" % i, [128, 4, 128]) for i in range(2)]
            wg = [tmp("a_wg%d" % i, [128, 4, 128]) for i in range(2)]
            rec = [tmp("a_rec%d" % i, [128, 8]) for i in range(2)]
            t0 = [tmp("a_t0%d" % i, [128, 128]) for i in range(4)]
            ot = [tmp("a_o%d" % i, [128, 128]) for i in range(4)]
            junk = [tmp("a_junk%d" % i, [128, 128]) for i in range(4)]
            ssq = tmp("a_ssq", [128, 4])
            rs = tmp("a_rs", [128, 4])
            gtok = [tmp("a_gt%d" % i, [128, 512], BF16) for i in range(2)]
            gst = [tmp("a_gs%d" % i, [128, T], BF16) for i in range(2)]
            for s in range(2):
                S.op("pool", "memset", vt[s][:, :, 128:130], 1.0, writes=[("vt", s)])
            SZv = SZ.rearrange("(t p) c -> p t c", p=128)
            trp = PS[4].bitcast(BF16)
            cidx = 0
            sidx = 0
            def load_head(h_):
                hs_ = h_ % 2
                S.dma("sp", kt[hs_].rearrange("p (r t) -> p r t", r=2), KT_all[h_ // 2, :, h_ % 2].rearrange("r p t -> p r t"), writes=[("kt", hs_)])
                S.dma("sp", qt[hs_], QT[h_], writes=[("qt", hs_)])
                for r_ in range(2):
                    S.dma("sp", vt[hs_][:, r_ * NT:(r_ + 1) * NT, 0:128], VD_all[h_ // 2, r_, h_ % 2].rearrange("(t p) c -> p t c", p=128), writes=[("vt", hs_)])

            load_head(0)
            for h in range(16):
                hs = h % 2
                for chi, (off, w, kind) in enumerate(CHUNKS):
                    if chi == 1 and h + 1 < 16:
                        load_head(h + 1)
                    if kind == 1 and not need_ctx:
                        continue
                    cs_ = cidx % 2
                    cidx += 1
                    nq = w // 128
                    t0i = off // 128
                    keytiles = list(range(NTK)) if kind == 0 else CTX_TILES
                    S.dma("sp", szt[cs_][:, :nq, :], SZv[:, t0i:t0i + nq, h * 128:(h + 1) * 128], writes=[("szt", cs_)])
                    for qs in range(nq):
                        S.op("pool", "tensor_tensor", wg[cs_][:, qs, :], szt[cs_][:, qs, :], subg[:, j, :], ALU.mult,
                             reads=[("szt", cs_)], writes=[("wg", cs_)])
                    started = set()
                    nkt = len(keytiles)

                    def qk_exp(kti):
                        pr = kti % 2
                        ktile = keytiles[kti]
                        for m in range(2):
                            bank = 2 * pr + m
                            S.op("pe", "matmul", PS[bank][:, :w], kt[hs][64 * m:64 * m + 64, ktile * 128:(ktile + 1) * 128],
                                 qt[hs][64 * m:64 * m + 64, off:off + w], start=True, stop=True,
                                 reads=[("kt", hs), ("qt", hs)], writes=[("ps", 2 * pr), ("ps", 2 * pr + 1)] if m == 1 else [("ps", 2 * pr)])
                        src = PSall[:, 2 * pr * 512:(2 * pr + 2) * 512].rearrange("p (m c) -> p m c", m=2)[:, :, 0:w]
                        dst = pt[kti % 4].rearrange("p (m c) -> p m c", m=2)[:, :, 0:w]
                        S.op("act", "activation", dst, src, AF.Exp, scale=0.125,
                             reads=[("ps", 2 * pr), ("ps", 2 * pr + 1)], writes=[("pt", kti % 4)])

                    qk_exp(0)
                    if nkt > 1:
                        qk_exp(1)
                    for kti, ktile in enumerate(keytiles):
                        if kti + 2 < nkt:
                            qk_exp(kti + 2)
                        pr = kti % 4
                        for m in range(2):
                            for qs in range(nq):
                                a = m * nq + qs
                                bank = 5 + a // 3
                                c0 = (a % 3) * 160
                                st = bank not in started
                                started.add(bank)
                                S.op("pe", "matmul", PS[bank][:, c0:c0 + 129], pt[pr][:, m * 512 + qs * 128:m * 512 + (qs + 1) * 128], vt[hs][:, ktile, 0:129],
                                     start=st, stop=(kti == nkt - 1), skip_group_check=True,
                                     reads=[("pt", pr), ("vt", hs)], writes=[("ps", bank)])
                    na = 2 * nq
                    accv = accB[cs_].rearrange("p b (a c) -> p (b a) c", a=3)
                    for b3 in range((na + 2) // 3):
                        cnt = min(3, na - 3 * b3)
                        S.op("dve", "tensor_copy", accv[:, 3 * b3:3 * b3 + cnt, 0:129],
                             PS[5 + b3][:, 0:480].rearrange("p (a c) -> p a c", a=3)[:, 0:cnt, 0:129],
                             reads=[("ps", 5 + b3)], writes=[("accB", cs_)])
                    S.op("dve", "reciprocal", rec[cs_][:, 0:na], accv[:, 0:na, 128], reads=[("accB", cs_)], writes=[("rec", cs_)])
                    S.op("dve", "tensor_scalar", rec[cs_][:, nq:na], rec[cs_][:, nq:na], neglam[:, j:j + 1], None, ALU.mult,
                         reads=[("rec", cs_)], writes=[("rec", cs_)])
                    for qs in range(nq):
                        S.op("dve", "tensor_scalar", t0[qs], accv[:, qs, 0:128], rec[cs_][:, qs:qs + 1], None, ALU.mult,
                             reads=[("accB", cs_), ("rec", cs_)], writes=[("t0", qs)])
                    for qs in range(nq):
                        S.op("dve", "scalar_tensor_tensor", ot[qs], accv[:, nq + qs, 0:128], rec[cs_][:, nq + qs:nq + qs + 1], t0[qs], ALU.mult, ALU.add,
                             reads=[("accB", cs_), ("rec", cs_), ("t0", qs)], writes=[("ot", qs)])
                    for qs in range(nq):
                        S.op("dve", "scalar_tensor_tensor", junk[qs], ot[qs], 1.0 / 128.0, ot[qs], ALU.mult, ALU.mult,
                             accum_out=ssq[:, qs:qs + 1], reads=[("ot", qs)], writes=[("junk", qs), ("ssq", qs)])
                    for qs in range(nq):
                        S.op("dve", "tensor_scalar", ssq[:, qs:qs + 1], ssq[:, qs:qs + 1], EPS, None, ALU.add,
                             reads=[("ssq", qs)], writes=[("ssq", qs)])
                    for qs in range(nq):
                        S.op("pool", "tensor_tensor", rs[:, qs:qs + 1], ssq[:, qs:qs + 1], mhalf, ALU.pow,
                             reads=[("ssq", qs), "mhalf"], writes=[("rs", qs)])
                    for qs in range(nq):
                        S.op("dve", "scalar_tensor_tensor", gtok[cs_][:, qs * 128:(qs + 1) * 128], ot[qs], rs[:, qs:qs + 1], wg[cs_][:, qs, :],
                             ALU.mult, ALU.mult, reads=[("ot", qs), ("rs", qs), ("wg", cs_)], writes=[("gtok", cs_, qs)])
                    for qs in range(nq):
                        S.op("pe", "transpose", trp[:, qs * 128:(qs + 1) * 128], gtok[cs_][:, qs * 128:(qs + 1) * 128], identb,
                             reads=[("gtok", cs_, qs), "identb"], writes=[("ps", 4)])
                    S.op("dve", "tensor_copy", gst[hs][:, off:off + w], trp[:, 0:w], reads=[("ps", 4)], writes=[("gst", hs)])
                if need_ctx:
                    S.dma("sp", GT[:, h, :], gst[hs], reads=[("gst", hs)])
                else:
                    S.dma("sp", GT[:, h, 0:NLAT], gst[hs][:, 0:NLAT], reads=[("gst", hs)])
            S.barrier()
        if STAGE_LIMIT <= 3:
            eso.close()
            return
        outproj_stage(li, wo_pre, need_ctx)
        eso.close()

    def layer_C(li, need_ctx):
        with contextlib.ExitStack() as es:
            hT = es.enter_context(sbt("hT", [128, 8, T], BF16)).ap()
            with contextlib.ExitStack() as es2:
                norm_stage(li, hT, es2)
                S.barrier()
            with contextlib.ExitStack() as es2:
                groups = [(2048, "k", 0), (2560, "v", 0)] + [(cg * 512, "q", cg * 4) for cg in range(4)] \
                    + [(3072 + cg * 512, "z", cg * 512) for cg in range(4)]

                def gatherC(step, nsteps):
                    for gj in ([0, 1] if step == 0 else []):
                        S.coll("AllGather", coll_groups, KTc[2 * gj:2 * gj + 2].rearrange("h p t -> (h p) t"),
                               KTc_all[gj].rearrange("r h p t -> (r h p) t"), reads=[("KTd", 2 * gj), ("KTd", 2 * gj + 1)])
                        S.coll("AllGather", coll_groups, VDc[2 * gj:2 * gj + 2].rearrange("h t c -> (h t) c"),
                               VDc_all[gj].rearrange("r h t c -> (r h t) c"), reads=[("VDd", 0, t) for t in range(NT)])
                inproj_stage(c_w_in[0], groups, hT, ropeC_d, permC_d, es2, KTc, VDc, hook=gatherC, hook_at=3)
                S.barrier()
        scale = 128.0 ** -0.5
        eso = contextlib.ExitStack()
        wo_pre = prefetch_wout(c_w_out[0], eso)
        XB, XA, XC0, XC1 = NT, NT + 1, NT + 2, NT + 3
        with contextlib.ExitStack() as es:
            def tmp(name, shape, dt=F32):
                return es.enter_context(sbt(name, list(shape), dt)).ap()
            kt = [tmp("c_kt%d" % i, [128, (NT + 4) * 128], BF16) for i in range(2)]
            qt = [tmp("c_qt%d" % i, [128, T], BF16) for i in range(2)]
            vt = [tmp("c_vt%d" % i, [128, NT + 4, 130], BF16) for i in range(2)]
            pt = [tmp("c_pt%d" % i, [128, 512], BF16) for i in range(4)]
            bandm = tmp("c_band", [128, 384], BF16)
            hbm = tmp("c_hbm", [128, 128], BF16)
            ham = tmp("c_ham", [128, 128], BF16)
            accB = [tmp("c_acc%d" % i, [128, 2, 480]) for i in range(2)]
            szt = [tmp("c_sz%d" % i, [128, 4, 128]) for i in range(2)]
            den = [tmp("c_den%d" % i, [128, 4]) for i in range(2)]
            gtok = [tmp("c_gt%d" % i, [128, 512], BF16) for i in range(2)]
            gst = [tmp("c_gs%d" % i, [128, T], BF16) for i in range(2)]
            S.dma("pool", bandm, band_d, writes=["bandm"])
            S.op("dve", "tensor_scalar", hbm, bandm[:, 256:384], hmask[:, 0:1], None, ALU.mult, reads=["bandm"], writes=["hbm"])
            S.op("dve", "tensor_scalar", ham, bandm[:, 0:128], hmask[:, 1:2], None, ALU.mult, reads=["bandm"], writes=["ham"])
            for s in range(2):
                S.op("pool", "memset", vt[s][:, :, 128:130], 1.0, writes=[("vt", s)])
            SZv = SZ.rearrange("(t p) c -> p t c", p=128)
            trp = PS[4].bitcast(BF16)
            cidx = 0
            sidx = 0
            hcount = 0
            def load_kv(n_):
                ns_ = n_ % 2
                S.dma("sp", kt[ns_][:, 0:NLAT], KTc[n_][:, 0:NLAT], writes=[("kt", ns_)])
                ka = KTc_all[n_ // 2, :, n_ % 2]
                va = VDc_all[n_ // 2, :, n_ % 2]
                S.dma("sp", kt[ns_][:, XB * 128:(XB + 1) * 128], ka[0][:, NLAT - 128:NLAT], writes=[("kt", ns_)])
                S.dma("sp", kt[ns_][:, XA * 128:(XA + 1) * 128], ka[1][:, 0:128], writes=[("kt", ns_)])
                S.dma("sp", kt[ns_][:, XC0 * 128:(XC0 + 1) * 128], ka[0][:, NLAT:NLAT + 128], writes=[("kt", ns_)])
                S.dma("sp", kt[ns_][:, XC1 * 128:(XC1 + 1) * 128], ka[1][:, NLAT:NLAT + 128], writes=[("kt", ns_)])
                S.dma("sp", vt[ns_][:, 0:NLT, 0:128], VDc[n_][0:NLAT].rearrange("(t p) c -> p t c", p=128), writes=[("vt", ns_)])
                S.dma("sp", vt[ns_][:, XB, 0:128], va[0][NLAT - 128:NLAT, :], writes=[("vt", ns_)])
                S.dma("sp", vt[ns_][:, XA, 0:128], va[1][0:128, :], writes=[("vt", ns_)])
                S.dma("sp", vt[ns_][:, XC0, 0:128], va[0][NLAT:NLAT + 128, :], writes=[("vt", ns_)])
                S.dma("sp", vt[ns_][:, XC1, 0:128], va[1][NLAT:NLAT + 128, :], writes=[("vt", ns_)])

            load_kv(0)
            S.dma("sp", qt[0], QT[0], writes=[("qt", 0)])
            for n in range(4):
                ns = n % 2
                for g in range(4):
                    hq = 4 * n + g
                    hs = hcount % 2
                    hcount += 1
                    if hq + 1 < 16:
                        S.dma("sp", qt[(hs + 1) % 2], QT[hq + 1], writes=[("qt", (hs + 1) % 2)])
                    if g == 1 and n + 1 < 4:
                        load_kv(n + 1)
                    for (off, w, kind) in CHUNKS:
                        if kind == 1 and not need_ctx:
                            continue
                        cs_ = cidx % 2
                        cidx += 1
                        nq = w // 128
                        qb0 = off // 128
                        S.dma("sp", szt[cs_][:, :nq, :], SZv[:, qb0:qb0 + nq, hq * 128:(hq + 1) * 128], writes=[("szt", cs_)])
                        items = []
                        if kind == 0:
                            for kb in range(qb0 - 1, qb0 + nq + 1):
                                qlo = max(kb - 1, qb0)
                                qhi = min(kb + 1, qb0 + nq - 1)
                                if kb < 0:
                                    items.append((XB, qlo, qhi, {0: hbm}))
                                elif kb >= NLT:
                                    items.append((XA, qlo, qhi, {NLT - 1: ham}))
                                else:
                                    mk = {}
                                    for qb in range(qlo, qhi + 1):
                                        if kb == qb + 1:
                                            mk[qb] = bandm[:, 0:128]
                                        elif kb == qb - 1:
                                            mk[qb] = bandm[:, 256:384]
                                    items.append((kb, qlo, qhi, mk))
                        items.append((XC0, qb0, qb0 + nq - 1, {}))
                        items.append((XC1, qb0, qb0 + nq - 1, {}))
                        lastc = {}
                        for ii, (kb, qlo, qhi, mk) in enumerate(items):
                            for qb in range(qlo, qhi + 1):
                                lastc[qb] = ii
                        started = set()
                        sbase = sidx
                        sidx += len(items)

                        def s_exp(ii):
                            kb, qlo, qhi, mk = items[ii]
                            bank = (sbase + ii) % 4
                            ncol = (qhi - qlo + 1) * 128
                            S.op("pe", "matmul", PS[bank][:, :ncol], kt[ns][:, kb * 128:(kb + 1) * 128], qt[hs][:, qlo * 128:(qhi + 1) * 128],
                                 start=True, stop=True, reads=[("kt", ns), ("qt", hs)], writes=[("ps", bank)])
                            S.op("act", "activation", pt[bank][:, :ncol], PS[bank][:, :ncol], AF.Exp, scale=scale,
                                 reads=[("ps", bank)], writes=[("pt", bank)])
                            for qb in range(qlo, qhi + 1):
                                c = (qb - qlo) * 128
                                if qb in mk:
                                    S.op("pool", "tensor_tensor", pt[bank][:, c:c + 128], pt[bank][:, c:c + 128], mk[qb], ALU.mult,
                                         reads=[("pt", bank), "bandm", "hbm", "ham"], writes=[("pt", bank)])

                        s_exp(0)
                        if len(items) > 1:
                            s_exp(1)
                        for ii, (kb, qlo, qhi, mk) in enumerate(items):
                            if ii + 2 < len(items):
                                s_exp(ii + 2)
                            bank = (sbase + ii) % 4
                            for qb in range(qlo, qhi + 1):
                                c = (qb - qlo) * 128
                                a = qb - qb0
                                abank = 5 + a // 3
                                c0 = (a % 3) * 160
                                st = abank not in started
                                started.add(abank)
                                S.op("pe", "matmul", PS[abank][:, c0:c0 + 129], pt[bank][:, c:c + 128], vt[ns][:, kb, 0:129],
                                     start=st, stop=(lastc[qb] == ii), skip_group_check=True,
                                     reads=[("pt", bank), ("vt", ns)], writes=[("ps", abank)])
                        accv = accB[cs_].rearrange("p b (a c) -> p (b a) c", a=3)
                        for b3 in range((nq + 2) // 3):
                            cnt = min(3, nq - 3 * b3)
                            S.op("dve", "tensor_copy", accv[:, 3 * b3:3 * b3 + cnt, 0:129],
                                 PS[5 + b3][:, 0:480].rearrange("p (a c) -> p a c", a=3)[:, 0:cnt, 0:129],
                                 reads=[("ps", 5 + b3)], writes=[("accB", cs_)])
                        S.op("dve", "tensor_scalar", den[cs_][:, 0:nq], accv[:, 0:nq, 128], esink[:, hq:hq + 1], None, ALU.add,
                             reads=[("accB", cs_), "esink"], writes=[("den", cs_)])
                        S.op("dve", "reciprocal", den[cs_][:, 0:nq], den[cs_][:, 0:nq], reads=[("den", cs_)], writes=[("den", cs_)])
                        for qs in range(nq):
                            S.op("dve", "scalar_tensor_tensor", gtok[cs_][:, qs * 128:(qs + 1) * 128], accv[:, qs, 0:128], den[cs_][:, qs:qs + 1],
                                 szt[cs_][:, qs, :], ALU.mult, ALU.mult,
                                 reads=[("accB", cs_), ("den", cs_), ("szt", cs_)], writes=[("gtok", cs_, qs)])
                        for qs in range(nq):
                            S.op("pe", "transpose", trp[:, qs * 128:(qs + 1) * 128], gtok[cs_][:, qs * 128:(qs + 1) * 128], identb,
                                 reads=[("gtok", cs_, qs), "identb"], writes=[("ps", 4)])
                        S.op("dve", "tensor_copy", gst[hs][:, off:off + w], trp[:, 0:w], reads=[("ps", 4)], writes=[("gst", hs)])
                    if need_ctx:
                        S.dma("sp", GT[:, hq, :], gst[hs], reads=[("gst", hs)])
                    else:
                        S.dma("sp", GT[:, hq, 0:NLAT], gst[hs][:, 0:NLAT], reads=[("gst", hs)])
            S.barrier()
        outproj_stage(li, wo_pre, need_ctx)
        eso.close()

    def layer_B(li, need_ctx):
        eso = contextlib.ExitStack()
        wo_pre = prefetch_wout(b_w_out[0], eso)
        with contextlib.ExitStack() as es:
            def tmp(name, shape, dt=F32):
                return es.enter_context(sbt(name, list(shape), dt)).ap()
            hT = tmp("hT", [128, 8, T], BF16)
            hTh = tmp("hTh", [128, 8, 32], BF16)
            S.dma("sp", xh_send[:, :, 0:8], xT[:, :, 0:8])
            S.dma("sp", xh_send[:, :, 8:16], xT[:, :, NLAT - 8:NLAT])
            S.dma("sp", xh_send[:, :, 16:24], xT[:, :, NLAT:NLAT + 8])
            S.dma("sp", xh_send[:, :, 24:32], xT[:, :, T - 8:T])
            S.barrier()
            S.coll("AllGather", coll_groups, xh_send.rearrange("p k c -> (p k) c"), xh_all.rearrange("r p k c -> (r p k) c"))
            S.barrier()
            with contextlib.ExitStack() as es2:
                norm_stage(li, hT, es2)
                hx = es2.enter_context(sbt("b_hx", [128, 8, 32], F32)).ap()
                sqh = es2.enter_context(sbt("b_sqh", [128, 8, 32], BF16)).ap()
                rsh = es2.enter_context(sbt("b_rsh", [128, 32], F32)).ap()
                tmh = es2.enter_context(sbt("b_tmh", [128, 8, 32], F32)).ap()
                S.dma("sp", hx[:, :, 0:8], xh_all[0][:, :, 8:16], writes=["hx"])
                S.dma("sp", hx[:, :, 8:16], xh_all[1][:, :, 0:8], writes=["hx"])
                S.dma("sp", hx[:, :, 16:24], xh_all[0][:, :, 24:32], writes=["hx"])
                S.dma("sp", hx[:, :, 24:32], xh_all[1][:, :, 16:24], writes=["hx"])
                S.op("dve", "tensor_tensor", sqh, hx, hx, ALU.mult, reads=["hx"], writes=["sqh"])
                for k in range(8):
                    S.op("pe", "matmul", PS[7][:, 0:32], onesb, sqh[:, k, :], start=(k == 0), stop=(k == 7),
                         reads=["onesb", "sqh"], writes=[("ps", 7)])
                S.op("act", "activation", rsh, PS[7][:, 0:32], AF.Sqrt, bias=epsc, scale=1.0 / D, reads=[("ps", 7)], writes=["rsh"])
                S.op("dve", "reciprocal", rsh, rsh, reads=["rsh"], writes=["rsh"])
                for k in range(8):
                    S.op("dve", "tensor_tensor", tmh[:, k, :], hx[:, k, :], rsh, ALU.mult, reads=["hx", "rsh"], writes=[("tmh", k)])
                    for (c0, c1, kind) in ((0, 16, 0), (16, 32, 1)):
                        S.op("act", "activation", hTh[:, k, c0:c1], tmh[:, k, c0:c1], AF.Identity,
                             bias=MODS[:, li, kind, 1, k:k + 1], scale=MODS[:, li, kind, 0, k:k + 1],
                             reads=[("tmh", k)], writes=["hTh"])
                S.barrier()
            up = tmp("b_up", [128, LP])
            ab = [tmp("b_ab%d" % i, [128, LP]) for i in range(2)]
            dT = [tmp("b_d%d" % i, [128, T], BF16) for i in range(4)]
            w1 = tmp("b_w1", [128, 8, 512], BF16)
            wgr = tmp("b_wg", [128, 4, 512], BF16)
            edge = tmp("b_edge", [128, 4, 32])
            etmp = tmp("b_etmp", [128, 8])
            bgs = tmp("b_bgs", [128, 2, 16])
            szs = [tmp("b_sz%d" % i, [128, 512]) for i in range(2)]
            tys = [tmp("b_ty%d" % i, [128, 512]) for i in range(2)]
            gch = [tmp("b_gc%d" % i, [128, 512], BF16) for i in range(2)]
            S.dma("sp", bgs[:, 0, :], b_bT, writes=["bgs"])
            S.dma("sp", bgs[:, 1, :], b_sT, writes=["bgs"])
            S.dma("sp", edge, edge_d.rearrange("g e -> (g e)").partition_broadcast(128).rearrange("p (g e) -> p g e", g=4), writes=["edge"])
            S.op("dve", "memset", up, 0.0, writes=["up"])
            wv = b_w_in[0].rearrange("(k p) c -> p k c", p=128)
            regions = [(PAD, 0, NLAT), (3 * PAD + NLAT, NLAT, NCTX)]
            ci2 = 0
            for g in range(4):
                wd = POOL_WINDOWS[g]
                nst = int(math.log2(wd))
                S.dma("pool", w1, wv[:, :, g * 512:(g + 1) * 512], writes=["w1"])
                S.dma("pool", wgr, b_w_grp[0, g].rearrange("(k p) c -> p k c", p=128), writes=["wgr"])
                for kc in range(4):
                    for ci, (off, w, kind) in enumerate(CHUNKS):
                        bank = ci % 4
                        for k in range(8):
                            S.op("pe", "matmul", PS[bank][:, :w], w1[:, k, kc * 128:(kc + 1) * 128], hT[:, k, off:off + w],
                                 start=(k == 0), stop=(k == 7), reads=["w1", ("hT", ci)], writes=[("ps", bank)])
                        pos = (PAD + off) if kind == 0 else (3 * PAD + off)
                        S.op("act", "copy", up[:, pos:pos + w], PS[bank][:, :w], reads=[("ps", bank)], writes=["up"])
                    for k in range(8):
                        S.op("pe", "matmul", PS[7][:, 0:32], w1[:, k, kc * 128:(kc + 1) * 128], hTh[:, k, :],
                             start=(k == 0), stop=(k == 7), reads=["w1", "hTh"], writes=[("ps", 7)])
                    for (pc, hc0, mi) in ((0, 0, 0), (PAD + NLAT, 8, 1), (2 * PAD + NLAT, 16, 0), (3 * PAD + T, 24, 1)):
                        S.op("dve", "tensor_scalar", up[:, pc:pc + 8], PS[7][:, hc0:hc0 + 8], hmask[:, mi:mi + 1], None, ALU.mult,
                             reads=[("ps", 7)], writes=["up"])
                    cur = up
                    curkey = "up"
                    L = LP
                    for s_ in range(nst):
                        sh = 2 ** s_
                        nxt = ab[s_ % 2]
                        nkey = ("ab", s_ % 2)
                        Ln = L - sh
                        S.op("pool", "tensor_tensor", nxt[:, 0:Ln], cur[:, 0:Ln], cur[:, sh:sh + Ln], ALU.add,
                             reads=[curkey], writes=[nkey])
                        cur, curkey, L = nxt, nkey, Ln
                    for (p0, t0_, n_) in regions:
                        S.op("dve", "scalar_tensor_tensor", dT[kc][:, t0_:t0_ + n_], cur[:, p0 - wd // 2:p0 - wd // 2 + n_], 1.0 / wd,
                             up[:, p0:p0 + n_], ALU.mult, ALU.subtract, reads=[curkey, "up"], writes=[("dT", kc)])
                    for ri, (p0, t0_, n_) in enumerate(regions):
                        for side in range(2):
                            e0 = 0 if side == 0 else n_ - 8
                            S.op("dve", "tensor_tensor", etmp, cur[:, p0 - wd // 2 + e0:p0 - wd // 2 + e0 + 8],
                                 edge[:, g, (ri * 2 + side) * 8:(ri * 2 + side) * 8 + 8], ALU.mult, reads=[curkey, "edge"], writes=["etmp"])
                            S.op("dve", "tensor_tensor", dT[kc][:, t0_ + e0:t0_ + e0 + 8], etmp, up[:, p0 + e0:p0 + e0 + 8], ALU.subtract,
                                 reads=["etmp", "up"], writes=[("dT", kc)])
                S.dma("pool", w1, wv[:, :, 2048 + g * 512:2048 + (g + 1) * 512], writes=["w1"])
                for ob in range(4):
                    jb = 4 * g + ob
                    for ci, (off, w, kind) in enumerate(CHUNKS):
                        if kind == 1 and not need_ctx:
                            continue
                        c2 = ci2 % 2
                        ci2 += 1
                        by = 4 + c2
                        bz = 6 + c2
                        for kc in range(4):
                            S.op("pe", "matmul", PS[by][:, :w], wgr[:, kc, ob * 128:(ob + 1) * 128], dT[kc][:, off:off + w],
                                 start=(kc == 0), stop=(kc == 3), reads=["wgr", ("dT", kc)], writes=[("ps", by)])
                        for k in range(8):
                            S.op("pe", "matmul", PS[bz][:, :w], w1[:, k, ob * 128:(ob + 1) * 128], hT[:, k, off:off + w],
                                 start=(k == 0), stop=(k == 7), reads=["w1", ("hT", ci)], writes=[("ps", bz)])
                        S.op("act", "activation", szs[c2][:, :w], PS[bz][:, :w], AF.Silu, reads=[("ps", bz)], writes=[("szs", c2)])
                        S.op("dve", "tensor_scalar", tys[c2][:, :w], PS[by][:, :w], bgs[:, 0, jb:jb + 1], bgs[:, 1, jb:jb + 1], ALU.add, ALU.mult,
                             reads=[("ps", by), "bgs"], writes=[("tys", c2)])
                        S.op("dve", "tensor_tensor", gch[c2][:, :w], tys[c2][:, :w], szs[c2][:, :w], ALU.mult,
                             reads=[("tys", c2), ("szs", c2)], writes=[("gch", c2)])
                        S.dma("sp", GT[:, jb, off:off + w], gch[c2][:, :w], reads=[("gch", c2)])
            S.barrier()
        outproj_stage(li, wo_pre, need_ctx)
        eso.close()

    def final_stage():
        with contextlib.ExitStack() as es:
            def tmp(name, shape, dt=F32):
                return es.enter_context(sbt(name, list(shape), dt)).ap()
            fg = tmp("f_g", [128, 8])
            xc = [tmp("f_xc%d" % i, [128, 8, 512]) for i in range(2)]
            sq = [tmp("f_sq%d" % i, [128, 8, 512], BF16) for i in range(2)]
            rstd = [tmp("f_rs%d" % i, [128, 512]) for i in range(2)]
            ot = [tmp("f_ot%d" % i, [128, D]) for i in range(2)]
            S.dma("sp", fg, final_gT, writes=["fg"])
            oi = 0
            for ci, (off, w, kind) in enumerate(CHUNKS):
                if kind == 1:
                    continue
                s = ci % 2
                bank = ci % 2
                S.dma("sp", xc[s], xT[:, :, off:off + w], writes=[("xc", s)])
                S.op("dve", "tensor_tensor", sq[s], xc[s], xc[s], ALU.mult, reads=[("xc", s)], writes=[("sq", s)])
                for k in range(8):
                    S.op("pe", "matmul", PS[bank], onesb, sq[s][:, k, :], start=(k == 0), stop=(k == 7),
                         reads=["onesb", ("sq", s)], writes=[("ps", bank)])
                S.op("act", "activation", rstd[s], PS[bank], AF.Sqrt, bias=epsc, scale=1.0 / D, reads=[("ps", bank)], writes=[("rstd", s)])
                S.op("dve", "reciprocal", rstd[s], rstd[s], reads=[("rstd", s)], writes=[("rstd", s)])
                for k in range(8):
                    S.op("dve", "scalar_tensor_tensor", xc[s][:, k, :], xc[s][:, k, :], fg[:, k:k + 1], rstd[s], ALU.mult, ALU.mult,
                         reads=[("xc", s), ("rstd", s), "fg"], writes=[("xc", s)])
                for tt in range(4):
                    o2 = oi % 2
                    oi += 1
                    for hb in range(2):
                        bank2 = 2 + (oi % 2) * 2 + hb
                        for k4 in range(4):
                            k = hb * 4 + k4
                            S.op("pe", "transpose", PS[bank2][:, k4 * 128:(k4 + 1) * 128], xc[s][:, k, tt * 128:(tt + 1) * 128], ident,
                                 reads=[("xc", s), "ident"], writes=[("ps", bank2)])
                        if hb == 0:
                            S.op("dve", "tensor_copy", ot[o2][:, 0:512], PS[bank2], reads=[("ps", bank2)], writes=[("ot", o2, 0)])
                        else:
                            S.op("act", "copy", ot[o2][:, 512:1024], PS[bank2], reads=[("ps", bank2)], writes=[("ot", o2, 1)])
                    S.dma("sp", out_d[off + tt * 128:off + (tt + 1) * 128, :], ot[o2], reads=[("ot", o2, 0), ("ot", o2, 1)])
            S.barrier()

    prologue()
    for li in range(depth):
        m = li % 3
        need_ctx = li < DEPTH - 1
        if m == 0:
            layer_A(li, li // 3, need_ctx)
        elif m == 1:
            layer_B(li, need_ctx)
        else:
            layer_C(li, need_ctx)
    if dbg is not None:
        with sbt("dbg_t", [128, 8, T], F32) as dt_:
            S.dma("sp", dt_.ap(), xT)
            S.barrier()
            S.dma("sp", dbg, dt_.ap())
            S.barrier()
    if depth == DEPTH:
        final_stage()
    S.finalize()
    nc._declared_inputs = declared
    return nc


NLAT_CORE = 2048
NCTX_CORE = 128
_PROG_CACHE = {}


def make_in_maps(inputs, NLAT=None, NCTX=None, n_cores=8):
    f = lambda a: np.ascontiguousarray(np.asarray(a, dtype=np.float32))
    x = f(inputs["x"]); c = f(inputs["c"]); ctx = f(inputs["ctx"]); c_ctx = f(inputs["c_ctx"])
    NLAT = NLAT or NLAT_CORE
    NCTX = NCTX or NCTX_CORE
    shared = {
        "w_ada": f(inputs["w_ada"]),
        "b_adaT": f(np.asarray(inputs["b_ada"]).reshape(DEPTH, 24, 128).transpose(0, 2, 1)),
        "norm_gT": f(np.asarray(inputs["norm_g"]).reshape(DEPTH, 8, 128).transpose(0, 2, 1)),
        "final_gT": f(np.asarray(inputs["final_g"]).reshape(8, 128).T),
        "a_w_in": f(inputs["a_w_in"]),
        "a_w_out": f(inputs["a_w_out"]),
        "a_lam": f(np.stack([inputs["a_lam_q1"], inputs["a_lam_k1"], inputs["a_lam_q2"], inputs["a_lam_k2"]], axis=1)),
        "a_subln": f(inputs["a_subln_g"]),
        "b_w_in": f(inputs["b_w_in"]),
        "b_w_grp": f(inputs["b_w_grp"]),
        "b_bT": f(np.asarray(inputs["b_b_grp"]).reshape(16, 128).T),
        "b_sT": f(np.asarray(inputs["b_scale"]).reshape(16, 128).T),
        "b_w_out": f(inputs["b_w_out"]),
        "c_w_in": f(inputs["c_w_in"]),
        "c_sink": f(inputs["c_sink"]),
        "c_w_out": f(inputs["c_w_out"]),
        "ident": np.eye(128, dtype=np.float32),
        "band": band_mask(),
    }
    per_rank = []
    for s_ in range(2):
        cosA, sinA, permA = rope_tables(64, 2, s_ * NLAT, NLAT)
        cosC, sinC, permC = rope_tables(128, 1, s_ * NLAT, NLAT)
        ic = invcnt_tables(NLAT, NCTX, s_ * NLAT, 2 * NLAT, s_ * NCTX, 2 * NCTX)
        e = np.zeros((4, 32), np.float32)
        for g in range(4):
            e[g, 0:8] = ic[g, 0:8]
            e[g, 8:16] = ic[g, NLAT - 8:NLAT]
            e[g, 16:24] = ic[g, NLAT:NLAT + 8]
            e[g, 24:32] = ic[g, NLAT + NCTX - 8:NLAT + NCTX]
        hm = np.zeros((128, 2), np.float32)
        hm[:, 0] = 1.0 if s_ == 1 else 0.0
        hm[:, 1] = 1.0 if s_ == 0 else 0.0
        per_rank.append({"ropeA": f(np.stack([cosA, sinA])), "ropeC": f(np.stack([cosC, sinC])), "permA": permA, "permC": permC,
                         "edge": e, "hmask": hm})
    maps = []
    for core in range(n_cores):
        b = core // 2
        s_ = core % 2
        m = dict(shared)
        m.update(per_rank[s_])
        m["x_tok"] = np.ascontiguousarray(x[b][s_ * NLAT:(s_ + 1) * NLAT])
        m["ctx_tok"] = np.ascontiguousarray(ctx[b][s_ * NCTX:(s_ + 1) * NCTX])
        cv = np.stack([c[b].reshape(8, 128).T, c_ctx.reshape(8, 128).T], axis=-1)
        m["cvec"] = f(cv)
        maps.append(m)
    return maps


def kernel(**inputs):
    key = "split"
    if key not in _PROG_CACHE:
        _PROG_CACHE[key] = build_program(NLAT_CORE, NCTX_CORE)
    nc = _PROG_CACHE[key]
    maps = make_in_maps(inputs)
    maps = [{k: v for k, v in m.items() if k in nc._declared_inputs} for m in maps]
    res = run_bass_kernel_spmd(nc, maps, core_ids=list(range(8)))
    B = np.asarray(inputs["x"]).shape[0]
    out = np.stack([np.concatenate([np.asarray(res.results[2 * b + s_]["out"], dtype=np.float32) for s_ in range(2)], axis=0)
                    for b in range(B)], axis=0)
    return out
```

```python
import contextlib
import math
import numpy as np
import concourse.bass as bass
import concourse.mybir as mybir
from concourse.bass_utils import run_bass_kernel_spmd

F32 = mybir.dt.float32
BF16 = mybir.dt.bfloat16
ALU = mybir.AluOpType
AF = mybir.ActivationFunctionType

D = 1024
DI = 2048
DEPTH = 4
GRID_W = 64
EPS = 1e-6
POOL_WINDOWS = (2, 4, 8, 16)
PAD = 8
PAIR_GROUPS = [[0, 1], [2, 3], [4, 5], [6, 7]]

SAME_ENGINE_SYNC = True
STAGE_LIMIT = 99
SEM_EPOCH_LIMIT = 6000
EMBED_WAIT = True
N_DMA_SEMS = 12


class Buf:
    __slots__ = ("last_w", "readers", "dma_readers")

    def __init__(self):
        self.last_w = None
        self.readers = {}
        self.dma_readers = []


class Op:
    __slots__ = ("eng", "meth", "args", "kw", "deps", "is_dma", "sig_needed", "sig_val", "sem", "val", "epoch")

    def __init__(self, eng, meth, args, kw, is_dma):
        self.eng = eng
        self.meth = meth
        self.args = args
        self.kw = kw
        self.deps = []
        self.is_dma = is_dma
        self.sig_needed = False
        self.sig_val = 0
        self.sem = None
        self.val = 0
        self.epoch = 0


class Sched:
    ENGS = ("pe", "act", "dve", "pool", "sp")
    ATTR = {"pe": "tensor", "act": "scalar", "dve": "vector", "pool": "gpsimd", "sp": "sync"}

    def __init__(self, nc):
        self.nc = nc
        self.ops = {e: [] for e in self.ENGS}
        self.dma_count = {e: 0 for e in self.ENGS}
        self.dma_hist = {e: [] for e in self.ENGS}
        self.bufs = {}
        self.last_real = {e: None for e in self.ENGS}
        self.dmas_since = []
        self.n_coll = 0

    def buf(self, key):
        b = self.bufs.get(key)
        if b is None:
            b = Buf()
            self.bufs[key] = b
        return b

    def _track(self, op, reads, writes):
        deps = op.deps
        rb = [self.buf(k) for k in reads]
        wb = [self.buf(k) for k in writes]
        for k, b in zip(reads, rb):
            if b.last_w is not None:
                deps.append(b.last_w)
            if isinstance(k, tuple) and k[0] == "ps":
                for e2, r in b.readers.items():
                    if e2 != op.eng:
                        deps.append(r)
        for b in wb:
            if b.last_w is not None:
                deps.append(b.last_w)
            deps.extend(b.readers.values())
            deps.extend(b.dma_readers)
        for b in rb:
            if op.is_dma:
                b.dma_readers.append(op)
            else:
                b.readers[op.eng] = op
        for b in wb:
            b.last_w = op
            b.readers = {}
            b.dma_readers = []

    def op(self, eng, meth, *args, reads=(), writes=(), **kw):
        o = Op(eng, meth, args, kw, False)
        self._track(o, reads, writes)
        self.ops[eng].append(o)
        self.last_real[eng] = o
        return o

    def dma(self, eng, out, in_, reads=(), writes=(), **kw):
        o = Op(eng, "dma_start", (), dict(out=out, in_=in_, **kw), True)
        i = self.dma_count[eng]
        self.dma_count[eng] = i + 1
        o.sem = (eng, i % N_DMA_SEMS)
        o.val = 16 * (i // N_DMA_SEMS + 1)
        hist = self.dma_hist[eng]
        if i >= N_DMA_SEMS:
            o.deps.append(hist[i - N_DMA_SEMS])
        hist.append(o)
        self._track(o, reads, writes)
        self.ops[eng].append(o)
        self.dmas_since.append(o)
        return o

    def coll(self, kind, groups, in_ap, out_ap, reads=()):
        o = Op("pool", "collective_compute", (kind, ALU.bypass), dict(replica_groups=groups, ins=[in_ap], outs=[out_ap]), True)
        self.n_coll += 1
        o.sem = ("cc", self.n_coll)
        o.val = 1
        self._track(o, reads, ())
        self.ops["pool"].append(o)
        self.dmas_since.append(o)
        return o

    def barrier(self):
        deps = [o for o in self.last_real.values() if o is not None] + self.dmas_since
        for e in self.ENGS:
            o = Op(e, None, (), {}, False)
            o.deps = list(deps)
            self.ops[e].append(o)
        self.dmas_since = []
        self.bufs = {}

    def finalize(self):
        nc = self.nc
        self.barrier()
        for e in self.ENGS:
            for o in self.ops[e]:
                for d in o.deps:
                    if d.is_dma:
                        continue
                    if d.eng == o.eng and not o.is_dma and (not SAME_ENGINE_SYNC or d.eng == "pe"):
                        continue
                    d.sig_needed = True
        nep = {}
        for e in self.ENGS:
            c = 0
            ep = 0
            for o in self.ops[e]:
                if o.meth is None and c > SEM_EPOCH_LIMIT:
                    ep += 1
                    c = 0
                o.epoch = ep
                if not o.is_dma and o.sig_needed:
                    c += 1
                    o.sig_val = c
            nep[e] = ep + 1
        with contextlib.ExitStack() as stack:
            esem = {}
            dsem = {}
            for e in self.ENGS:
                for ep in range(nep[e]):
                    esem[(e, ep)] = stack.enter_context(nc.semaphore("es_%s_%d" % (e, ep)))
                if self.dma_count[e]:
                    for j in range(N_DMA_SEMS):
                        dsem[(e, j)] = stack.enter_context(nc.semaphore("ds_%s_%d" % (e, j)))
            for j in range(1, self.n_coll + 1):
                dsem[("cc", j)] = stack.enter_context(nc.semaphore("cc_%d" % j))
            block = stack.enter_context(nc.Block())
            for e in self.ENGS:
                self._emit_engine(block, e, self.ops[e], esem, dsem)

    def _emit_engine(self, block, e, ops, esem, dsem):
        known = {}

        def body(eng):
            for o in ops:
                w = {}
                for d in o.deps:
                    if d.is_dma:
                        key = ("d",) + d.sem
                        v = d.val
                    else:
                        if d.eng == e and not o.is_dma and (not SAME_ENGINE_SYNC or e == "pe"):
                            continue
                        key = ("e", d.eng, d.epoch)
                        v = d.sig_val
                    if known.get(key, 0) >= v:
                        continue
                    if w.get(key, 0) < v:
                        w[key] = v
                wl = list(w.items())
                embed = None
                if EMBED_WAIT and o.meth is not None and wl and not o.is_dma:
                    embed = wl.pop()
                for key, v in wl:
                    sem = esem[(key[1], key[2])] if key[0] == "e" else dsem[(key[1], key[2])]
                    eng.wait_ge(sem, v)
                    known[key] = v
                if o.meth is None:
                    continue
                inst = getattr(eng, o.meth)(*o.args, **o.kw)
                if embed is not None:
                    key, v = embed
                    sem = esem[(key[1], key[2])] if key[0] == "e" else dsem[(key[1], key[2])]
                    inst._wait_ge(sem, v)
                    known[key] = v
                if o.is_dma:
                    inst.then_inc(dsem[o.sem], 1 if o.sem[0] == "cc" else 16)
                elif o.sig_needed:
                    inst.then_inc(esem[(e, o.epoch)], 1)

        getattr(block, self.ATTR[e])(body)


def rope_tables(head_dim, rep, pos0, n):
    half = head_dim // 2
    quarter = head_dim // 4
    inv = (10000.0 ** (-np.arange(quarter, dtype=np.float32) / quarter)).astype(np.float32)
    pos = np.arange(pos0, pos0 + n)
    row = (pos // GRID_W).astype(np.float32)
    col = (pos % GRID_W).astype(np.float32)
    cos = np.zeros((128, n), np.float32)
    sin = np.zeros((128, n), np.float32)
    perm = np.zeros((128, 128), np.float32)
    for p in range(128):
        d = p % head_dim
        base = p - d
        hsel = d // half
        dd = d % half
        i = dd % quarter
        ang = ((row if hsel == 0 else col) * inv[i]).astype(np.float32)
        cos[p] = np.cos(ang)
        if dd < quarter:
            partner = d + quarter
            sin[p] = -np.sin(ang)
        else:
            partner = d - quarter
            sin[p] = np.sin(ang)
        perm[base + partner, p] = 1.0
    return cos, sin, perm


def invcnt_tables(nlat, nctx, lat0, lat_total, ctx0, ctx_total):
    T = nlat + nctx
    out = np.zeros((4, T), np.float32)
    for g, w in enumerate(POOL_WINDOWS):
        for (n, o, p0, tot) in ((nlat, 0, lat0, lat_total), (nctx, nlat, ctx0, ctx_total)):
            t = np.arange(p0, p0 + n)
            lo = np.clip(t - w // 2, 0, tot)
            hi = np.clip(t - w // 2 + w, 0, tot)
            out[g, o:o + n] = 1.0 / (hi - lo).astype(np.float32)
    return out


def band_mask():
    k = np.arange(128)[:, None]
    q = np.arange(128)[None, :]
    m = np.ones((128, 384), np.float32)
    m[:, 0:128] = (k <= q)
    m[:, 256:384] = (k >= q)
    return m


def build_program(NLAT, NCTX, depth=DEPTH, debug_x=False, groups=PAIR_GROUPS):
    T = NLAT + NCTX
    NT = T // 128
    NLT = NLAT // 128
    NTK = 2 * NT
    CTX_TILES = [NT - 1, 2 * NT - 1]
    CHUNKS = [(o, 512, 0) for o in range(0, NLAT, 512)] + [(NLAT + o, min(512, NCTX - o), 1) for o in range(0, NCTX, 512)]
    LP = T + 4 * PAD

    nc = bass.Bass("TRN2", target_bir_lowering=False)
    S = Sched(nc)
    coll_groups = groups

    declared = set()

    def din(name, shape, dt=F32):
        declared.add(name)
        return nc.dram_tensor(name, list(shape), dt, kind="ExternalInput").ap()

    x_tok = din("x_tok", [NLAT, D])
    ctx_tok = din("ctx_tok", [NCTX, D])
    cvec = din("cvec", [128, 8, 2])
    w_ada = din("w_ada", [DEPTH, D, 3 * D])
    b_adaT = din("b_adaT", [DEPTH, 128, 24])
    norm_gT = din("norm_gT", [DEPTH, 128, 8])
    final_gT = din("final_gT", [128, 8])
    a_w_in = din("a_w_in", [2, D, 4 * DI]) if depth >= 1 else None
    a_w_out = din("a_w_out", [2, DI, D]) if depth >= 1 else None
    a_lam = din("a_lam", [2, 4, 64])
    a_subln = din("a_subln", [2, 128])
    b_w_in = din("b_w_in", [1, D, 2 * DI]) if depth >= 2 else None
    b_w_grp = din("b_w_grp", [1, 4, 512, 512]) if depth >= 2 else None
    b_bT = din("b_bT", [128, 16])
    b_sT = din("b_sT", [128, 16])
    b_w_out = din("b_w_out", [1, DI, D]) if depth >= 2 else None
    c_w_in = din("c_w_in", [1, D, 5120]) if depth >= 3 else None
    c_sink = din("c_sink", [1, 16])
    c_w_out = din("c_w_out", [1, DI, D]) if depth >= 3 else None
    ident_d = din("ident", [128, 128])
    ropeA_d = din("ropeA", [2, 128, NLAT])
    ropeC_d = din("ropeC", [2, 128, NLAT])
    permA_d = din("permA", [128, 128])
    permC_d = din("permC", [128, 128])
    edge_d = din("edge", [4, 32])
    band_d = din("band", [128, 384])
    hmask_d = din("hmask", [128, 2])
    out_d = nc.dram_tensor("out", [NLAT, D], F32, kind="ExternalOutput").ap()

    xT = nc.dram_tensor("xT_s", [128, 8, T], F32).ap()
    QT = nc.dram_tensor("QT_s", [16, 128, T], BF16).ap()
    KT = nc.dram_tensor("KT_s", [16, 128, T], BF16).ap()
    VD = nc.dram_tensor("VD_s", [16, T, 128], BF16).ap()
    SZ = nc.dram_tensor("SZ_s", [T, DI], F32).ap()
    GT = nc.dram_tensor("GT_s", [128, 16, T], BF16).ap()
    KT_all = nc.dram_tensor("KT_all", [8, 2, 2, 128, T], BF16).ap()
    VD_all = nc.dram_tensor("VD_all", [8, 2, 2, T, 128], BF16).ap()
    KTc = nc.dram_tensor("KTc_s", [4, 128, T], BF16).ap()
    VDc = nc.dram_tensor("VDc_s", [4, T, 128], BF16).ap()
    KTc_all = nc.dram_tensor("KTc_all", [2, 2, 2, 128, T], BF16).ap()
    VDc_all = nc.dram_tensor("VDc_all", [2, 2, 2, T, 128], BF16).ap()
    xh_send = nc.dram_tensor("xh_send", [128, 8, 32], F32).ap()
    xh_all = nc.dram_tensor("xh_all", [2, 128, 8, 32], F32).ap()
    dbg = None
    if debug_x:
        dbg = nc.dram_tensor("dbg", [128, 8, T], F32, kind="ExternalOutput").ap()

    PSall = nc.alloc_psum_tensor("psall", [128, 8 * 512], F32).ap()
    PS = [PSall[:, i * 512:(i + 1) * 512] for i in range(8)]

    def sb(name, shape, dt=F32):
        return nc.alloc_sbuf_tensor(name, list(shape), dt).ap()

    _uid = [0]

    def sbt(name, shape, dt):
        _uid[0] += 1
        return nc.sbuf_tensor("%s_u%d" % (name, _uid[0]), list(shape), dt)

    ident = sb("ident_f", [128, 128])
    identb = sb("ident_b", [128, 128], BF16)
    onesb = sb("ones_b", [128, 128], BF16)
    MODS = sb("mods", [128, DEPTH, 2, 3, 8])
    neglam = sb("neglam", [128, 2])
    subg = sb("subg", [128, 2, 128])
    esink = sb("esink", [128, 16])
    mhalf = sb("mhalf", [128, 1])
    hmask = sb("hmask_sb", [128, 2])
    epsc = sb("epsc", [128, 1])

    def prologue():
        with contextlib.ExitStack() as es:
            def tmp(name, shape, dt=F32):
                return es.enter_context(sbt(name, list(shape), dt)).ap()
            S.dma("sp", ident, ident_d, writes=["ident"])
            S.dma("sp", hmask, hmask_d, writes=["hmask"])
            S.dma("pool", identb, ident_d, writes=["identb"])
            S.op("dve", "memset", onesb, 1.0, writes=["onesb"])
            S.op("dve", "memset", mhalf, -0.5, writes=["mhalf"])
            S.op("dve", "memset", epsc, EPS, writes=["epsc"])
            xin = [tmp("xin%d" % i, [128, D]) for i in range(2)]
            xst = [tmp("xst%d" % i, [128, 8, 128]) for i in range(2)]
            for t in range(NT):
                s = t % 2
                src = x_tok[t * 128:(t + 1) * 128, :] if t < NLT else ctx_tok[(t - NLT) * 128:(t - NLT + 1) * 128, :]
                S.dma("sp", xin[s], src, writes=[("xin", s)])
                for hb in range(2):
                    bank = (t % 2) * 2 + hb
                    for k4 in range(4):
                        k = hb * 4 + k4
                        S.op("pe", "transpose", PS[bank][:, k4 * 128:(k4 + 1) * 128], xin[s][:, k * 128:(k + 1) * 128], ident,
                             reads=[("xin", s), "ident"], writes=[("ps", bank)])
                    dst = xst[s][:, hb * 4:(hb + 1) * 4, :]
                    srcp = PS[bank].rearrange("p (k c) -> p k c", k=4)
                    if hb == 0:
                        S.op("dve", "tensor_copy", dst, srcp, reads=[("ps", bank)], writes=[("xst", s, hb)])
                    else:
                        S.op("act", "copy", dst, srcp, reads=[("ps", bank)], writes=[("xst", s, hb)])
                S.dma("sp", xT[:, :, t * 128:(t + 1) * 128], xst[s], reads=[("xst", s, 0), ("xst", s, 1)])
            cv = tmp("cv", [128, 8, 2])
            cs = tmp("cs", [128, 8, 2])
            S.dma("sp", cv, cvec, writes=["cv"])
            S.op("act", "activation", cs, cv, AF.Silu, reads=["cv"], writes=["cs"])
            wa = [tmp("wa%d" % i, [128, 8, 512]) for i in range(2)]
            modraw = tmp("modraw", [128, 24, 2])
            bT = tmp("bT", [128, DEPTH, 24])
            gT_ = tmp("gT_", [128, DEPTH, 8])
            S.dma("sp", bT, b_adaT.rearrange("l p c -> p l c"), writes=["bT"])
            S.dma("sp", gT_, norm_gT.rearrange("l p c -> p l c"), writes=["gT_"])
            mod = tmp("mod", [128, 2, 24])
            pi = 0
            for i in range(depth):
                for pc in range(6):
                    s = pi % 2
                    pi += 1
                    S.dma("sp", wa[s], w_ada[i].rearrange("(k p) c -> p k c", p=128)[:, :, pc * 512:(pc + 1) * 512], writes=[("wa", s)])
                    for blk in range(4):
                        cb = pc * 4 + blk
                        for k in range(8):
                            S.op("pe", "matmul", PS[4][:, cb * 2:cb * 2 + 2], wa[s][:, k, blk * 128:(blk + 1) * 128], cs[:, k, :],
                                 start=(k == 0), stop=(k == 7), reads=[("wa", s), "cs"], writes=[("ps", 4)])
                S.op("dve", "tensor_copy", modraw, PS[4][:, 0:48].rearrange("p (c j) -> p c j", j=2), reads=[("ps", 4)], writes=["modraw"])
                for kind in range(2):
                    S.op("dve", "tensor_tensor", mod[:, kind, :], modraw[:, :, kind], bT[:, i, :], ALU.add,
                         reads=["modraw", "bT"], writes=[("mod", kind)])
                    S.op("dve", "scalar_tensor_tensor", MODS[:, i, kind, 0, :], mod[:, kind, 8:16], 1.0, gT_[:, i, :], ALU.add, ALU.mult,
                         reads=[("mod", kind), "gT_"], writes=["MODS"])
                    S.op("dve", "tensor_copy", MODS[:, i, kind, 1, :], mod[:, kind, 0:8], reads=[("mod", kind)], writes=["MODS"])
                    S.op("dve", "tensor_copy", MODS[:, i, kind, 2, :], mod[:, kind, 16:24], reads=[("mod", kind)], writes=["MODS"])
            lv = tmp("lv", [128, 2, 4, 64])
            S.dma("sp", lv, a_lam.rearrange("j f d -> (j f d)").partition_broadcast(128).rearrange("p (j f d) -> p j f d", j=2, f=4), writes=["lv"])
            junk = tmp("junk", [128, 64])
            ssum = tmp("ssum", [128, 4])
            esum = tmp("esum", [128, 4])
            for j in range(2):
                for q in range(2):
                    S.op("dve", "scalar_tensor_tensor", junk, lv[:, j, 2 * q, :], 1.0, lv[:, j, 2 * q + 1, :], ALU.mult, ALU.mult,
                         accum_out=ssum[:, 2 * j + q:2 * j + q + 1], reads=["lv"], writes=["junk", "ssum"])
            S.op("act", "activation", esum, ssum, AF.Exp, reads=["ssum"], writes=["esum"])
            for j in range(2):
                li = 0.8 - 0.6 * math.exp(-0.3 * (3 * j))
                S.op("dve", "scalar_tensor_tensor", neglam[:, j:j + 1], esum[:, 2 * j + 1:2 * j + 2], -li, esum[:, 2 * j:2 * j + 1], ALU.add, ALU.subtract,
                     reads=["esum"], writes=["neglam"])
                S.dma("sp", subg[:, j, :], a_subln[j].partition_broadcast(128), writes=[("subg", j)])
                S.op("dve", "tensor_scalar", subg[:, j, :], subg[:, j, :], 1.0 - li, None, ALU.mult, reads=[("subg", j)], writes=[("subg", j)])
            sk = tmp("sk", [128, 16])
            S.dma("sp", sk, c_sink[0].partition_broadcast(128), writes=["sk"])
            S.op("act", "activation", esink, sk, AF.Exp, reads=["sk"], writes=["esink"])
            S.barrier()

    def norm_stage(li, hT, es):
        xc = [es.enter_context(sbt("n_xc%d" % i, [128, 8, 512], F32)).ap() for i in range(2)]
        sq = [es.enter_context(sbt("n_sq%d" % i, [128, 8, 512], BF16)).ap() for i in range(2)]
        rstd = [es.enter_context(sbt("n_rs%d" % i, [128, 512], F32)).ap() for i in range(2)]
        tmpx = [es.enter_context(sbt("n_tx%d" % i, [128, 512], F32)).ap() for i in range(2)]
        for ci, (off, w, kind) in enumerate(CHUNKS):
            s = ci % 2
            bank = ci % 2
            S.dma("sp", xc[s][:, :, :w], xT[:, :, off:off + w], writes=[("xc", s)])
            S.op("dve", "tensor_tensor", sq[s][:, :, :w], xc[s][:, :, :w], xc[s][:, :, :w], ALU.mult, reads=[("xc", s)], writes=[("sq", s)])
            for k in range(8):
                S.op("pe", "matmul", PS[bank][:, :w], onesb, sq[s][:, k, :w], start=(k == 0), stop=(k == 7),
                     reads=["onesb", ("sq", s)], writes=[("ps", bank)])
            S.op("act", "activation", rstd[s][:, :w], PS[bank][:, :w], AF.Sqrt, bias=epsc, scale=1.0 / D,
                 reads=[("ps", bank), "epsc"], writes=[("rstd", s)])
            S.op("dve", "reciprocal", rstd[s][:, :w], rstd[s][:, :w], reads=[("rstd", s)], writes=[("rstd", s)])
            for k in range(8):
                ts = k % 2
                S.op("dve", "tensor_tensor", tmpx[ts][:, :w], xc[s][:, k, :w], rstd[s][:, :w], ALU.mult,
                     reads=[("xc", s), ("rstd", s)], writes=[("tmpx", ts)])
                S.op("act", "activation", hT[:, k, off:off + w], tmpx[ts][:, :w], AF.Identity,
                     bias=MODS[:, li, kind, 1, k:k + 1], scale=MODS[:, li, kind, 0, k:k + 1],
                     reads=[("tmpx", ts)], writes=[("hT", ci)])

    def inproj_stage(w_d, groups, hT, rope_d, perm_d, es, kt_dst, vd_dst, hook=None, hook_at=-1):
        wt = [es.enter_context(sbt("ip_w%d" % i, [128, 8, 512], BF16)).ap() for i in range(2)]
        cosT = es.enter_context(sbt("ip_cos", [128, NLAT], F32)).ap()
        sinT = es.enter_context(sbt("ip_sin", [128, NLAT], F32)).ap()
        permb = es.enter_context(sbt("ip_perm", [128, 128], BF16)).ap()
        qb = [es.enter_context(sbt("ip_qb%d" % i, [128, 512], BF16)).ap() for i in range(2)]
        t1 = [es.enter_context(sbt("ip_t1%d" % i, [128, 512], F32)).ap() for i in range(2)]
        t2 = [es.enter_context(sbt("ip_t2%d" % i, [128, 512], F32)).ap() for i in range(2)]
        stg = [es.enter_context(sbt("ip_st%d" % i, [128, T], BF16)).ap() for i in range(2)]
        vst = [es.enter_context(sbt("ip_vs%d" % i, [128, 512], BF16)).ap() for i in range(4)]
        zst = [es.enter_context(sbt("ip_zs%d" % i, [128, 512], F32)).ap() for i in range(4)]
        S.dma("sp", cosT, rope_d[0], writes=["cosT"])
        S.dma("sp", sinT, rope_d[1], writes=["sinT"])
        S.dma("pool", permb, perm_d, writes=["permb"])
        wv = w_d.rearrange("(k p) c -> p k c", p=128)
        bi = 0
        ti = 0
        for gi, (col0, role, idx0) in enumerate(groups):
            s = gi % 2
            S.dma("pool", wt[s], wv[:, :, col0:col0 + 512], writes=[("wt", s)])
            if hook is not None and gi >= hook_at:
                hook(gi - hook_at, len(groups) - hook_at)
            if role in ("q", "k"):
                dst = QT if role == "q" else kt_dst
                for blk in range(4):
                    ss = bi % 2
                    pending = None

                    def rope_tail(p):
                        bank_, c2_, off_, w_ = p
                        pb = 4 + c2_
                        S.op("pe", "matmul", PS[pb][:, :w_], permb, qb[c2_][:, :w_], start=True, stop=True,
                             reads=["permb", ("qb", c2_)], writes=[("ps", pb)])
                        S.op("dve", "tensor_tensor", t1[c2_][:, :w_], PS[bank_][:, :w_], cosT[:, off_:off_ + w_], ALU.mult,
                             reads=[("ps", bank_), "cosT"], writes=[("t1", c2_)])
                        S.op("dve", "tensor_tensor", t2[c2_][:, :w_], PS[pb][:, :w_], sinT[:, off_:off_ + w_], ALU.mult,
                             reads=[("ps", pb), "sinT"], writes=[("t2", c2_)])
                        S.op("dve", "tensor_tensor", stg[ss][:, off_:off_ + w_], t1[c2_][:, :w_], t2[c2_][:, :w_], ALU.add,
                             reads=[("t1", c2_), ("t2", c2_)], writes=[("stg", ss)])

                    for ci, (off, w, kind) in enumerate(CHUNKS):
                        bank = bi % 2 * 2 + ci % 2
                        c2 = ci % 2
                        for k in range(8):
                            S.op("pe", "matmul", PS[bank][:, :w], wt[s][:, k, blk * 128:(blk + 1) * 128], hT[:, k, off:off + w],
                                 start=(k == 0), stop=(k == 7), reads=[("wt", s), ("hT", ci)], writes=[("ps", bank)])
                        if kind == 0:
                            S.op("act", "copy", qb[c2][:, :w], PS[bank][:, :w], reads=[("ps", bank)], writes=[("qb", c2)])
                            if pending is not None:
                                rope_tail(pending)
                            pending = (bank, c2, off, w)
                        else:
                            S.op("act", "copy", stg[ss][:, off:off + w], PS[bank][:, :w], reads=[("ps", bank)], writes=[("stg", ss)])
                    if pending is not None:
                        rope_tail(pending)
                    S.dma("sp", dst[idx0 + blk], stg[ss], reads=[("stg", ss)], writes=([("KTd", idx0 + blk)] if role == "k" else []))
                    bi += 1
            else:
                for t in range(NT):
                    bank = 4 + ti % 4
                    s2 = ti % 4
                    ti += 1
                    for k in range(8):
                        S.op("pe", "matmul", PS[bank], hT[:, k, t * 128:(t + 1) * 128], wt[s][:, k, :],
                             start=(k == 0), stop=(k == 7), reads=[("wt", s), ("hT", (t * 128) // 512 if t < NLT else len(CHUNKS) - 1)], writes=[("ps", bank)])
                    if role == "v":
                        S.op("dve", "tensor_copy", vst[s2], PS[bank], reads=[("ps", bank)], writes=[("vst", s2)])
                        S.dma("sp", vd_dst[idx0:idx0 + 4, t * 128:(t + 1) * 128, :].rearrange("h t c -> t h c"), vst[s2].rearrange("p (h c) -> p h c", h=4), reads=[("vst", s2)], writes=[("VDd", idx0 // 4, t)])
                    else:
                        S.op("act", "activation", zst[s2], PS[bank], AF.Silu, reads=[("ps", bank)], writes=[("zst", s2)])
                        S.dma("sp", SZ[t * 128:(t + 1) * 128, idx0:idx0 + 512], zst[s2], reads=[("zst", s2)])

    def prefetch_wout(w_d, es):
        wo = es.enter_context(sbt("op_w", [128, 16, D], BF16)).ap()
        wv = w_d.rearrange("(k p) c -> p k c", p=128)
        for h in range(2):
            S.dma("pool", wo[:, h * 8:(h + 1) * 8, :], wv[:, h * 8:(h + 1) * 8, :], writes=[("wo", h)])
        return wo

    def outproj_stage(li, wo, need_ctx):
        with contextlib.ExitStack() as es:
            gch = [es.enter_context(sbt("op_g%d" % i, [128, 16, 512], BF16)).ap() for i in range(2)]
            xch = [es.enter_context(sbt("op_x%d" % i, [128, 8, 512], F32)).ap() for i in range(2)]
            ci = 0
            for (off, w, kind) in CHUNKS:
                if kind == 1 and not need_ctx:
                    continue
                s = ci % 2
                ci += 1
                S.dma("sp", gch[s][:, :, :w], GT[:, :, off:off + w], writes=[("gch", s)])
                S.dma("sp", xch[s][:, :, :w], xT[:, :, off:off + w], writes=[("xch", s)])
                for blk in range(8):
                    bank = blk % 4
                    for k in range(16):
                        S.op("pe", "matmul", PS[bank][:, :w], wo[:, k, blk * 128:(blk + 1) * 128], gch[s][:, k, :w],
                             start=(k == 0), stop=(k == 15), reads=[("wo", k // 8), ("gch", s)], writes=[("ps", bank)])
                    S.op("dve", "scalar_tensor_tensor", xch[s][:, blk, :w], PS[bank][:, :w], MODS[:, li, kind, 2, blk:blk + 1], xch[s][:, blk, :w],
                         ALU.mult, ALU.add, reads=[("ps", bank), ("xch", s)], writes=[("xch", s)])
                S.dma("sp", xT[:, :, off:off + w], xch[s][:, :, :w], reads=[("xch", s)])
            S.barrier()

    def layer_A(li, j, need_ctx):
        with contextlib.ExitStack() as es:
            hT = es.enter_context(sbt("hT", [128, 8, T], BF16)).ap()
            with contextlib.ExitStack() as es2:
                norm_stage(li, hT, es2)
                S.barrier()
            if STAGE_LIMIT <= 1:
                return None
            with contextlib.ExitStack() as es2:
                groups = [(2048 + cg * 512, "k", cg * 4) for cg in range(4)] + [(4096 + cg * 512, "v", cg * 4) for cg in range(4)] \
                    + [(cg * 512, "q", cg * 4) for cg in range(4)] + [(6144 + cg * 512, "z", cg * 512) for cg in range(4)]

                def gatherA(step, nsteps):
                    per = (8 + nsteps - 1) // nsteps
                    for gj in range(step * per, min(8, (step + 1) * per)):
                        S.coll("AllGather", coll_groups, KT[2 * gj:2 * gj + 2].rearrange("h p t -> (h p) t"),
                               KT_all[gj].rearrange("r h p t -> (r h p) t"), reads=[("KTd", 2 * gj), ("KTd", 2 * gj + 1)])
                        S.coll("AllGather", coll_groups, VD[2 * gj:2 * gj + 2].rearrange("h t c -> (h t) c"),
                               VD_all[gj].rearrange("r h t c -> (r h t) c"), reads=[("VDd", gj // 2, t) for t in range(NT)])
                inproj_stage(a_w_in[j], groups, hT, ropeA_d, permA_d, es2, KT, VD, hook=gatherA, hook_at=9)
                S.barrier()
        if STAGE_LIMIT <= 2:
            return
        eso = contextlib.ExitStack()
        wo_pre = prefetch_wout(a_w_out[j], eso)
        with contextlib.ExitStack() as es:
            def tmp(name, shape, dt=F32):
                return es.enter_context(sbt(name, list(shape), dt)).ap()
            kt = [tmp("a_kt%d" % i, [128, 2 * T], BF16) for i in range(2)]
            qt = [tmp("a_qt%d" % i, [128, T], BF16) for i in range(2)]
            vt = [tmp("a_vt%d" % i, [128, NTK, 130], BF16) for i in range(2)]
            pt = [tmp("a_pt%d" % i, [128, 1024], BF16) for i in range(4)]
            accB = [tmp("a_acc%d" % i, [128, 3, 480]) for i in range(2)]
            szt = [tmp("a_sz%d" % i, [128, 4, 128]) for i in range(2)]
            wg = [tmp("a_wg%d" % i, [128, 4, 128]) for i in range(2)]
            rec = [tmp("a_rec%d" % i, [128, 8]) for i in range(2)]
            t0 = [tmp("a_t0%d" % i, [128, 128]) for i in range(4)]
            ot = [tmp("a_o%d" % i, [128, 128]) for i in range(4)]
            junk = [tmp("a_junk%d" % i, [128, 128]) for i in range(4)]
            ssq = tmp("a_ssq", [128, 4])
            rs = tmp("a_rs", [128, 4])
            gtok = [tmp("a_gt%d" % i, [128, 512], BF16) for i in range(2)]
            gst = [tmp("a_gs%d" % i, [128, T], BF16) for i in range(2)]
            for s in range(2):
                S.op("pool", "memset", vt[s][:, :, 128:130], 1.0, writes=[("vt", s)])
            SZv = SZ.rearrange("(t p) c -> p t c", p=128)
            trp = PS[4].bitcast(BF16)
            cidx = 0
            sidx = 0
            def load_head(h_):
                hs_ = h_ % 2
                S.dma("sp", kt[hs_].rearrange("p (r t) -> p r t", r=2), KT_all[h_ // 2, :, h_ % 2].rearrange("r p t -> p r t"), writes=[("kt", hs_)])
                S.dma("sp", qt[hs_], QT[h_], writes=[("qt", hs_)])
                for r_ in range(2):
                    S.dma("sp", vt[hs_][:, r_ * NT:(r_ + 1) * NT, 0:128], VD_all[h_ // 2, r_, h_ % 2].rearrange("(t p) c -> p t c", p=128), writes=[("vt", hs_)])

            load_head(0)
            for h in range(16):
                hs = h % 2
                for chi, (off, w, kind) in enumerate(CHUNKS):
                    if chi == 1 and h + 1 < 16:
                        load_head(h + 1)
                    if kind == 1 and not need_ctx:
                        continue
                    cs_ = cidx % 2
                    cidx += 1
                    nq = w // 128
                    t0i = off // 128
                    keytiles = list(range(NTK)) if kind == 0 else CTX_TILES
                    S.dma("sp", szt[cs_][:, :nq, :], SZv[:, t0i:t0i + nq, h * 128:(h + 1) * 128], writes=[("szt", cs_)])
                    for qs in range(nq):
                        S.op("pool", "tensor_tensor", wg[cs_][:, qs, :], szt[cs_][:, qs, :], subg[:, j, :], ALU.mult,
                             reads=[("szt", cs_)], writes=[("wg", cs_)])
                    started = set()
                    nkt = len(keytiles)

                    def qk_exp(kti):
                        pr = kti % 2
                        ktile = keytiles[kti]
                        for m in range(2):
                            bank = 2 * pr + m
                            S.op("pe", "matmul", PS[bank][:, :w], kt[hs][64 * m:64 * m + 64, ktile * 128:(ktile + 1) * 128],
                                 qt[hs][64 * m:64 * m + 64, off:off + w], start=True, stop=True,
                                 reads=[("kt", hs), ("qt", hs)], writes=[("ps", 2 * pr), ("ps", 2 * pr + 1)] if m == 1 else [("ps", 2 * pr)])
                        src = PSall[:, 2 * pr * 512:(2 * pr + 2) * 512].rearrange("p (m c) -> p m c", m=2)[:, :, 0:w]
                        dst = pt[kti % 4].rearrange("p (m c) -> p m c", m=2)[:, :, 0:w]
                        S.op("act", "activation", dst, src, AF.Exp, scale=0.125,
                             reads=[("ps", 2 * pr), ("ps", 2 * pr + 1)], writes=[("pt", kti % 4)])

                    qk_exp(0)
                    if nkt > 1:
                        qk_exp(1)
                    for kti, ktile in enumerate(keytiles):
                        if kti + 2 < nkt:
                            qk_exp(kti + 2)
                        pr = kti % 4
                        for m in range(2):
                            for qs in range(nq):
                                a = m * nq + qs
                                bank = 5 + a // 3
                                c0 = (a % 3) * 160
                                st = bank not in started
                                started.add(bank)
                                S.op("pe", "matmul", PS[bank][:, c0:c0 + 129], pt[pr][:, m * 512 + qs * 128:m * 512 + (qs + 1) * 128], vt[hs][:, ktile, 0:129],
                                     start=st, stop=(kti == nkt - 1), skip_group_check=True,
                                     reads=[("pt", pr), ("vt", hs)], writes=[("ps", bank)])
                    na = 2 * nq
                    accv = accB[cs_].rearrange("p b (a c) -> p (b a) c", a=3)
                    for b3 in range((na + 2) // 3):
                        cnt = min(3, na - 3 * b3)
                        S.op("dve", "tensor_copy", accv[:, 3 * b3:3 * b3 + cnt, 0:129],
                             PS[5 + b3][:, 0:480].rearrange("p (a c) -> p a c", a=3)[:, 0:cnt, 0:129],
                             reads=[("ps", 5 + b3)], writes=[("accB", cs_)])
                    S.op("dve", "reciprocal", rec[cs_][:, 0:na], accv[:, 0:na, 128], reads=[("accB", cs_)], writes=[("rec", cs_)])
                    S.op("dve", "tensor_scalar", rec[cs_][:, nq:na], rec[cs_][:, nq:na], neglam[:, j:j + 1], None, ALU.mult,
                         reads=[("rec", cs_)], writes=[("rec", cs_)])
                    for qs in range(nq):
                        S.op("dve", "tensor_scalar", t0[qs], accv[:, qs, 0:128], rec[cs_][:, qs:qs + 1], None, ALU.mult,
                             reads=[("accB", cs_), ("rec", cs_)], writes=[("t0", qs)])
                    for qs in range(nq):
                        S.op("dve", "scalar_tensor_tensor", ot[qs], accv[:, nq + qs, 0:128], rec[cs_][:, nq + qs:nq + qs + 1], t0[qs], ALU.mult, ALU.add,
                             reads=[("accB", cs_), ("rec", cs_), ("t0", qs)], writes=[("ot", qs)])
                    for qs in range(nq):
                        S.op("dve", "scalar_tensor_tensor", junk[qs], ot[qs], 1.0 / 128.0, ot[qs], ALU.mult, ALU.mult,
                             accum_out=ssq[:, qs:qs + 1], reads=[("ot", qs)], writes=[("junk", qs), ("ssq", qs)])
                    for qs in range(nq):
                        S.op("dve", "tensor_scalar", ssq[:, qs:qs + 1], ssq[:, qs:qs + 1], EPS, None, ALU.add,
                             reads=[("ssq", qs)], writes=[("ssq", qs)])
                    for qs in range(nq):
                        S.op("pool", "tensor_tensor", rs[:, qs:qs + 1], ssq[:, qs:qs + 1], mhalf, ALU.pow,
                             reads=[("ssq", qs), "mhalf"], writes=[("rs", qs)])
                    for qs in range(nq):
                        S.op("dve", "scalar_tensor_tensor", gtok[cs_][:, qs * 128:(qs + 1) * 128], ot[qs], rs[:, qs:qs + 1], wg[cs_][:, qs, :],
                             ALU.mult, ALU.mult, reads=[("ot", qs), ("rs", qs), ("wg", cs_)], writes=[("gtok", cs_, qs)])
                    for qs in range(nq):
                        S.op("pe", "transpose", trp[:, qs * 128:(qs + 1) * 128], gtok[cs_][:, qs * 128:(qs + 1) * 128], identb,
                             reads=[("gtok", cs_, qs), "identb"], writes=[("ps", 4)])
                    S.op("dve", "tensor_copy", gst[hs][:, off:off + w], trp[:, 0:w], reads=[("ps", 4)], writes=[("gst", hs)])
                if need_ctx:
                    S.dma("sp", GT[:, h, :], gst[hs], reads=[("gst", hs)])
                else:
                    S.dma("sp", GT[:, h, 0:NLAT], gst[hs][:, 0:NLAT], reads=[("gst", hs)])
            S.barrier()
        if STAGE_LIMIT <= 3:
            eso.close()
            return
        outproj_stage(li, wo_pre, need_ctx)
        eso.close()

    def layer_C(li, need_ctx):
        with contextlib.ExitStack() as es:
            hT = es.enter_context(sbt("hT", [128, 8, T], BF16)).ap()
            with contextlib.ExitStack() as es2:
                norm_stage(li, hT, es2)
                S.barrier()
            with contextlib.ExitStack() as es2:
                groups = [(2048, "k", 0), (2560, "v", 0)] + [(cg * 512, "q", cg * 4) for cg in range(4)] \
                    + [(3072 + cg * 512, "z", cg * 512) for cg in range(4)]

                def gatherC(step, nsteps):
                    for gj in ([0, 1] if step == 0 else []):
                        S.coll("AllGather", coll_groups, KTc[2 * gj:2 * gj + 2].rearrange("h p t -> (h p) t"),
                               KTc_all[gj].rearrange("r h p t -> (r h p) t"), reads=[("KTd", 2 * gj), ("KTd", 2 * gj + 1)])
                        S.coll("AllGather", coll_groups, VDc[2 * gj:2 * gj + 2].rearrange("h t c -> (h t) c"),
                               VDc_all[gj].rearrange("r h t c -> (r h t) c"), reads=[("VDd", 0, t) for t in range(NT)])
                inproj_stage(c_w_in[0], groups, hT, ropeC_d, permC_d, es2, KTc, VDc, hook=gatherC, hook_at=3)
                S.barrier()
        scale = 128.0 ** -0.5
        eso = contextlib.ExitStack()
        wo_pre = prefetch_wout(c_w_out[0], eso)
        XB, XA, XC0, XC1 = NT, NT + 1, NT + 2, NT + 3
        with contextlib.ExitStack() as es:
            def tmp(name, shape, dt=F32):
                return es.enter_context(sbt(name, list(shape), dt)).ap()
            kt = [tmp("c_kt%d" % i, [128, (NT + 4) * 128], BF16) for i in range(2)]
            qt = [tmp("c_qt%d" % i, [128, T], BF16) for i in range(2)]
            vt = [tmp("c_vt%d" % i, [128, NT + 4, 130], BF16) for i in range(2)]
            pt = [tmp("c_pt%d" % i, [128, 512], BF16) for i in range(4)]
            bandm = tmp("c_band", [128, 384], BF16)
            hbm = tmp("c_hbm", [128, 128], BF16)
            ham = tmp("c_ham", [128, 128], BF16)
            accB = [tmp("c_acc%d" % i, [128, 2, 480]) for i in range(2)]
            szt = [tmp("c_sz%d" % i, [128, 4, 128]) for i in range(2)]
            den = [tmp("c_den%d" % i, [128, 4]) for i in range(2)]
            gtok = [tmp("c_gt%d" % i, [128, 512], BF16) for i in range(2)]
            gst = [tmp("c_gs%d" % i, [128, T], BF16) for i in range(2)]
            S.dma("pool", bandm, band_d, writes=["bandm"])
            S.op("dve", "tensor_scalar", hbm, bandm[:, 256:384], hmask[:, 0:1], None, ALU.mult, reads=["bandm"], writes=["hbm"])
            S.op("dve", "tensor_scalar", ham, bandm[:, 0:128], hmask[:, 1:2], None, ALU.mult, reads=["bandm"], writes=["ham"])
            for s in range(2):
                S.op("pool", "memset", vt[s][:, :, 128:130], 1.0, writes=[("vt", s)])
            SZv = SZ.rearrange("(t p) c -> p t c", p=128)
            trp = PS[4].bitcast(BF16)
            cidx = 0
            sidx = 0
            hcount = 0
            def load_kv(n_):
                ns_ = n_ % 2
                S.dma("sp", kt[ns_][:, 0:NLAT], KTc[n_][:, 0:NLAT], writes=[("kt", ns_)])
                ka = KTc_all[n_ // 2, :, n_ % 2]
                va = VDc_all[n_ // 2, :, n_ % 2]
                S.dma("sp", kt[ns_][:, XB * 128:(XB + 1) * 128], ka[0][:, NLAT - 128:NLAT], writes=[("kt", ns_)])
                S.dma("sp", kt[ns_][:, XA * 128:(XA + 1) * 128], ka[1][:, 0:128], writes=[("kt", ns_)])
                S.dma("sp", kt[ns_][:, XC0 * 128:(XC0 + 1) * 128], ka[0][:, NLAT:NLAT + 128], writes=[("kt", ns_)])
                S.dma("sp", kt[ns_][:, XC1 * 128:(XC1 + 1) * 128], ka[1][:, NLAT:NLAT + 128], writes=[("kt", ns_)])
                S.dma("sp", vt[ns_][:, 0:NLT, 0:128], VDc[n_][0:NLAT].rearrange("(t p) c -> p t c", p=128), writes=[("vt", ns_)])
                S.dma("sp", vt[ns_][:, XB, 0:128], va[0][NLAT - 128:NLAT, :], writes=[("vt", ns_)])
                S.dma("sp", vt[ns_][:, XA, 0:128], va[1][0:128, :], writes=[("vt", ns_)])
                S.dma("sp", vt[ns_][:, XC0, 0:128], va[0][NLAT:NLAT + 128, :], writes=[("vt", ns_)])
                S.dma("sp", vt[ns_][:, XC1, 0:128], va[1][NLAT:NLAT + 128, :], writes=[("vt", ns_)])

            load_kv(0)
            S.dma("sp", qt[0], QT[0], writes=[("qt", 0)])
            for n in range(4):
                ns = n % 2
                for g in range(4):
                    hq = 4 * n + g
                    hs = hcount % 2
                    hcount += 1
                    if hq + 1 < 16:
                        S.dma("sp", qt[(hs + 1) % 2], QT[hq + 1], writes=[("qt", (hs + 1) % 2)])
                    if g == 1 and n + 1 < 4:
                        load_kv(n + 1)
                    for (off, w, kind) in CHUNKS:
                        if kind == 1 and not need_ctx:
                            continue
                        cs_ = cidx % 2
                        cidx += 1
                        nq = w // 128
                        qb0 = off // 128
                        S.dma("sp", szt[cs_][:, :nq, :], SZv[:, qb0:qb0 + nq, hq * 128:(hq + 1) * 128], writes=[("szt", cs_)])
                        items = []
                        if kind == 0:
                            for kb in range(qb0 - 1, qb0 + nq + 1):
                                qlo = max(kb - 1, qb0)
                                qhi = min(kb + 1, qb0 + nq - 1)
                                if kb < 0:
                                    items.append((XB, qlo, qhi, {0: hbm}))
                                elif kb >= NLT:
                                    items.append((XA, qlo, qhi, {NLT - 1: ham}))
                                else:
                                    mk = {}
                                    for qb in range(qlo, qhi + 1):
                                        if kb == qb + 1:
                                            mk[qb] = bandm[:, 0:128]
                                        elif kb == qb - 1:
                                            mk[qb] = bandm[:, 256:384]
                                    items.append((kb, qlo, qhi, mk))
                        items.append((XC0, qb0, qb0 + nq - 1, {}))
                        items.append((XC1, qb0, qb0 + nq - 1, {}))
                        lastc = {}
                        for ii, (kb, qlo, qhi, mk) in enumerate(items):
                            for qb in range(qlo, qhi + 1):
                                lastc[qb] = ii
                        started = set()
                        sbase = sidx
                        sidx += len(items)

                        def s_exp(ii):
                            kb, qlo, qhi, mk = items[ii]
                            bank = (sbase + ii) % 4
                            ncol = (qhi - qlo + 1) * 128
                            S.op("pe", "matmul", PS[bank][:, :ncol], kt[ns][:, kb * 128:(kb + 1) * 128], qt[hs][:, qlo * 128:(qhi + 1) * 128],
                                 start=True, stop=True, reads=[("kt", ns), ("qt", hs)], writes=[("ps", bank)])
                            S.op("act", "activation", pt[bank][:, :ncol], PS[bank][:, :ncol], AF.Exp, scale=scale,
                                 reads=[("ps", bank)], writes=[("pt", bank)])
                            for qb in range(qlo, qhi + 1):
                                c = (qb - qlo) * 128
                                if qb in mk:
                                    S.op("pool", "tensor_tensor", pt[bank][:, c:c + 128], pt[bank][:, c:c + 128], mk[qb], ALU.mult,
                                         reads=[("pt", bank), "bandm", "hbm", "ham"], writes=[("pt", bank)])

                        s_exp(0)
                        if len(items) > 1:
                            s_exp(1)
                        for ii, (kb, qlo, qhi, mk) in enumerate(items):
                            if ii + 2 < len(items):
                                s_exp(ii + 2)
                            bank = (sbase + ii) % 4
                            for qb in range(qlo, qhi + 1):
                                c = (qb - qlo) * 128
                                a = qb - qb0
                                abank = 5 + a // 3
                                c0 = (a % 3) * 160
                                st = abank not in started
                                started.add(abank)
                                S.op("pe", "matmul", PS[abank][:, c0:c0 + 129], pt[bank][:, c:c + 128], vt[ns][:, kb, 0:129],
                                     start=st, stop=(lastc[qb] == ii), skip_group_check=True,
                                     reads=[("pt", bank), ("vt", ns)], writes=[("ps", abank)])
                        accv = accB[cs_].rearrange("p b (a c) -> p (b a) c", a=3)
                        for b3 in range((nq + 2) // 3):
                            cnt = min(3, nq - 3 * b3)
                            S.op("dve", "tensor_copy", accv[:, 3 * b3:3 * b3 + cnt, 0:129],
                                 PS[5 + b3][:, 0:480].rearrange("p (a c) -> p a c", a=3)[:, 0:cnt, 0:129],
                                 reads=[("ps", 5 + b3)], writes=[("accB", cs_)])
                        S.op("dve", "tensor_scalar", den[cs_][:, 0:nq], accv[:, 0:nq, 128], esink[:, hq:hq + 1], None, ALU.add,
                             reads=[("accB", cs_), "esink"], writes=[("den", cs_)])
                        S.op("dve", "reciprocal", den[cs_][:, 0:nq], den[cs_][:, 0:nq], reads=[("den", cs_)], writes=[("den", cs_)])
                        for qs in range(nq):
                            S.op("dve", "scalar_tensor_tensor", gtok[cs_][:, qs * 128:(qs + 1) * 128], accv[:, qs, 0:128], den[cs_][:, qs:qs + 1],
                                 szt[cs_][:, qs, :], ALU.mult, ALU.mult,
                                 reads=[("accB", cs_), ("den", cs_), ("szt", cs_)], writes=[("gtok", cs_, qs)])
                        for qs in range(nq):
                            S.op("pe", "transpose", trp[:, qs * 128:(qs + 1) * 128], gtok[cs_][:, qs * 128:(qs + 1) * 128], identb,
                                 reads=[("gtok", cs_, qs), "identb"], writes=[("ps", 4)])
                        S.op("dve", "tensor_copy", gst[hs][:, off:off + w], trp[:, 0:w], reads=[("ps", 4)], writes=[("gst", hs)])
                    if need_ctx:
                        S.dma("sp", GT[:, hq, :], gst[hs], reads=[("gst", hs)])
                    else:
                        S.dma("sp", GT[:, hq, 0:NLAT], gst[hs][:, 0:NLAT], reads=[("gst", hs)])
            S.barrier()
        outproj_stage(li, wo_pre, need_ctx)
        eso.close()

    def layer_B(li, need_ctx):
        eso = contextlib.ExitStack()
        wo_pre = prefetch_wout(b_w_out[0], eso)
        with contextlib.ExitStack() as es:
            def tmp(name, shape, dt=F32):
                return es.enter_context(sbt(name, list(shape), dt)).ap()
            hT = tmp("hT", [128, 8, T], BF16)
            hTh = tmp("hTh", [128, 8, 32], BF16)
            S.dma("sp", xh_send[:, :, 0:8], xT[:, :, 0:8])
            S.dma("sp", xh_send[:, :, 8:16], xT[:, :, NLAT - 8:NLAT])
            S.dma("sp", xh_send[:, :, 16:24], xT[:, :, NLAT:NLAT + 8])
            S.dma("sp", xh_send[:, :, 24:32], xT[:, :, T - 8:T])
            S.barrier()
            S.coll("AllGather", coll_groups, xh_send.rearrange("p k c -> (p k) c"), xh_all.rearrange("r p k c -> (r p k) c"))
            S.barrier()
            with contextlib.ExitStack() as es2:
                norm_stage(li, hT, es2)
                hx = es2.enter_context(sbt("b_hx", [128, 8, 32], F32)).ap()
                sqh = es2.enter_context(sbt("b_sqh", [128, 8, 32], BF16)).ap()
                rsh = es2.enter_context(sbt("b_rsh", [128, 32], F32)).ap()
                tmh = es2.enter_context(sbt("b_tmh", [128, 8, 32], F32)).ap()
                S.dma("sp", hx[:, :, 0:8], xh_all[0][:, :, 8:16], writes=["hx"])
                S.dma("sp", hx[:, :, 8:16], xh_all[1][:, :, 0:8], writes=["hx"])
                S.dma("sp", hx[:, :, 16:24], xh_all[0][:, :, 24:32], writes=["hx"])
                S.dma("sp", hx[:, :, 24:32], xh_all[1][:, :, 16:24], writes=["hx"])
                S.op("dve", "tensor_tensor", sqh, hx, hx, ALU.mult, reads=["hx"], writes=["sqh"])
                for k in range(8):
                    S.op("pe", "matmul", PS[7][:, 0:32], onesb, sqh[:, k, :], start=(k == 0), stop=(k == 7),
                         reads=["onesb", "sqh"], writes=[("ps", 7)])
                S.op("act", "activation", rsh, PS[7][:, 0:32], AF.Sqrt, bias=epsc, scale=1.0 / D, reads=[("ps", 7)], writes=["rsh"])
                S.op("dve", "reciprocal", rsh, rsh, reads=["rsh"], writes=["rsh"])
                for k in range(8):
                    S.op("dve", "tensor_tensor", tmh[:, k, :], hx[:, k, :], rsh, ALU.mult, reads=["hx", "rsh"], writes=[("tmh", k)])
                    for (c0, c1, kind) in ((0, 16, 0), (16, 32, 1)):
                        S.op("act", "activation", hTh[:, k, c0:c1], tmh[:, k, c0:c1], AF.Identity,
                             bias=MODS[:, li, kind, 1, k:k + 1], scale=MODS[:, li, kind, 0, k:k + 1],
                             reads=[("tmh", k)], writes=["hTh"])
                S.barrier()
            up = tmp("b_up", [128, LP])
            ab = [tmp("b_ab%d" % i, [128, LP]) for i in range(2)]
            dT = [tmp("b_d%d" % i, [128, T], BF16) for i in range(4)]
            w1 = tmp("b_w1", [128, 8, 512], BF16)
            wgr = tmp("b_wg", [128, 4, 512], BF16)
            edge = tmp("b_edge", [128, 4, 32])
            etmp = tmp("b_etmp", [128, 8])
            bgs = tmp("b_bgs", [128, 2, 16])
            szs = [tmp("b_sz%d" % i, [128, 512]) for i in range(2)]
            tys = [tmp("b_ty%d" % i, [128, 512]) for i in range(2)]
            gch = [tmp("b_gc%d" % i, [128, 512], BF16) for i in range(2)]
            S.dma("sp", bgs[:, 0, :], b_bT, writes=["bgs"])
            S.dma("sp", bgs[:, 1, :], b_sT, writes=["bgs"])
            S.dma("sp", edge, edge_d.rearrange("g e -> (g e)").partition_broadcast(128).rearrange("p (g e) -> p g e", g=4), writes=["edge"])
            S.op("dve", "memset", up, 0.0, writes=["up"])
            wv = b_w_in[0].rearrange("(k p) c -> p k c", p=128)
            regions = [(PAD, 0, NLAT), (3 * PAD + NLAT, NLAT, NCTX)]
            ci2 = 0
            for g in range(4):
                wd = POOL_WINDOWS[g]
                nst = int(math.log2(wd))
                S.dma("pool", w1, wv[:, :, g * 512:(g + 1) * 512], writes=["w1"])
                S.dma("pool", wgr, b_w_grp[0, g].rearrange("(k p) c -> p k c", p=128), writes=["wgr"])
                for kc in range(4):
                    for ci, (off, w, kind) in enumerate(CHUNKS):
                        bank = ci % 4
                        for k in range(8):
                            S.op("pe", "matmul", PS[bank][:, :w], w1[:, k, kc * 128:(kc + 1) * 128], hT[:, k, off:off + w],
                                 start=(k == 0), stop=(k == 7), reads=["w1", ("hT", ci)], writes=[("ps", bank)])
                        pos = (PAD + off) if kind == 0 else (3 * PAD + off)
                        S.op("act", "copy", up[:, pos:pos + w], PS[bank][:, :w], reads=[("ps", bank)], writes=["up"])
                    for k in range(8):
                        S.op("pe", "matmul", PS[7][:, 0:32], w1[:, k, kc * 128:(kc + 1) * 128], hTh[:, k, :],
                             start=(k == 0), stop=(k == 7), reads=["w1", "hTh"], writes=[("ps", 7)])
                    for (pc, hc0, mi) in ((0, 0, 0), (PAD + NLAT, 8, 1), (2 * PAD + NLAT, 16, 0), (3 * PAD + T, 24, 1)):
                        S.op("dve", "tensor_scalar", up[:, pc:pc + 8], PS[7][:, hc0:hc0 + 8], hmask[:, mi:mi + 1], None, ALU.mult,
                             reads=[("ps", 7)], writes=["up"])
                    cur = up
                    curkey = "up"
                    L = LP
                    for s_ in range(nst):
                        sh = 2 ** s_
                        nxt = ab[s_ % 2]
                        nkey = ("ab", s_ % 2)
                        Ln = L - sh
                        S.op("pool", "tensor_tensor", nxt[:, 0:Ln], cur[:, 0:Ln], cur[:, sh:sh + Ln], ALU.add,
                             reads=[curkey], writes=[nkey])
                        cur, curkey, L = nxt, nkey, Ln
                    for (p0, t0_, n_) in regions:
                        S.op("dve", "scalar_tensor_tensor", dT[kc][:, t0_:t0_ + n_], cur[:, p0 - wd // 2:p0 - wd // 2 + n_], 1.0 / wd,
                             up[:, p0:p0 + n_], ALU.mult, ALU.subtract, reads=[curkey, "up"], writes=[("dT", kc)])
                    for ri, (p0, t0_, n_) in enumerate(regions):
                        for side in range(2):
                            e0 = 0 if side == 0 else n_ - 8
                            S.op("dve", "tensor_tensor", etmp, cur[:, p0 - wd // 2 + e0:p0 - wd // 2 + e0 + 8],
                                 edge[:, g, (ri * 2 + side) * 8:(ri * 2 + side) * 8 + 8], ALU.mult, reads=[curkey, "edge"], writes=["etmp"])
                            S.op("dve", "tensor_tensor", dT[kc][:, t0_ + e0:t0_ + e0 + 8], etmp, up[:, p0 + e0:p0 + e0 + 8], ALU.subtract,
                                 reads=["etmp", "up"], writes=[("dT", kc)])
                S.dma("pool", w1, wv[:, :, 2048 + g * 512:2048 + (g + 1) * 512], writes=["w1"])
                for ob in range(4):
                    jb = 4 * g + ob
                    for ci, (off, w, kind) in enumerate(CHUNKS):
                        if kind == 1 and not need_ctx:
                            continue
                        c2 = ci2 % 2
                        ci2 += 1
                        by = 4 + c2
                        bz = 6 + c2
                        for kc in range(4):
                            S.op("pe", "matmul", PS[by][:, :w], wgr[:, kc, ob * 128:(ob + 1) * 128], dT[kc][:, off:off + w],
                                 start=(kc == 0), stop=(kc == 3), reads=["wgr", ("dT", kc)], writes=[("ps", by)])
                        for k in range(8):
                            S.op("pe", "matmul", PS[bz][:, :w], w1[:, k, ob * 128:(ob + 1) * 128], hT[:, k, off:off + w],
                                 start=(k == 0), stop=(k == 7), reads=["w1", ("hT", ci)], writes=[("ps", bz)])
                        S.op("act", "activation", szs[c2][:, :w], PS[bz][:, :w], AF.Silu, reads=[("ps", bz)], writes=[("szs", c2)])
                        S.op("dve", "tensor_scalar", tys[c2][:, :w], PS[by][:, :w], bgs[:, 0, jb:jb + 1], bgs[:, 1, jb:jb + 1], ALU.add, ALU.mult,
                             reads=[("ps", by), "bgs"], writes=[("tys", c2)])
                        S.op("dve", "tensor_tensor", gch[c2][:, :w], tys[c2][:, :w], szs[c2][:, :w], ALU.mult,
                             reads=[("tys", c2), ("szs", c2)], writes=[("gch", c2)])
                        S.dma("sp", GT[:, jb, off:off + w], gch[c2][:, :w], reads=[("gch", c2)])
            S.barrier()
        outproj_stage(li, wo_pre, need_ctx)
        eso.close()

    def final_stage():
        with contextlib.ExitStack() as es:
            def tmp(name, shape, dt=F32):
                return es.enter_context(sbt(name, list(shape), dt)).ap()
            fg = tmp("f_g", [128, 8])
            xc = [tmp("f_xc%d" % i, [128, 8, 512]) for i in range(2)]
            sq = [tmp("f_sq%d" % i, [128, 8, 512], BF16) for i in range(2)]
            rstd = [tmp("f_rs%d" % i, [128, 512]) for i in range(2)]
            ot = [tmp("f_ot%d" % i, [128, D]) for i in range(2)]
            S.dma("sp", fg, final_gT, writes=["fg"])
            oi = 0
            for ci, (off, w, kind) in enumerate(CHUNKS):
                if kind == 1:
                    continue
                s = ci % 2
                bank = ci % 2
                S.dma("sp", xc[s], xT[:, :, off:off + w], writes=[("xc", s)])
                S.op("dve", "tensor_tensor", sq[s], xc[s], xc[s], ALU.mult, reads=[("xc", s)], writes=[("sq", s)])
                for k in range(8):
                    S.op("pe", "matmul", PS[bank], onesb, sq[s][:, k, :], start=(k == 0), stop=(k == 7),
                         reads=["onesb", ("sq", s)], writes=[("ps", bank)])
                S.op("act", "activation", rstd[s], PS[bank], AF.Sqrt, bias=epsc, scale=1.0 / D, reads=[("ps", bank)], writes=[("rstd", s)])
                S.op("dve", "reciprocal", rstd[s], rstd[s], reads=[("rstd", s)], writes=[("rstd", s)])
                for k in range(8):
                    S.op("dve", "scalar_tensor_tensor", xc[s][:, k, :], xc[s][:, k, :], fg[:, k:k + 1], rstd[s], ALU.mult, ALU.mult,
                         reads=[("xc", s), ("rstd", s), "fg"], writes=[("xc", s)])
                for tt in range(4):
                    o2 = oi % 2
                    oi += 1
                    for hb in range(2):
                        bank2 = 2 + (oi % 2) * 2 + hb
                        for k4 in range(4):
                            k = hb * 4 + k4
                            S.op("pe", "transpose", PS[bank2][:, k4 * 128:(k4 + 1) * 128], xc[s][:, k, tt * 128:(tt + 1) * 128], ident,
                                 reads=[("xc", s), "ident"], writes=[("ps", bank2)])
                        if hb == 0:
                            S.op("dve", "tensor_copy", ot[o2][:, 0:512], PS[bank2], reads=[("ps", bank2)], writes=[("ot", o2, 0)])
                        else:
                            S.op("act", "copy", ot[o2][:, 512:1024], PS[bank2], reads=[("ps", bank2)], writes=[("ot", o2, 1)])
                    S.dma("sp", out_d[off + tt * 128:off + (tt + 1) * 128, :], ot[o2], reads=[("ot", o2, 0), ("ot", o2, 1)])
            S.barrier()

    prologue()
    for li in range(depth):
        m = li % 3
        need_ctx = li < DEPTH - 1
        if m == 0:
            layer_A(li, li // 3, need_ctx)
        elif m == 1:
            layer_B(li, need_ctx)
        else:
            layer_C(li, need_ctx)
    if dbg is not None:
        with sbt("dbg_t", [128, 8, T], F32) as dt_:
            S.dma("sp", dt_.ap(), xT)
            S.barrier()
            S.dma("sp", dbg, dt_.ap())
            S.barrier()
    if depth == DEPTH:
        final_stage()
    S.finalize()
    nc._declared_inputs = declared
    return nc


NLAT_CORE = 2048
NCTX_CORE = 128
_PROG_CACHE = {}


def make_in_maps(inputs, NLAT=None, NCTX=None, n_cores=8):
    f = lambda a: np.ascontiguousarray(np.asarray(a, dtype=np.float32))
    x = f(inputs["x"]); c = f(inputs["c"]); ctx = f(inputs["ctx"]); c_ctx = f(inputs["c_ctx"])
    NLAT = NLAT or NLAT_CORE
    NCTX = NCTX or NCTX_CORE
    shared = {
        "w_ada": f(inputs["w_ada"]),
        "b_adaT": f(np.asarray(inputs["b_ada"]).reshape(DEPTH, 24, 128).transpose(0, 2, 1)),
        "norm_gT": f(np.asarray(inputs["norm_g"]).reshape(DEPTH, 8, 128).transpose(0, 2, 1)),
        "final_gT": f(np.asarray(inputs["final_g"]).reshape(8, 128).T),
        "a_w_in": f(inputs["a_w_in"]),
        "a_w_out": f(inputs["a_w_out"]),
        "a_lam": f(np.stack([inputs["a_lam_q1"], inputs["a_lam_k1"], inputs["a_lam_q2"], inputs["a_lam_k2"]], axis=1)),
        "a_subln": f(inputs["a_subln_g"]),
        "b_w_in": f(inputs["b_w_in"]),
        "b_w_grp": f(inputs["b_w_grp"]),
        "b_bT": f(np.asarray(inputs["b_b_grp"]).reshape(16, 128).T),
        "b_sT": f(np.asarray(inputs["b_scale"]).reshape(16, 128).T),
        "b_w_out": f(inputs["b_w_out"]),
        "c_w_in": f(inputs["c_w_in"]),
        "c_sink": f(inputs["c_sink"]),
        "c_w_out": f(inputs["c_w_out"]),
        "ident": np.eye(128, dtype=np.float32),
        "band": band_mask(),
    }
    per_rank = []
    for s_ in range(2):
        cosA, sinA, permA = rope_tables(64, 2, s_ * NLAT, NLAT)
        cosC, sinC, permC = rope_tables(128, 1, s_ * NLAT, NLAT)
        ic = invcnt_tables(NLAT, NCTX, s_ * NLAT, 2 * NLAT, s_ * NCTX, 2 * NCTX)
        e = np.zeros((4, 32), np.float32)
        for g in range(4):
            e[g, 0:8] = ic[g, 0:8]
            e[g, 8:16] = ic[g, NLAT - 8:NLAT]
            e[g, 16:24] = ic[g, NLAT:NLAT + 8]
            e[g, 24:32] = ic[g, NLAT + NCTX - 8:NLAT + NCTX]
        hm = np.zeros((128, 2), np.float32)
        hm[:, 0] = 1.0 if s_ == 1 else 0.0
        hm[:, 1] = 1.0 if s_ == 0 else 0.0
        per_rank.append({"ropeA": f(np.stack([cosA, sinA])), "ropeC": f(np.stack([cosC, sinC])), "permA": permA, "permC": permC,
                         "edge": e, "hmask": hm})
    maps = []
    for core in range(n_cores):
        b = core // 2
        s_ = core % 2
        m = dict(shared)
        m.update(per_rank[s_])
        m["x_tok"] = np.ascontiguousarray(x[b][s_ * NLAT:(s_ + 1) * NLAT])
        m["ctx_tok"] = np.ascontiguousarray(ctx[b][s_ * NCTX:(s_ + 1) * NCTX])
        cv = np.stack([c[b].reshape(8, 128).T, c_ctx.reshape(8, 128).T], axis=-1)
        m["cvec"] = f(cv)
        maps.append(m)
    return maps


def kernel(**inputs):
    key = "split"
    if key not in _PROG_CACHE:
        _PROG_CACHE[key] = build_program(NLAT_CORE, NCTX_CORE)
    nc = _PROG_CACHE[key]
    maps = make_in_maps(inputs)
    maps = [{k: v for k, v in m.items() if k in nc._declared_inputs} for m in maps]
    res = run_bass_kernel_spmd(nc, maps, core_ids=list(range(8)))
    B = np.asarray(inputs["x"]).shape[0]
    out = np.stack([np.concatenate([np.asarray(res.results[2 * b + s_]["out"], dtype=np.float32) for s_ in range(2)], axis=0)
                    for b in range(B)], axis=0)
    return out
```
